# Optimizing a Trainium2 kernel written in Bass

```python
import jax, jax.numpy as jnp
from jax import lax
import numpy as np

D_MODEL = 2048
BATCH = 1
SEQ = 16384
DEPTH = 1

LRU_WIDTH = D_MODEL
LRU_BLOCKS = 16
LRU_BLOCK_DIM = LRU_WIDTH // LRU_BLOCKS
LRU_C = 8.0
CONV_WIDTH = 4
CONV_LEFT = 2
N_HEADS = 16
N_KV_HEADS = 4
HEAD_DIM = 128
GROUP = N_HEADS // N_KV_HEADS
ATTN_WIDTH = N_HEADS * HEAD_DIM
KV_WIDTH = N_KV_HEADS * HEAD_DIM
WINDOW = 128
BLOCK = 128
ROPE_THETA = 500000.0
ROT_DIM = HEAD_DIM // 4
NORM_EPS = 1e-6
IN_SPLITS = (LRU_WIDTH, LRU_WIDTH, ATTN_WIDTH, KV_WIDTH, KV_WIDTH, ATTN_WIDTH, D_MODEL, D_MODEL)
IN_WIDTH = sum(IN_SPLITS)

kernel_name = 'hybrid_rglru_swa_gqa_encoder_block'


def rms_norm(x, w):
    xf = x.astype(jnp.float32)
    y = xf * lax.rsqrt(jnp.mean(xf * xf, axis=-1, keepdims=True) + NORM_EPS)
    return (y * w.astype(jnp.float32)).astype(x.dtype)


def centred_depthwise_conv(u, w, b):
    s = u.shape[1]
    up = jnp.pad(u, ((0, 0), (CONV_LEFT, CONV_WIDTH - 1 - CONV_LEFT), (0, 0)))
    y = b
    for tap in range(CONV_WIDTH):
        y = y + up[:, tap:tap + s] * w[tap]
    return y


def _linear_combine(e1, e2):
    a1, b1 = e1
    a2, b2 = e2
    return a1 * a2, a2 * b1 + b2


def rg_lru(u, w_r, b_r, w_i, b_i, lam, reverse):
    bsz, s, _ = u.shape
    ub = u.reshape(bsz, s, LRU_BLOCKS, LRU_BLOCK_DIM)
    gate_r = jax.nn.sigmoid(jnp.einsum('bsni,nij->bsnj', ub, w_r).reshape(bsz, s, LRU_WIDTH) + b_r)
    gate_i = jax.nn.sigmoid(jnp.einsum('bsni,nij->bsnj', ub, w_i).reshape(bsz, s, LRU_WIDTH) + b_i)
    log_a = -LRU_C * gate_r.astype(jnp.float32) * jax.nn.softplus(-lam.astype(jnp.float32))
    a = jnp.exp(log_a)
    in_scale = jnp.sqrt(-jnp.expm1(2.0 * log_a))
    b = in_scale * (gate_i * u).astype(jnp.float32)
    _, h = lax.associative_scan(_linear_combine, (a, b), axis=1, reverse=reverse)
    return h


def partial_rope(t, cos, sin):
    half = ROT_DIM // 2
    tf = t[..., :ROT_DIM].astype(jnp.float32)
    t1, t2 = tf[..., :half], tf[..., half:]
    c = cos[None, :, None, :]
    sn = sin[None, :, None, :]
    rot = jnp.concatenate([t1 * c - t2 * sn, t2 * c + t1 * sn], axis=-1).astype(t.dtype)
    return jnp.concatenate([rot, t[..., ROT_DIM:]], axis=-1)


def windowed_gqa(q, k, v, sink):
    bsz, s = q.shape[:2]
    nb = s // BLOCK
    qb = q.reshape(bsz, nb, BLOCK, N_KV_HEADS, GROUP, HEAD_DIM)

    def band(t):
        tp = jnp.pad(t, ((0, 0), (BLOCK, BLOCK), (0, 0), (0, 0)))
        tp = tp.reshape(bsz, nb + 2, BLOCK, N_KV_HEADS, HEAD_DIM)
        return jnp.concatenate([tp[:, :-2], tp[:, 1:-1], tp[:, 2:]], axis=2)

    kw, vw = band(k), band(v)
    scores = jnp.einsum('bnqkgd,bnskd->bnkgqs', qb, kw).astype(jnp.float32) * (HEAD_DIM ** -0.5)
    q_idx = jnp.arange(BLOCK)[:, None]
    s_idx = jnp.arange(3 * BLOCK)[None, :]
    band_ok = jnp.abs(s_idx - BLOCK - q_idx) <= WINDOW
    key_pos = (jnp.arange(nb)[:, None] - 1) * BLOCK + jnp.arange(3 * BLOCK)[None, :]
    in_range = (key_pos >= 0) & (key_pos < s)
    mask = band_ok[None] & in_range[:, None, :]
    scores = jnp.where(mask[None, :, None, None], scores, -1e30)
    sink_l = sink.astype(jnp.float32).reshape(1, 1, N_KV_HEADS, GROUP, 1, 1)
    m = jnp.maximum(scores.max(axis=-1, keepdims=True), sink_l)
    e = jnp.exp(scores - m)
    p = e / (e.sum(axis=-1, keepdims=True) + jnp.exp(sink_l - m))
    out = jnp.einsum('bnkgqs,bnskd->bnqkgd', p.astype(v.dtype), vw)
    return out.reshape(bsz, s, ATTN_WIDTH)


def setup_inputs(seed: int = 0) -> dict:
    key = jax.random.key(seed)
    ks = jax.random.split(key, 16)

    def nrm(k, shape, fan_in):
        return jax.random.normal(k, shape, jnp.float32) * (fan_in ** -0.5)

    L = DEPTH
    x = jax.random.normal(ks[0], (BATCH, SEQ, D_MODEL), jnp.float32)
    norm_pre_w = 1.0 + 0.02 * jax.random.normal(ks[1], (L, D_MODEL), jnp.float32)
    w_in = nrm(ks[2], (L, D_MODEL, IN_WIDTH), D_MODEL)
    conv_w = nrm(ks[3], (L, CONV_WIDTH, LRU_WIDTH), CONV_WIDTH)
    conv_b = 0.01 * jax.random.normal(ks[4], (L, LRU_WIDTH), jnp.float32)
    lru_w_r = nrm(ks[5], (L, 2, LRU_BLOCKS, LRU_BLOCK_DIM, LRU_BLOCK_DIM), LRU_BLOCK_DIM)
    lru_b_r = 0.01 * jax.random.normal(ks[6], (L, 2, LRU_WIDTH), jnp.float32)
    lru_w_i = nrm(ks[7], (L, 2, LRU_BLOCKS, LRU_BLOCK_DIM, LRU_BLOCK_DIM), LRU_BLOCK_DIM)
    lru_b_i = 0.01 * jax.random.normal(ks[8], (L, 2, LRU_WIDTH), jnp.float32)
    a_c = jax.random.uniform(ks[9], (L, 2, LRU_WIDTH), jnp.float32, minval=0.9, maxval=0.999)
    sig = a_c ** (1.0 / LRU_C)
    lru_lambda = jnp.log(sig) - jnp.log1p(-sig)
    attn_sink = jax.random.normal(ks[10], (L, N_HEADS), jnp.float32)
    w_proj_a = nrm(ks[11], (L, LRU_WIDTH, D_MODEL), LRU_WIDTH)
    w_proj_b = nrm(ks[12], (L, ATTN_WIDTH, D_MODEL), ATTN_WIDTH)
    w_out = nrm(ks[13], (L, D_MODEL, D_MODEL), D_MODEL)
    norm_post_w = 1.0 + 0.02 * jax.random.normal(ks[14], (L, D_MODEL), jnp.float32)
    return {'x': x, 'norm_pre_w': norm_pre_w, 'w_in': w_in, 'conv_w': conv_w, 'conv_b': conv_b,
            'lru_w_r': lru_w_r, 'lru_b_r': lru_b_r, 'lru_w_i': lru_w_i, 'lru_b_i': lru_b_i,
            'lru_lambda': lru_lambda, 'attn_sink': attn_sink, 'w_proj_a': w_proj_a,
            'w_proj_b': w_proj_b, 'w_out': w_out, 'norm_post_w': norm_post_w}


def reference(x, norm_pre_w, w_in, conv_w, conv_b, lru_w_r, lru_b_r, lru_w_i, lru_b_i,
              lru_lambda, attn_sink, w_proj_a, w_proj_b, w_out, norm_post_w):
    bsz, s, _ = x.shape
    pos = jnp.arange(s, dtype=jnp.float32)
    inv_freq = ROPE_THETA ** (-jnp.arange(0, ROT_DIM, 2, dtype=jnp.float32) / ROT_DIM)
    ang = pos[:, None] * inv_freq[None, :]
    cos, sin = jnp.cos(ang), jnp.sin(ang)
    split_at = np.cumsum(IN_SPLITS)[:-1].tolist()
    for l in range(DEPTH):
        xn = rms_norm(x, norm_pre_w[l])
        z = xn @ w_in[l]
        u_lru, g_lru, q, k, v, g_attn, m_lru, m_attn = jnp.split(z, split_at, axis=-1)
        u = centred_depthwise_conv(u_lru, conv_w[l], conv_b[l])
        h = (rg_lru(u, lru_w_r[l, 0], lru_b_r[l, 0], lru_w_i[l, 0], lru_b_i[l, 0], lru_lambda[l, 0], False)
             + rg_lru(u, lru_w_r[l, 1], lru_b_r[l, 1], lru_w_i[l, 1], lru_b_i[l, 1], lru_lambda[l, 1], True))
        y_a = h.astype(x.dtype) * jax.nn.silu(g_lru)
        q = partial_rope(q.reshape(bsz, s, N_HEADS, HEAD_DIM), cos, sin)
        k = partial_rope(k.reshape(bsz, s, N_KV_HEADS, HEAD_DIM), cos, sin)
        v = v.reshape(bsz, s, N_KV_HEADS, HEAD_DIM)
        y_b = windowed_gqa(q, k, v, attn_sink[l]) * jax.nn.silu(g_attn)
        merged = (jax.nn.sigmoid(m_lru) * (y_a @ w_proj_a[l])
                  + jax.nn.sigmoid(m_attn) * (y_b @ w_proj_b[l]))
        x = x + rms_norm(merged @ w_out[l], norm_post_w[l])
    return x
```

```python
import numpy as np
import ml_dtypes
import concourse.bass as bass
import concourse.mybir as mybir
from concourse.bass_utils import run_bass_kernel_spmd

F32 = mybir.dt.float32
BF16 = mybir.dt.bfloat16
ALU = mybir.AluOpType
AF = mybir.ActivationFunctionType
AX = mybir.AxisListType

NCORES = 8
S_FULL = 16384
D = 2048
T = S_FULL // NCORES
SB = 1024
NSB = T // SB
H = 128
TX = SB + 2 * H
TE = T + 2 * H
KC = 16
IN_W = 13312
EPS = 1e-6
LRU_C = 8.0
SCALE = 128 ** -0.5
CB_U, CB_GL, CB_Q, CB_K, CB_V, CB_GA, CB_ML, CB_MA = 0, 16, 32, 48, 52, 56, 72, 88

SBUF_BASE = 16640
SBUF_END = 229344


class Tile:
    __slots__ = ("name", "t", "lo", "hi", "w", "r", "over", "ds")

    def __init__(self, name, t, lo, hi):
        self.name, self.t, self.lo, self.hi = name, t, lo, hi
        self.w = None
        self.r = {}
        self.over = [self]
        self.ds = None

    def __getitem__(self, k):
        return self.t[k]


class DSem:
    def __init__(self, h):
        self.h, self.n = h, 0


class Sched:
    def __init__(self, nc):
        self.nc = nc
        self.E = {"pe": nc.tensor, "act": nc.scalar, "dve": nc.vector, "pool": nc.gpsimd, "sp": nc.sync}
        self.sem = {e: nc.alloc_semaphore("sem_" + e) for e in ("pe", "act", "dve", "pool")}
        self.cnt = {e: 0 for e in self.sem}
        self.prog = {e: [] for e in self.E}
        self.seen = {e: {} for e in self.E}
        self.sb_tiles = []
        self.final = []
        self.nops = 0
        self.nwaits = 0

    def sb(self, name, off, shape, dtype, esz, dsem=False):
        n = 1
        for s in shape[1:]:
            n *= s
        assert SBUF_BASE <= off and off + n * esz <= SBUF_END, (name, off, n * esz)
        t = self.nc.alloc_sbuf_tensor_at(name, list(shape), dtype, offset=off)
        tl = Tile(name, t, off, off + n * esz)
        for o in self.sb_tiles:
            if o.lo < tl.hi and tl.lo < o.hi:
                o.over.append(tl)
                tl.over.append(o)
        self.sb_tiles.append(tl)
        if dsem:
            tl.ds = self.dsem("d_" + name)
        return tl

    def raw(self, name, t=None, dsem=False):
        tl = Tile(name, t, 0, 0)
        if dsem:
            tl.ds = self.dsem("d_" + name)
        return tl

    def dsem(self, name):
        return DSem(self.nc.alloc_semaphore(name))

    def _deps(self, eng, reads, writes, extra=()):
        deps = {}

        def add(ev):
            if ev is None:
                return
            s, v = ev
            k = s.name
            if k not in deps or deps[k][1] < v:
                deps[k] = (s, v)

        for t in reads:
            for o in t.over:
                add(o.w)
        for t in writes:
            for o in t.over:
                add(o.w)
                for ev in o.r.values():
                    add(ev)
        for ev in extra:
            add(ev)
        out = []
        seen = self.seen[eng]
        own = self.sem[eng].name if eng in self.sem else None
        for k, (s, v) in deps.items():
            if eng == "pe" and k == own:
                continue
            if seen.get(k, 0) >= v:
                continue
            seen[k] = v
            out.append((s, v))
        return out

    def _commit(self, ev, reads, writes):
        k = ev[0].name
        for t in writes:
            t.w = ev
            t.r = {}
        for t in reads:
            if k not in t.r or t.r[k][1] < ev[1]:
                t.r[k] = ev

    def op(self, eng, fn, reads=(), writes=(), extra=()):
        waits = self._deps(eng, reads, writes, extra)
        sem = self.sem[eng]
        self.cnt[eng] += 1
        ev = (sem, self.cnt[eng])
        self.nops += 1
        self.nwaits += len(waits)

        def emit(e, waits=waits, fn=fn, sem=sem):
            for s, v in waits:
                e.wait_ge(s, v)
            fn(e).then_inc(sem, 1)

        self.prog[eng].append(emit)
        self._commit(ev, reads, writes)
        return ev

    def dma(self, q, out_ap, in_ap, ds, reads=(), writes=(), extra=(), inc=16, fn=None):
        ex = list(extra)
        if ds.n > 0:
            ex.append((ds.h, ds.n))
        waits = self._deps(q, reads, writes, ex)
        ds.n += inc
        ev = (ds.h, ds.n)
        self.nops += 1
        self.nwaits += len(waits)

        def emit(e, waits=waits, out_ap=out_ap, in_ap=in_ap, h=ds.h, inc=inc, fn=fn):
            for s, v in waits:
                e.wait_ge(s, v)
            ins = fn(e) if fn is not None else e.dma_start(out=out_ap, in_=in_ap)
            ins.then_inc(h, inc)

        self.prog[q].append(emit)
        self._commit(ev, reads, writes)
        return ev

    def finish(self):
        fin = {}
        for s, v in self.final:
            if s.name not in fin or fin[s.name][1] < v:
                fin[s.name] = (s, v)

        def run(name):
            def f(e):
                for fn in self.prog[name]:
                    fn(e)
                if name == "sp":
                    for s, v in fin.values():
                        e.wait_ge(s, v)
            return f

        with self.nc.Block() as block:
            block.tensor(run("pe"))
            block.scalar(run("act"))
            block.vector(run("dve"))
            block.gpsimd(run("pool"))
            block.sync(run("sp"))


def build_program(stop_after=5, debug=False):
    nc = bass.Bass("TRN2", target_bir_lowering=False)
    S = Sched(nc)
    dk = "ExternalOutput" if debug else "Internal"

    def din(name, shape, dt=F32):
        return nc.dram_tensor(name, list(shape), dt, kind="ExternalInput").ap()

    x_ext = din("x_ext", [TE, D])
    w_in = din("w_in", [D, IN_W])
    w_a = din("w_a", [D, D])
    w_b = din("w_b", [D, D])
    w_o = din("w_o", [D, D])
    gate_w = din("gate_w", [2, 2, 16, 128, 128])
    pvec_d = din("pvec", [128, 16, 11])
    wpre_d = din("wpre", [D])
    wpost_d = din("wpost", [D])
    sink_d = din("sink", [16])
    cs_d = din("cs", [32, 2, TE])
    masks_d = din("masks", [128, 4, 128], BF16)
    ident_d = din("ident", [128, 128], BF16)
    sel_d = din("sel", [128, 2, 16])
    out_d = nc.dram_tensor("out", [T, D], F32, kind="ExternalOutput").ap()
    xnT_d = nc.dram_tensor("xnT_d", [KC, 128, TE], BF16, kind=dk).ap()
    ab_d = nc.dram_tensor("ab_d", [NSB, 16, 4, 128, SB], F32, kind=dk).ap()
    ag_in = nc.dram_tensor("ag_in", [128, 128], F32)
    ag_out = nc.dram_tensor("ag_out", [NCORES * 128, 128], F32)
    dbg = {}
    if debug:
        dbg["ya"] = nc.dram_tensor("dbg_ya", [NSB, 128, 16, SB], BF16, kind="ExternalOutput").ap()
        dbg["yb"] = nc.dram_tensor("dbg_yb", [NSB, 128, 16, SB], BF16, kind="ExternalOutput").ap()
        dbg["mg"] = nc.dram_tensor("dbg_mg", [NSB, 128, 16, SB], BF16, kind="ExternalOutput").ap()
        dbg["hin"] = nc.dram_tensor("dbg_hin", [128, 64], F32, kind="ExternalOutput").ap()
        dbg["kT"] = nc.dram_tensor("dbg_kT", [NSB, 128, 4, TX], BF16, kind="ExternalOutput").ap()
        dbg["v"] = nc.dram_tensor("dbg_v", [NSB, 128, 10, 512], BF16, kind="ExternalOutput").ap()
        dbg["qT"] = nc.dram_tensor("dbg_qT", [NSB, 4, 128, 4, SB], BF16, kind="ExternalOutput").ap()
    dbg_sem = S.dsem("d_dbg")
    xnT_t = [S.raw("xnT_d%d" % i) for i in range(18)]
    ab_t = [[[S.raw("ab_d_%d_%d_%d" % (s, n, j)) for j in range(4)] for n in range(16)] for s in range(NSB)]
    agin_t = S.raw("ag_in")
    agout_t = S.raw("ag_out")

    P_OFF = SBUF_BASE
    poff = [P_OFF]

    def pers(name, shape, dt, esz, dsem=False):
        n = int(np.prod(shape[1:])) * esz
        t = S.sb(name, poff[0], shape, dt, esz, dsem)
        poff[0] += (n + 63) // 64 * 64
        return t

    pvec = pers("pvec", [128, 16, 11], F32, 4, True)
    cvec = pers("cvec", [128, 16, 4], F32, 4)
    sinkb = pers("sinkb", [128, 16], F32, 4, True)
    esink = pers("esink", [128, 16], F32, 4)
    esbc = pers("esbc", [128, 16, 128], F32, 4)
    masks = pers("masks", [128, 4, 128], BF16, 2, True)
    ident = pers("ident", [128, 128], BF16, 2, True)
    ones = pers("ones", [128, 128], BF16, 2)
    AE = pers("AE", [128, 128], F32, 4, True)
    G = pers("G", [128, NCORES, 128], F32, 4, True)
    Hf = pers("Hf", [128, 17, 16], F32, 4)
    Hb = pers("Hb", [128, 17, 16], F32, 4)
    Hin = pers("Hin", [128, 64], F32, 4)
    sel = pers("sel", [128, 2, 16], F32, 4, True)
    htmp = pers("htmp", [128, 16, 16], F32, 4)
    rsum = pers("rsum", [128, 4], F32, 4)
    ss_t = pers("ss", [128, 8], F32, 4)
    rs_t = pers("rs", [128, 2], F32, 4)
    lam_t = pers("lam_t", [128, 32, 4], F32, 4)
    assert poff[0] <= P_OFF + 20480, poff[0]
    RING_OFF = P_OFF + 20480
    NSLOT = 8
    ring = [S.sb("ring%d" % i, RING_OFF + i * 4096, [128, KC, 128], BF16, 2, True) for i in range(NSLOT)]
    XN_OFF = RING_OFF + NSLOT * 4096
    XN = S.sb("XN", XN_OFF, [128, KC, TX], BF16, 2, True)
    PH = XN_OFF + KC * TX * 2
    PH_SIZE = SBUF_END - PH
    assert PH_SIZE >= 118400, PH_SIZE

    def ph(name, rel, shape, dt, esz, dsem=False):
        n = int(np.prod(shape[1:])) * esz
        assert rel + n <= PH_SIZE, (name, rel, n, PH_SIZE)
        return S.sb(name, PH + rel, shape, dt, esz, dsem)

    YB = ph("YB", 0, [128, 16, SB], BF16, 2, True)
    YA = ph("YA", 32768, [128, 16, SB], BF16, 2, True)
    MG = ph("MG", 65536, [128, 16, SB], BF16, 2, True)
    EXTRA = 98304

    pst = nc.alloc_psum_tensor("pst", [128, 8, 512], F32)
    bank = [S.raw("bank%d" % i) for i in range(8)]
    pbv = [pst[:, b, :].bitcast(BF16) for b in range(8)]
    bp = [0]

    def nb(k=1):
        p = bp[0]
        if k == 2 and p % 2 == 1:
            p += 1
        if p + k > 8:
            p = 0
        bp[0] = (p + k) % 8
        return p

    wsched = []
    for s in range(NSB):
        for n in range(16):
            wsched.append(("in", CB_U + n))
    for s in range(NSB):
        for g in range(4):
            wsched.append(("in", CB_K + g))
        for g in range(4):
            wsched.append(("in", CB_V + g))
        for g in range(4):
            for hh in range(4):
                wsched.append(("in", CB_Q + 4 * g + hh))
                wsched.append(("in", CB_GA + 4 * g + hh))
        for n in range(16):
            wsched.append(("in", CB_GL + n))
        for f in range(16):
            wsched.append(("a", f))
            wsched.append(("in", CB_ML + f))
            wsched.append(("b", f))
            wsched.append(("in", CB_MA + f))
    wsrc = {"in": w_in, "a": w_a, "b": w_b}
    wstate = {"issued": 0, "next": 0}
    PREFETCH = 4

    def wissue(upto):
        while wstate["issued"] < min(upto, len(wsched)):
            j = wstate["issued"]
            kind, cb = wsched[j]
            src = wsrc[kind][:, cb * 128:(cb + 1) * 128].rearrange("(kc p) n -> p kc n", p=128)
            slot = ring[j % NSLOT]
            S.dma("pool", slot[:], src, slot.ds, writes=[slot])
            wstate["issued"] += 1

    def wnext(key):
        i = wstate["next"]
        assert wsched[i] == key, (i, wsched[i], key)
        wissue(i + PREFETCH + 1)
        wstate["next"] += 1
        return ring[i % NSLOT]

    S.dma("sp", pvec[:], pvec_d, pvec.ds, writes=[pvec])
    S.dma("sp", sinkb[:], sink_d.partition_broadcast(128), sinkb.ds, writes=[sinkb])
    S.dma("sp", masks[:], masks_d, masks.ds, writes=[masks])
    S.dma("sp", ident[:], ident_d, ident.ds, writes=[ident])
    S.dma("sp", sel[:], sel_d, sel.ds, writes=[sel])
    S.op("pool", lambda e: e.memset(ones[:], 1.0), [], [ones])
    S.op("pool", lambda e: e.memset(Hf[:], 0.0), [], [Hf])
    S.op("pool", lambda e: e.memset(Hb[:], 0.0), [], [Hb])
    S.op("pool", lambda e: e.memset(AE[:], 0.0), [], [AE])
    lamv = pvec[:, :, 9:11]
    y_ = lam_t[:, 0:16, 0:2]
    z_ = lam_t[:, 0:16, 2:4]
    z2_ = lam_t[:, 16:32, 0:2]
    acc_ = lam_t[:, 16:32, 2:4]
    S.op("act", lambda e: e.activation(out=y_, in_=lamv, func=AF.Exp, scale=-1.0), [pvec], [lam_t])
    S.op("dve", lambda e: e.tensor_scalar(out=z_, in0=y_, scalar1=2.0, scalar2=None, op0=ALU.add), [lam_t], [lam_t])
    S.op("dve", lambda e: e.reciprocal(out=z_, in_=z_), [lam_t], [lam_t])
    S.op("dve", lambda e: e.tensor_tensor(out=z_, in0=z_, in1=y_, op=ALU.mult), [lam_t], [lam_t])
    S.op("dve", lambda e: e.tensor_tensor(out=z2_, in0=z_, in1=z_, op=ALU.mult), [lam_t], [lam_t])
    NT = 9
    S.op("dve", lambda e: e.memset(acc_, 1.0 / (2 * NT + 1)), [], [lam_t])
    for k in range(NT - 1, -1, -1):
        S.op("dve", lambda e: e.tensor_tensor(out=acc_, in0=acc_, in1=z2_, op=ALU.mult), [lam_t], [lam_t])
        S.op("dve", lambda e, k=k: e.tensor_scalar(out=acc_, in0=acc_, scalar1=1.0 / (2 * k + 1), scalar2=None,
                                                  op0=ALU.add), [lam_t], [lam_t])
    S.op("dve", lambda e: e.tensor_tensor(out=acc_, in0=acc_, in1=z_, op=ALU.mult), [lam_t], [lam_t])
    S.op("dve", lambda e: e.tensor_scalar(out=cvec[:, :, 0:2], in0=acc_, scalar1=-2.0 * LRU_C, scalar2=None,
                                          op0=ALU.mult), [lam_t], [cvec])
    S.op("dve", lambda e: e.tensor_scalar(out=cvec[:, :, 2:4], in0=acc_, scalar1=-4.0 * LRU_C, scalar2=None,
                                          op0=ALU.mult), [lam_t], [cvec])
    S.op("act", lambda e: e.activation(out=esink[:], in_=sinkb[:], func=AF.Exp), [sinkb], [esink])
    S.op("pool", lambda e: e.tensor_copy(out=esbc[:], in_=esink[:].unsqueeze(2).broadcast_to([128, 16, 128])),
         [esink], [esbc])

    wpre_bc = ph("wpre_bc", 0, [128, D], F32, 4, True)
    xt0 = [ph("xt0_%d" % i, 8192 + i * 8192, [128, D], F32, 4, True) for i in range(2)]
    xs0 = [ph("xs0_%d" % i, 24576 + i * 4096, [128, D], BF16, 2) for i in range(2)]
    xT0 = [ph("xT0_%d" % i, 32768 + i * 4096, [128, KC, 128], BF16, 2, True) for i in range(2)]
    S.dma("sp", wpre_bc[:], wpre_d.partition_broadcast(128), wpre_bc.ds, writes=[wpre_bc])
    for i in range(TE // 128):
        xt, xs, xT = xt0[i % 2], xs0[i % 2], xT0[i % 2]
        sc = ss_t[:, (i % 2):(i % 2) + 1]
        rc = rs_t[:, (i % 2):(i % 2) + 1]
        S.dma("sp", xt[:], x_ext[i * 128:(i + 1) * 128, :], xt.ds, writes=[xt])
        S.op("act", lambda e, xt=xt, xs=xs, sc=sc: e.activation(out=xs[:], in_=xt[:], func=AF.Square, accum_out=sc),
             [xt], [xs, ss_t])
        S.op("dve", lambda e, sc=sc, rc=rc: e.tensor_scalar(out=rc, in0=sc, scalar1=1.0 / D, scalar2=EPS,
                                                            op0=ALU.mult, op1=ALU.add), [ss_t], [rs_t])
        S.op("act", lambda e, rc=rc: e.activation(out=rc, in_=rc, func=AF.Sqrt), [rs_t], [rs_t])
        S.op("dve", lambda e, rc=rc: e.reciprocal(out=rc, in_=rc), [rs_t], [rs_t])
        S.op("dve", lambda e, xt=xt, xs=xs, rc=rc: e.scalar_tensor_tensor(
            out=xs[:], in0=xt[:], scalar=rc, in1=wpre_bc[:], op0=ALU.mult, op1=ALU.mult), [xt, rs_t, wpre_bc], [xs])
        b = nb(2)

        def tr(e, xs=xs, b=b):
            for kc in range(KC):
                ins = e.transpose(out=pbv[b + kc // 8][:, (kc % 8) * 128:(kc % 8 + 1) * 128],
                                  in_=xs[:, kc * 128:(kc + 1) * 128], identity=ident[:])
            return ins
        S.op("pe", tr, [xs, ident], [bank[b], bank[b + 1]])
        S.op("act", lambda e, xT=xT, b=b: e.activation(
            out=xT[:, 0:8, :], in_=pbv[b].rearrange("p (k t) -> p k t", k=8), func=AF.Copy), [bank[b]], [xT])
        S.op("dve", lambda e, xT=xT, b=b: e.tensor_copy(
            out=xT[:, 8:16, :], in_=pbv[b + 1].rearrange("p (k t) -> p k t", k=8)), [bank[b + 1]], [xT])
        S.dma("sp", xnT_d[:, :, i * 128:(i + 1) * 128].rearrange("k p t -> p k t"), xT[:], xT.ds,
              reads=[xT], writes=[xnT_t[i]])

    def load_xn(sb_i):
        tiles = xnT_t[sb_i * 8: sb_i * 8 + 10]
        S.dma("sp", XN[:], xnT_d[:, :, sb_i * SB: sb_i * SB + TX].rearrange("k p t -> p k t"), XN.ds,
              reads=tiles, writes=[XN])

    if stop_after < 1:
        S.final.append((xT0[1].ds.h, xT0[1].ds.n))
        S.final.append((xT0[0].ds.h, xT0[0].ds.n))
        S.dma("sp", out_d[0:128, :], xt0[0][:], xt0[0].ds, reads=[xt0[0]])
        S.final.append((xt0[0].ds.h, xt0[0].ds.n))
        S.finish()
        return nc

    gateW = ph("gateW", 0, [128, 2, 2, 16, 128], BF16, 2, True)
    for dd_ in range(2):
        for gi_ in range(2):
            S.dma("pool", gateW[:, dd_, gi_], gate_w[dd_, gi_].rearrange("n i j -> i n j"), gateW.ds, writes=[gateW])
    SET1 = 43072

    def p1set(s):
        base = 16384 + s * SET1
        d = {}
        d["u_sb"] = ph("u_sb%d" % s, base, [128, 1028], F32, 4)
        d["u"] = ph("u%d" % s, base + 4160, [128, SB], F32, 4)
        d["u_bf"] = ph("u_bf%d" % s, base + 8256, [128, SB], BF16, 2)
        o = base + 10304
        for dd in range(2):
            d["r%d" % dd] = ph("rbuf%d_%d" % (s, dd), o, [128, SB], F32, 4)
            d["i%d" % dd] = ph("ibuf%d_%d" % (s, dd), o + 4096, [128, SB], F32, 4, True)
            d["a%d" % dd] = ph("abuf%d_%d" % (s, dd), o + 8192, [128, SB], F32, 4, True)
            d["h%d" % dd] = ph("hscr%d_%d" % (s, dd), o + 12288, [128, SB], F32, 4)
            o += 16384
        return d
    p1sets = [p1set(0), p1set(1)]
    AE5 = AE[:].rearrange("p (s d a n) -> p s d a n", s=2, d=2, a=2)
    it = 0
    for sb_i in range(NSB):
        load_xn(sb_i)
        for n in range(16):
            d = p1sets[it % 2]
            it += 1
            W = wnext(("in", CB_U + n))
            b0 = nb(2)
            b2 = nb(1)

            def mm(e, W=W, b0=b0, b2=b2):
                for kc in range(KC):
                    for (bk, lo, n_) in ((b0, H - 2, 512), (b0 + 1, H + 510, 512), (b2, H + 1022, 4)):
                        ins = e.matmul(pst[:, bk, 0:n_], lhsT=W[:, kc, :], rhs=XN[:, kc, lo:lo + n_],
                                       start=(kc == 0), stop=(kc == KC - 1))
                return ins
            S.op("pe", mm, [W, XN], [bank[b0], bank[b0 + 1], bank[b2]])
            u_sb, u, u_bf = d["u_sb"], d["u"], d["u_bf"]
            S.op("act", lambda e, u_sb=u_sb, b0=b0: e.activation(
                out=u_sb[:, 0:1024].rearrange("p (a b) -> p a b", a=2), in_=pst[:, b0:b0 + 2, :], func=AF.Copy),
                [bank[b0], bank[b0 + 1]], [u_sb])
            S.op("act", lambda e, u_sb=u_sb, b2=b2: e.activation(
                out=u_sb[:, 1024:1028], in_=pst[:, b2, 0:4], func=AF.Copy), [bank[b2]], [u_sb])
            S.op("dve", lambda e, u=u, u_sb=u_sb, n=n: e.tensor_scalar(
                out=u[:], in0=u_sb[:, 0:SB], scalar1=pvec[:, n, 0:1], scalar2=pvec[:, n, 4:5],
                op0=ALU.mult, op1=ALU.add), [u_sb, pvec], [u])
            for tap in range(1, 4):
                S.op("dve", lambda e, u=u, u_sb=u_sb, n=n, tap=tap: e.scalar_tensor_tensor(
                    out=u[:], in0=u_sb[:, tap:tap + SB], scalar=pvec[:, n, tap:tap + 1], in1=u[:],
                    op0=ALU.mult, op1=ALU.add), [u_sb, pvec, u], [u])
            S.op("pool", lambda e, u=u, u_bf=u_bf: e.tensor_copy(out=u_bf[:], in_=u[:]), [u], [u_bf])
            for dd in range(2):
                br = nb(2)
                bi = nb(2)

                def gm(e, dd=dd, n=n, br=br, bi=bi, u_bf=u_bf):
                    for gi, bb in ((0, br), (1, bi)):
                        for tb in range(2):
                            ins = e.matmul(pst[:, bb + tb, :], lhsT=gateW[:, dd, gi, n, :],
                                           rhs=u_bf[:, tb * 512:(tb + 1) * 512], start=True, stop=True)
                    return ins
                S.op("pe", gm, [gateW, u_bf], [bank[br], bank[br + 1], bank[bi], bank[bi + 1]])
                rb, ib = d["r%d" % dd], d["i%d" % dd]
                S.op("act", lambda e, rb=rb, br=br, n=n, dd=dd: e.activation(
                    out=rb[:].rearrange("p (a b) -> p a b", a=2), in_=pst[:, br:br + 2, :], func=AF.Sigmoid,
                    bias=pvec[:, n, 5 + dd:6 + dd], accum_out=rsum[:, dd:dd + 1]),
                    [bank[br], bank[br + 1], pvec], [rb, rsum])
                S.op("act", lambda e, ib=ib, bi=bi, n=n, dd=dd: e.activation(
                    out=ib[:].rearrange("p (a b) -> p a b", a=2), in_=pst[:, bi:bi + 2, :], func=AF.Sigmoid,
                    bias=pvec[:, n, 7 + dd:8 + dd]), [bank[bi], bank[bi + 1], pvec], [ib])
            for dd in range(2):
                rb, ab = d["r%d" % dd], d["a%d" % dd]
                S.op("act", lambda e, rb=rb, ab=ab, n=n, dd=dd: e.activation(
                    out=ab[:], in_=rb[:], func=AF.Exp, scale=cvec[:, n, dd:dd + 1]), [rb, cvec], [ab])
                S.op("act", lambda e, rb=rb, n=n, dd=dd: e.activation(
                    out=rb[:], in_=rb[:], func=AF.Exp, scale=cvec[:, n, 2 + dd:3 + dd]), [rb, cvec], [rb])
                S.op("act", lambda e, n=n, dd=dd, sb_i=sb_i: e.activation(
                    out=AE5[:, sb_i, dd, 0, n:n + 1], in_=rsum[:, dd:dd + 1], func=AF.Exp,
                    scale=cvec[:, n, dd:dd + 1]), [rsum, cvec], [AE])
            for dd in range(2):
                rb = d["r%d" % dd]
                S.op("act", lambda e, rb=rb: e.activation(out=rb[:], in_=rb[:], func=AF.Sqrt, scale=-1.0, bias=1.0),
                     [rb], [rb])
            for dd in range(2):
                rb, ib, ab, hs = d["r%d" % dd], d["i%d" % dd], d["a%d" % dd], d["h%d" % dd]
                S.op("pool", lambda e, ib=ib, u=u: e.tensor_tensor(out=ib[:], in0=ib[:], in1=u[:], op=ALU.mult),
                     [ib, u], [ib])
                S.op("pool", lambda e, ib=ib, rb=rb: e.tensor_tensor(out=ib[:], in0=ib[:], in1=rb[:], op=ALU.mult),
                     [ib, rb], [ib])
                if dd == 0:
                    S.op("dve", lambda e, ab=ab, ib=ib, hs=hs: e.tensor_tensor_scan(
                        out=hs[:], data0=ab[:], data1=ib[:], initial=0.0, op0=ALU.mult, op1=ALU.add), [ab, ib], [hs])
                    S.op("pool", lambda e, hs=hs, n=n, sb_i=sb_i: e.tensor_copy(
                        out=AE5[:, sb_i, 0, 1, n:n + 1], in_=hs[:, SB - 1:SB]), [hs], [AE])
                else:
                    S.op("dve", lambda e, ab=ab, ib=ib, hs=hs: e.tensor_tensor_scan(
                        out=hs[:, ::-1], data0=ab[:, ::-1], data1=ib[:, ::-1], initial=0.0,
                        op0=ALU.mult, op1=ALU.add), [ab, ib], [hs])
                    S.op("pool", lambda e, hs=hs, n=n, sb_i=sb_i: e.tensor_copy(
                        out=AE5[:, sb_i, 1, 1, n:n + 1], in_=hs[:, 0:1]), [hs], [AE])
                S.dma("sp", ab_d[sb_i, n, 2 * dd], ab[:], ab.ds, reads=[ab], writes=[ab_t[sb_i][n][2 * dd]])
                S.dma("sp", ab_d[sb_i, n, 2 * dd + 1], ib[:], ib.ds, reads=[ib], writes=[ab_t[sb_i][n][2 * dd + 1]])

    S.dma("sp", ag_in.ap(), AE[:], AE.ds, reads=[AE], writes=[agin_t])
    cc_sem = S.dsem("cc_sem")
    S.dma("pool", None, None, cc_sem, reads=[agin_t], writes=[agout_t], inc=1,
          fn=lambda e: e.collective_compute("AllGather", ALU.bypass, replica_groups=[list(range(NCORES))],
                                            ins=[ag_in.ap().opt()], outs=[ag_out.ap().opt()]))
    S.dma("sp", G[:], ag_out.ap().rearrange("(r p) f -> p r f", p=128), G.ds, reads=[agout_t], writes=[G])
    G6 = G[:].rearrange("p r (s d a n) -> p r s d a n", s=2, d=2, a=2)

    def carry_chain():
        for v in range(16):
            r, s = divmod(v, 2)
            S.op("dve", lambda e, v=v, r=r, s=s: e.tensor_tensor(
                out=htmp[:, 0, :], in0=Hf[:, v, :], in1=G6[:, r, s, 0, 0, :], op=ALU.mult), [Hf, G], [htmp])
            S.op("dve", lambda e, v=v, r=r, s=s: e.tensor_tensor(
                out=Hf[:, v + 1, :], in0=htmp[:, 0, :], in1=G6[:, r, s, 0, 1, :], op=ALU.add), [htmp, G], [Hf])
        for v in range(15, 0, -1):
            r, s = divmod(v, 2)
            S.op("dve", lambda e, v=v, r=r, s=s: e.tensor_tensor(
                out=htmp[:, 1, :], in0=Hb[:, v, :], in1=G6[:, r, s, 1, 0, :], op=ALU.mult), [Hb, G], [htmp])
            S.op("dve", lambda e, v=v, r=r, s=s: e.tensor_tensor(
                out=Hb[:, v - 1, :], in0=htmp[:, 1, :], in1=G6[:, r, s, 1, 1, :], op=ALU.add), [htmp, G], [Hb])
        Hin4 = Hin[:].rearrange("p (s d n) -> p s d n", s=2, d=2)
        for s in range(2):
            for dd, HH in ((0, Hf), (1, Hb)):
                S.op("dve", lambda e, s=s, HH=HH: e.tensor_tensor(
                    out=htmp[:], in0=HH[:, 0:16, :], in1=sel[:, s, :].unsqueeze(2).broadcast_to([128, 16, 16]),
                    op=ALU.mult), [HH, sel], [htmp])
                S.op("dve", lambda e, s=s, dd=dd: e.tensor_reduce(
                    out=Hin4[:, s, dd, :], in_=htmp[:].rearrange("p v n -> p n v"), axis=AX.X, op=ALU.add),
                    [htmp], [Hin])
    Hin4 = Hin[:].rearrange("p (s d n) -> p s d n", s=2, d=2)

    if stop_after < 2:
        carry_chain()
        hd = S.dsem("d_hin")
        S.final.append(S.dma("sp", dbg["hin"], Hin[:], hd, reads=[Hin]))
        S.dma("sp", out_d[0:128, 0:128], G[:, 0, :], G.ds, reads=[G])
        S.final.append((G.ds.h, G.ds.n))
        for dd in range(2):
            for s in range(2):
                for k in ("a", "i"):
                    t = p1sets[s]["%s%d" % (k, dd)]
                    S.final.append((t.ds.h, t.ds.n))
        S.finish()
        return nc

    P2B = 32768
    cs_t = ph("cs_t", P2B, [32, 2, TX], F32, 4, True)
    kT = ph("kT", P2B + 10240, [128, 4, TX], BF16, 2)
    Vt = ph("Vt", P2B + 20480, [128, 10, 512], BF16, 2)
    qT = [ph("qT%d" % i, P2B + 30720 + i * 8192, [128, 4, SB], BF16, 2) for i in range(2)]
    PT = [ph("PT%d" % i, P2B + 47104 + i * 3072, [128, 3, 512], BF16, 2) for i in range(2)]
    qf = ph("qf", P2B + 53248, [32, TX], F32, 4)
    sw = ph("sw", P2B + 58368, [32, TX], F32, 4)
    Dt = [ph("Dt%d" % i, P2B + 63488 + i * 2048, [128, 512], F32, 4) for i in range(2)]
    ot = [ph("ot%d" % i, P2B + 67584 + i * 2048, [128, 512], F32, 4) for i in range(2)]
    P3B = 65536
    RL = [[ph("RL%d_%d" % (i, j), P3B + i * 16384 + j * 4096, [128, SB], F32, 4, True) for j in range(4)]
          for i in range(2)]
    hfb = ph("hfb", P3B + 32768, [128, SB], F32, 4)
    hbb = ph("hbb", P3B + 36864, [128, SB], F32, 4)
    sgb = ph("sgb", P3B + 40960, [128, SB], F32, 4)
    sm = [ph("sm%d" % i, EXTRA + i * 4096, [128, SB], F32, 4) for i in range(2)]
    t12 = [ph("t12_%d" % i, EXTRA + 8192 + i * 4096, [128, SB], F32, 4) for i in range(2)]
    WO = [ph("WO%d" % i, i * 16384, [128, KC, 512], BF16, 2, True) for i in range(4)]
    otile = [ph("otile%d" % i, EXTRA + i * 8192, [128, D], F32, 4, True) for i in range(2)]
    xt5 = [S.sb("xt5_%d" % i, XN_OFF + i * 8192, [128, D], F32, 4, True) for i in range(3)]
    wpost_bc = S.sb("wpost_bc", XN_OFF + 24576, [128, D], F32, 4, True)
    shuf = [(i + 16) % 32 for i in range(32)]

    def mmq(e, W, b0):
        for kc in range(KC):
            for tb in range(2):
                ins = e.matmul(pst[:, b0 + tb, :], lhsT=W[:, kc, :], rhs=XN[:, kc, H + tb * 512:H + (tb + 1) * 512],
                               start=(kc == 0), stop=(kc == KC - 1))
        return ins

    def rope(regions, Tn, dst, csoff):
        rb = []
        for (b0, nbk, c0, ncols) in regions:
            rb += [bank[b0 + k] for k in range(nbk)]
        for (b0, nbk, c0, ncols) in regions:
            if nbk == 2:
                src_all = pst[:, b0:b0 + 2, :]
                src_lo = pst[0:32, b0:b0 + 2, :]
                d_all = dst[:, c0:c0 + ncols].rearrange("p (a b) -> p a b", a=2)
                d_lo = qf[:, c0:c0 + ncols].rearrange("p (a b) -> p a b", a=2)
            else:
                src_all = pst[:, b0, 0:ncols]
                src_lo = pst[0:32, b0, 0:ncols]
                d_all = dst[:, c0:c0 + ncols]
                d_lo = qf[:, c0:c0 + ncols]
            S.op("act", lambda e, s_=src_all, d_=d_all: e.activation(out=d_, in_=s_, func=AF.Copy), rb, [dst_tile[0]])
            S.op("act", lambda e, s_=src_lo, d_=d_lo: e.activation(out=d_, in_=s_, func=AF.Copy), rb, [qf])
        S.op("dve", lambda e: e.stream_shuffle(out=sw[:, 0:Tn], in_=qf[:, 0:Tn], mask=shuf), [qf], [sw])
        S.op("pool", lambda e: e.tensor_tensor(out=qf[:, 0:Tn], in0=qf[:, 0:Tn], in1=cs_t[:, 0, csoff:csoff + Tn],
                                               op=ALU.mult), [qf, cs_t], [qf])
        S.op("pool", lambda e: e.tensor_tensor(out=sw[:, 0:Tn], in0=sw[:, 0:Tn], in1=cs_t[:, 1, csoff:csoff + Tn],
                                               op=ALU.mult), [sw, cs_t], [sw])
        S.op("pool", lambda e: e.tensor_tensor(out=dst[0:32, 0:Tn], in0=qf[:, 0:Tn], in1=sw[:, 0:Tn], op=ALU.add),
             [qf, sw], [dst_tile[0]])

    dst_tile = [None]
    att_it = 0
    for sb_i in range(NSB):
        if sb_i > 0 or True:
            load_xn(sb_i)
        if sb_i == 0:
            carry_chain()
        S.dma("sp", cs_t[:], cs_d[:, :, sb_i * SB: sb_i * SB + TX], cs_t.ds, writes=[cs_t])
        for g in range(4):
            W = wnext(("in", CB_K + g))
            b0 = nb(2)
            b2 = nb(1)

            def mmk(e, W=W, b0=b0, b2=b2):
                for kc in range(KC):
                    for (bk, lo, n_) in ((b0, 0, 512), (b0 + 1, 512, 512), (b2, 1024, 256)):
                        ins = e.matmul(pst[:, bk, 0:n_], lhsT=W[:, kc, :], rhs=XN[:, kc, lo:lo + n_],
                                       start=(kc == 0), stop=(kc == KC - 1))
                return ins
            S.op("pe", mmk, [W, XN], [bank[b0], bank[b0 + 1], bank[b2]])
            dst_tile[0] = kT
            rope([(b0, 2, 0, 1024), (b2, 1, 1024, 256)], TX, kT[:, g, :], 0)
        W4 = [wnext(("in", CB_V + g)) for g in range(4)]
        for j in range(10):
            bv = nb(1)

            def mmv(e, j=j, bv=bv, W4=W4):
                for g in range(4):
                    for kc in range(KC):
                        ins = e.matmul(pst[:, bv, g * 128:(g + 1) * 128], lhsT=XN[:, kc, j * 128:(j + 1) * 128],
                                       rhs=W4[g][:, kc, :], start=(kc == 0), stop=(kc == KC - 1))
                return ins
            S.op("pe", mmv, W4 + [XN], [bank[bv]])
            if j % 2 == 0:
                S.op("act", lambda e, j=j, bv=bv: e.activation(out=Vt[:, j, :], in_=pst[:, bv, :], func=AF.Copy),
                     [bank[bv]], [Vt])
            else:
                S.op("dve", lambda e, j=j, bv=bv: e.tensor_copy(out=Vt[:, j, :], in_=pst[:, bv, :]),
                     [bank[bv]], [Vt])
        if debug:
            S.dma("sp", dbg["kT"][sb_i], kT[:], dbg_sem, reads=[kT])
            S.dma("sp", dbg["v"][sb_i], Vt[:], dbg_sem, reads=[Vt])
        for g in range(4):
            qTg = qT[g % 2]
            for hh in range(4):
                h = 4 * g + hh
                W = wnext(("in", CB_Q + h))
                b0 = nb(2)

                S.op("pe", lambda e, W=W, b0=b0: mmq(e, W, b0), [W, XN], [bank[b0], bank[b0 + 1]])
                dst_tile[0] = qTg
                rope([(b0, 2, 0, 1024)], SB, qTg[:, hh, :], H)
                W = wnext(("in", CB_GA + h))
                b0 = nb(2)
                S.op("pe", lambda e, W=W, b0=b0: mmq(e, W, b0), [W, XN], [bank[b0], bank[b0 + 1]])
                S.op("act", lambda e, h=h, b0=b0: e.activation(
                    out=YB[:, h, :].rearrange("p (a b) -> p a b", a=2), in_=pst[:, b0:b0 + 2, :], func=AF.Silu),
                    [bank[b0], bank[b0 + 1]], [YB])
            if debug:
                S.dma("sp", dbg["qT"][sb_i, g], qTg[:], dbg_sem, reads=[qTg])
            for n in range(8):
                pt = PT[att_it % 2]
                dtt = Dt[att_it % 2]
                ott = ot[att_it % 2]
                att_it += 1
                sbk = [nb(1) for _ in range(3)]

                def mms(e, g=g, n=n, sbk=sbk, qTg=qTg):
                    for dj in range(3):
                        ins = e.matmul(pst[:, sbk[dj], :], lhsT=kT[:, g, (n + dj) * 128:(n + dj + 1) * 128],
                                       rhs=qTg[:, :, n * 128:(n + 1) * 128], start=True, stop=True)
                    return ins
                S.op("pe", mms, [kT, qTg], [bank[b] for b in sbk])
                for dj in range(3):
                    S.op("act", lambda e, dj=dj, pt=pt, sbk=sbk: e.activation(
                        out=pt[:, dj, :], in_=pst[:, sbk[dj], :], func=AF.Exp, scale=SCALE), [bank[sbk[dj]]], [pt])
                mprev = 2 if (sb_i == 0 and n == 0) else 0
                mnext = 3 if (sb_i == NSB - 1 and n == 7) else 1
                for dj, mi in ((0, mprev), (2, mnext)):
                    S.op("pool", lambda e, dj=dj, mi=mi, pt=pt: e.tensor_tensor(
                        out=pt[:, dj, :].rearrange("p (a b) -> p a b", a=4),
                        in0=pt[:, dj, :].rearrange("p (a b) -> p a b", a=4),
                        in1=masks[:, mi, :].unsqueeze(1).broadcast_to([128, 4, 128]), op=ALU.mult), [pt, masks], [pt])
                bd = nb(1)
                bo = nb(1)

                def mmd(e, pt=pt, bd=bd):
                    for dj in range(3):
                        ins = e.matmul(pst[:, bd, :], lhsT=ones[:], rhs=pt[:, dj, :], start=(dj == 0), stop=(dj == 2))
                    return ins
                S.op("pe", mmd, [ones, pt], [bank[bd]])

                def mmo(e, pt=pt, bo=bo, g=g, n=n):
                    for dj in range(3):
                        ins = e.matmul(pst[:, bo, :], lhsT=Vt[:, n + dj, g * 128:(g + 1) * 128], rhs=pt[:, dj, :],
                                       start=(dj == 0), stop=(dj == 2))
                    return ins
                S.op("pe", mmo, [Vt, pt], [bank[bo]])
                S.op("dve", lambda e, dtt=dtt, bd=bd, g=g: e.tensor_tensor(
                    out=dtt[:].rearrange("p (a b) -> p a b", a=4), in0=pst[:, bd, :].rearrange("p (a b) -> p a b", a=4),
                    in1=esbc[:, 4 * g:4 * g + 4, :], op=ALU.add), [bank[bd], esbc], [dtt])
                S.op("act", lambda e, dtt=dtt: e.activation(out=dtt[:], in_=dtt[:], func=AF.Ln), [dtt], [dtt])
                S.op("act", lambda e, dtt=dtt: e.activation(out=dtt[:], in_=dtt[:], func=AF.Exp, scale=-1.0),
                     [dtt], [dtt])
                S.op("dve", lambda e, ott=ott, bo=bo, dtt=dtt: e.tensor_tensor(
                    out=ott[:], in0=pst[:, bo, :], in1=dtt[:], op=ALU.mult), [bank[bo], dtt], [ott])
                S.op("pool", lambda e, ott=ott, g=g, n=n: e.tensor_tensor(
                    out=YB[:, 4 * g:4 * g + 4, n * 128:(n + 1) * 128],
                    in0=YB[:, 4 * g:4 * g + 4, n * 128:(n + 1) * 128],
                    in1=ott[:].rearrange("p (a b) -> p a b", a=4), op=ALU.mult), [YB, ott], [YB])
        if debug:
            S.dma("sp", dbg["yb"][sb_i], YB[:], dbg_sem, reads=[YB])
        if stop_after < 3:
            continue
        def reload(n, slot):
            for j in range(4):
                t = RL[slot][j]
                S.dma("sp", t[:], ab_d[sb_i, n, j], t.ds, reads=[ab_t[sb_i][n][j]], writes=[t])
        reload(0, 0)
        for n in range(16):
            if n + 1 < 16:
                reload(n + 1, (n + 1) % 2)
            af, bf_, ab_, bb_ = RL[n % 2]
            W = wnext(("in", CB_GL + n))
            b0 = nb(2)
            S.op("pe", lambda e, W=W, b0=b0: mmq(e, W, b0), [W, XN], [bank[b0], bank[b0 + 1]])
            S.op("act", lambda e, b0=b0: e.activation(
                out=sgb[:].rearrange("p (a b) -> p a b", a=2), in_=pst[:, b0:b0 + 2, :], func=AF.Silu),
                [bank[b0], bank[b0 + 1]], [sgb])
            S.op("dve", lambda e, af=af, bf_=bf_, n=n, sb_i=sb_i: e.tensor_tensor_scan(
                out=hfb[:], data0=af[:], data1=bf_[:], initial=Hin4[:, sb_i, 0, n:n + 1],
                op0=ALU.mult, op1=ALU.add), [af, bf_, Hin], [hfb])
            S.op("dve", lambda e, ab_=ab_, bb_=bb_, n=n, sb_i=sb_i: e.tensor_tensor_scan(
                out=hbb[:, ::-1], data0=ab_[:, ::-1], data1=bb_[:, ::-1], initial=Hin4[:, sb_i, 1, n:n + 1],
                op0=ALU.mult, op1=ALU.add), [ab_, bb_, Hin], [hbb])
            S.op("pool", lambda e: e.tensor_tensor(out=hfb[:], in0=hfb[:], in1=hbb[:], op=ALU.add), [hfb, hbb], [hfb])
            S.op("pool", lambda e, n=n: e.tensor_tensor(out=YA[:, n, :], in0=hfb[:], in1=sgb[:], op=ALU.mult),
                 [hfb, sgb], [YA])
        if debug:
            S.dma("sp", dbg["ya"][sb_i], YA[:], dbg_sem, reads=[YA])
        if stop_after < 4:
            continue
        for f in range(16):
            for half, (wk, mk_cb, Y) in enumerate(((("a", f), CB_ML + f, YA), (("b", f), CB_MA + f, YB))):
                Wp = wnext(wk)
                Wm = wnext(("in", mk_cb))
                bpj = nb(2)
                bm = nb(2)

                def mmp(e, Wp=Wp, bpj=bpj, Y=Y):
                    for kc in range(KC):
                        for tb in range(2):
                            ins = e.matmul(pst[:, bpj + tb, :], lhsT=Wp[:, kc, :], rhs=Y[:, kc, tb * 512:(tb + 1) * 512],
                                           start=(kc == 0), stop=(kc == KC - 1))
                    return ins
                S.op("pe", mmp, [Wp, Y], [bank[bpj], bank[bpj + 1]])
                S.op("pe", lambda e, Wm=Wm, bm=bm: mmq(e, Wm, bm), [Wm, XN], [bank[bm], bank[bm + 1]])
                S.op("act", lambda e, half=half, bm=bm: e.activation(
                    out=sm[half][:].rearrange("p (a b) -> p a b", a=2), in_=pst[:, bm:bm + 2, :], func=AF.Sigmoid),
                    [bank[bm], bank[bm + 1]], [sm[half]])
                S.op("dve", lambda e, half=half, bpj=bpj: e.tensor_tensor(
                    out=t12[half][:].rearrange("p (a b) -> p a b", a=2), in0=pst[:, bpj:bpj + 2, :],
                    in1=sm[half][:].rearrange("p (a b) -> p a b", a=2), op=ALU.mult),
                    [bank[bpj], bank[bpj + 1], sm[half]], [t12[half]])
            S.op("pool", lambda e, f=f: e.tensor_tensor(out=MG[:, f, :], in0=t12[0][:], in1=t12[1][:], op=ALU.add),
                 [t12[0], t12[1]], [MG])
        if debug:
            S.dma("sp", dbg["mg"][sb_i], MG[:], dbg_sem, reads=[MG])
        if stop_after < 5:
            continue
        for cg in range(4):
            S.dma("pool", WO[cg][:], w_o[:, cg * 512:(cg + 1) * 512].rearrange("(kc p) n -> p kc n", p=128),
                  WO[cg].ds, writes=[WO[cg]])
        S.dma("sp", wpost_bc[:], wpost_d.partition_broadcast(128), wpost_bc.ds, writes=[wpost_bc])
        for i in range(8):
            xt = xt5[i % 3]
            ot_ = otile[i % 2]
            r0 = H + sb_i * SB + i * 128
            S.dma("sp", xt[:], x_ext[r0:r0 + 128, :], xt.ds, writes=[xt])
            bq = [nb(2), nb(2)]
            bks = [bq[0], bq[0] + 1, bq[1], bq[1] + 1]

            def mmf(e, i=i, bks=bks):
                for cg in range(4):
                    for kc in range(KC):
                        ins = e.matmul(pst[:, bks[cg], :], lhsT=MG[:, kc, i * 128:(i + 1) * 128],
                                       rhs=WO[cg][:, kc, :], start=(kc == 0), stop=(kc == KC - 1))
                return ins
            S.op("pe", mmf, [MG] + WO, [bank[b] for b in bks])
            for cg in range(4):
                S.op("act", lambda e, cg=cg, ot_=ot_, bks=bks: e.activation(
                    out=ot_[:, cg * 512:(cg + 1) * 512], in_=pst[:, bks[cg], :], func=AF.Square,
                    accum_out=ss_t[:, 4 + cg:5 + cg]), [bank[bks[cg]]], [ot_, ss_t])
            S.op("dve", lambda e: e.tensor_reduce(out=rs_t[:, 0:1], in_=ss_t[:, 4:8], axis=AX.X, op=ALU.add),
                 [ss_t], [rs_t])
            S.op("dve", lambda e: e.tensor_scalar(out=rs_t[:, 0:1], in0=rs_t[:, 0:1], scalar1=1.0 / D, scalar2=EPS,
                                                  op0=ALU.mult, op1=ALU.add), [rs_t], [rs_t])
            S.op("act", lambda e: e.activation(out=rs_t[:, 0:1], in_=rs_t[:, 0:1], func=AF.Sqrt), [rs_t], [rs_t])
            S.op("dve", lambda e: e.reciprocal(out=rs_t[:, 0:1], in_=rs_t[:, 0:1]), [rs_t], [rs_t])
            for cg in range(4):
                S.op("dve", lambda e, cg=cg, ot_=ot_, bks=bks: e.scalar_tensor_tensor(
                    out=ot_[:, cg * 512:(cg + 1) * 512], in0=pst[:, bks[cg], :], scalar=rs_t[:, 0:1],
                    in1=wpost_bc[:, cg * 512:(cg + 1) * 512], op0=ALU.mult, op1=ALU.mult),
                    [bank[bks[cg]], rs_t, wpost_bc], [ot_])
            S.op("pool", lambda e, ot_=ot_, xt=xt: e.tensor_tensor(out=ot_[:], in0=ot_[:], in1=xt[:], op=ALU.add),
                 [ot_, xt], [ot_])
            o0 = sb_i * SB + i * 128
            S.final.append(S.dma("sp", out_d[o0:o0 + 128, :], ot_[:], ot_.ds, reads=[ot_]))

    if debug:
        S.final.append((dbg_sem.h, dbg_sem.n))
    if stop_after < 5:
        S.dma("sp", out_d[0:128, 0:128], G[:, 0, :], G.ds, reads=[G])
        S.final.append((G.ds.h, G.ds.n))
    S.finish()
    return nc


def make_in_maps(x, norm_pre_w, w_in, conv_w, conv_b, lru_w_r, lru_b_r, lru_w_i, lru_b_i, lru_lambda,
                 attn_sink, w_proj_a, w_proj_b, w_out, norm_post_w):
    f32 = np.float32
    x2 = np.asarray(x, f32).reshape(S_FULL, D)
    xpad = np.zeros((S_FULL + 2 * H, D), f32)
    xpad[H:H + S_FULL] = x2
    w_in2 = np.ascontiguousarray(np.asarray(w_in, f32)[0])
    w_a2 = np.ascontiguousarray(np.asarray(w_proj_a, f32)[0])
    w_b2 = np.ascontiguousarray(np.asarray(w_proj_b, f32)[0])
    w_o2 = np.ascontiguousarray(np.asarray(w_out, f32)[0])
    gate_w = np.ascontiguousarray(np.stack([np.asarray(lru_w_r, f32)[0], np.asarray(lru_w_i, f32)[0]], axis=1))
    vecs = [np.asarray(conv_w, f32)[0, t] for t in range(4)] + [np.asarray(conv_b, f32)[0]]
    vecs += [np.asarray(lru_b_r, f32)[0, 0], np.asarray(lru_b_r, f32)[0, 1]]
    vecs += [np.asarray(lru_b_i, f32)[0, 0], np.asarray(lru_b_i, f32)[0, 1]]
    vecs += [np.asarray(lru_lambda, f32)[0, 0], np.asarray(lru_lambda, f32)[0, 1]]
    pvec = np.ascontiguousarray(np.stack([v.reshape(16, 128).T for v in vecs], axis=-1))
    wpre = np.ascontiguousarray(np.asarray(norm_pre_w, f32)[0])
    wpost = np.ascontiguousarray(np.asarray(norm_post_w, f32)[0])
    sink = np.ascontiguousarray(np.asarray(attn_sink, f32)[0])
    ident = np.eye(128, dtype=f32).astype(ml_dtypes.bfloat16)
    inv_freq = (np.float32(500000.0) ** (-np.arange(0, 32, 2, dtype=f32) / np.float32(32))).astype(f32)
    kk = np.arange(128)[:, None]
    qq = np.arange(128)[None, :]
    tri_prev = (kk >= qq).astype(f32)
    tri_next = (kk <= qq).astype(f32)
    maps = []
    for c in range(NCORES):
        pos = (np.arange(TE, dtype=np.int64) + c * T - H).astype(f32)
        ang = pos[:, None] * inv_freq[None, :]
        cos, sin = np.cos(ang).astype(f32), np.sin(ang).astype(f32)
        cs = np.zeros((32, 2, TE), f32)
        cs[0:16, 0] = cos.T
        cs[16:32, 0] = cos.T
        cs[0:16, 1] = -sin.T
        cs[16:32, 1] = sin.T
        masks = np.stack([tri_prev, tri_next, tri_prev * (0.0 if c == 0 else 1.0),
                          tri_next * (0.0 if c == NCORES - 1 else 1.0)], axis=1).astype(ml_dtypes.bfloat16)
        sel = np.zeros((128, 2, 16), f32)
        for s in range(2):
            sel[:, s, 2 * c + s] = 1.0
        maps.append({
            "x_ext": np.ascontiguousarray(xpad[c * T: c * T + TE]),
            "w_in": w_in2, "w_a": w_a2, "w_b": w_b2, "w_o": w_o2, "gate_w": gate_w, "pvec": pvec,
            "wpre": wpre, "wpost": wpost, "sink": sink, "cs": cs, "masks": np.ascontiguousarray(masks),
            "ident": ident, "sel": sel,
        })
    return maps


_NC_CACHE = {}


def kernel(**inputs):
    maps = make_in_maps(**inputs)
    if "nc" not in _NC_CACHE:
        _NC_CACHE["nc"] = build_program()
    res = run_bass_kernel_spmd(_NC_CACHE["nc"], maps, core_ids=list(range(NCORES)))
    out = np.concatenate([np.asarray(r["out"], np.float32) for r in res.results], axis=0)
    return out.reshape(1, S_FULL, D)
```

```python
import numpy as np
import ml_dtypes
import concourse.bass as bass
import concourse.mybir as mybir
from concourse.bass_utils import run_bass_kernel_spmd

F32 = mybir.dt.float32
BF16 = mybir.dt.bfloat16
ALU = mybir.AluOpType
AF = mybir.ActivationFunctionType
AX = mybir.AxisListType

NCORES = 8
S_FULL = 16384
D = 2048
T = S_FULL // NCORES
SB = 1024
NSB = T // SB
H = 128
TX = SB + 2 * H
TE = T + 2 * H
KC = 16
IN_W = 13312
EPS = 1e-6
LRU_C = 8.0
SCALE = 128 ** -0.5
CB_U, CB_GL, CB_Q, CB_K, CB_V, CB_GA, CB_ML, CB_MA = 0, 16, 32, 48, 52, 56, 72, 88

SBUF_BASE = 16640
SBUF_END = 229344


class Tile:
    __slots__ = ("name", "t", "lo", "hi", "w", "r", "over", "ds")

    def __init__(self, name, t, lo, hi):
        self.name, self.t, self.lo, self.hi = name, t, lo, hi
        self.w = None
        self.r = {}
        self.over = [self]
        self.ds = None

    def __getitem__(self, k):
        return self.t[k]


class DSem:
    def __init__(self, h):
        self.h, self.n = h, 0


class Sched:
    def __init__(self, nc):
        self.nc = nc
        self.E = {"pe": nc.tensor, "act": nc.scalar, "dve": nc.vector, "pool": nc.gpsimd, "sp": nc.sync}
        self.sem = {e: nc.alloc_semaphore("sem_" + e) for e in ("pe", "act", "dve", "pool")}
        self.cnt = {e: 0 for e in self.sem}
        self.prog = {e: [] for e in self.E}
        self.seen = {e: {} for e in self.E}
        self.sb_tiles = []
        self.final = []
        self.nops = 0
        self.nwaits = 0

    def sb(self, name, off, shape, dtype, esz, dsem=False):
        n = 1
        for s in shape[1:]:
            n *= s
        assert SBUF_BASE <= off and off + n * esz <= SBUF_END, (name, off, n * esz)
        t = self.nc.alloc_sbuf_tensor_at(name, list(shape), dtype, offset=off)
        tl = Tile(name, t, off, off + n * esz)
        for o in self.sb_tiles:
            if o.lo < tl.hi and tl.lo < o.hi:
                o.over.append(tl)
                tl.over.append(o)
        self.sb_tiles.append(tl)
        if dsem:
            tl.ds = self.dsem("d_" + name)
        return tl

    def raw(self, name, t=None, dsem=False):
        tl = Tile(name, t, 0, 0)
        if dsem:
            tl.ds = self.dsem("d_" + name)
        return tl

    def dsem(self, name):
        return DSem(self.nc.alloc_semaphore(name))

    def _deps(self, eng, reads, writes, extra=()):
        deps = {}

        def add(ev):
            if ev is None:
                return
            s, v = ev
            k = s.name
            if k not in deps or deps[k][1] < v:
                deps[k] = (s, v)

        for t in reads:
            for o in t.over:
                add(o.w)
        for t in writes:
            for o in t.over:
                add(o.w)
                for ev in o.r.values():
                    add(ev)
        for ev in extra:
            add(ev)
        out = []
        seen = self.seen[eng]
        own = self.sem[eng].name if eng in self.sem else None
        for k, (s, v) in deps.items():
            if eng == "pe" and k == own:
                continue
            if seen.get(k, 0) >= v:
                continue
            seen[k] = v
            out.append((s, v))
        return out

    def _commit(self, ev, reads, writes):
        k = ev[0].name
        for t in writes:
            t.w = ev
            t.r = {}
        for t in reads:
            if k not in t.r or t.r[k][1] < ev[1]:
                t.r[k] = ev

    def op(self, eng, fn, reads=(), writes=(), extra=()):
        waits = self._deps(eng, reads, writes, extra)
        sem = self.sem[eng]
        self.cnt[eng] += 1
        ev = (sem, self.cnt[eng])
        self.nops += 1
        self.nwaits += len(waits)

        def emit(e, waits=waits, fn=fn, sem=sem):
            for s, v in waits:
                e.wait_ge(s, v)
            fn(e).then_inc(sem, 1)

        self.prog[eng].append(emit)
        self._commit(ev, reads, writes)
        return ev

    def dma(self, q, out_ap, in_ap, ds, reads=(), writes=(), extra=(), inc=16, fn=None):
        ex = list(extra)
        if ds.n > 0:
            ex.append((ds.h, ds.n))
        waits = self._deps(q, reads, writes, ex)
        ds.n += inc
        ev = (ds.h, ds.n)
        self.nops += 1
        self.nwaits += len(waits)

        def emit(e, waits=waits, out_ap=out_ap, in_ap=in_ap, h=ds.h, inc=inc, fn=fn):
            for s, v in waits:
                e.wait_ge(s, v)
            ins = fn(e) if fn is not None else e.dma_start(out=out_ap, in_=in_ap)
            ins.then_inc(h, inc)

        self.prog[q].append(emit)
        self._commit(ev, reads, writes)
        return ev

    def finish(self):
        fin = {}
        for s, v in self.final:
            if s.name not in fin or fin[s.name][1] < v:
                fin[s.name] = (s, v)

        def run(name):
            def f(e):
                for fn in self.prog[name]:
                    fn(e)
                if name == "sp":
                    for s, v in fin.values():
                        e.wait_ge(s, v)
            return f

        with self.nc.Block() as block:
            block.tensor(run("pe"))
            block.scalar(run("act"))
            block.vector(run("dve"))
            block.gpsimd(run("pool"))
            block.sync(run("sp"))


def build_program(stop_after=5, debug=False):
    nc = bass.Bass("TRN2", target_bir_lowering=False)
    S = Sched(nc)
    dk = "ExternalOutput" if debug else "Internal"

    def din(name, shape, dt=F32):
        return nc.dram_tensor(name, list(shape), dt, kind="ExternalInput").ap()

    x_ext = din("x_ext", [TE, D])
    w_in = din("w_in", [D, IN_W])
    w_a = din("w_a", [D, D])
    w_b = din("w_b", [D, D])
    w_o = din("w_o", [D, D])
    gate_w = din("gate_w", [2, 2, 16, 128, 128])
    pvec_d = din("pvec", [128, 16, 11])
    wpre_d = din("wpre", [D])
    wpost_d = din("wpost", [D])
    sink_d = din("sink", [16])
    cs_d = din("cs", [32, 2, TE])
    masks_d = din("masks", [128, 4, 128], BF16)
    ident_d = din("ident", [128, 128], BF16)
    sel_d = din("sel", [128, 2, 16])
    out_d = nc.dram_tensor("out", [T, D], F32, kind="ExternalOutput").ap()
    xnT_d = nc.dram_tensor("xnT_d", [KC, 128, TE], BF16, kind=dk).ap()
    ab_d = nc.dram_tensor("ab_d", [NSB, 16, 4, 128, SB], F32, kind=dk).ap()
    ag_in = nc.dram_tensor("ag_in", [128, 128], F32)
    ag_out = nc.dram_tensor("ag_out", [NCORES * 128, 128], F32)
    dbg = {}
    if debug:
        dbg["ya"] = nc.dram_tensor("dbg_ya", [NSB, 128, 16, SB], BF16, kind="ExternalOutput").ap()
        dbg["yb"] = nc.dram_tensor("dbg_yb", [NSB, 128, 16, SB], BF16, kind="ExternalOutput").ap()
        dbg["mg"] = nc.dram_tensor("dbg_mg", [NSB, 128, 16, SB], BF16, kind="ExternalOutput").ap()
        dbg["hin"] = nc.dram_tensor("dbg_hin", [128, 64], F32, kind="ExternalOutput").ap()
        dbg["kT"] = nc.dram_tensor("dbg_kT", [NSB, 128, 4, TX], BF16, kind="ExternalOutput").ap()
        dbg["v"] = nc.dram_tensor("dbg_v", [NSB, 128, 10, 512], BF16, kind="ExternalOutput").ap()
        dbg["qT"] = nc.dram_tensor("dbg_qT", [NSB, 4, 128, 4, SB], BF16, kind="ExternalOutput").ap()
    dbg_sem = S.dsem("d_dbg")
    xnT_t = [S.raw("xnT_d%d" % i) for i in range(18)]
    ab_t = [[[S.raw("ab_d_%d_%d_%d" % (s, n, j)) for j in range(4)] for n in range(16)] for s in range(NSB)]
    agin_t = S.raw("ag_in")
    agout_t = S.raw("ag_out")

    P_OFF = SBUF_BASE
    poff = [P_OFF]

    def pers(name, shape, dt, esz, dsem=False):
        n = int(np.prod(shape[1:])) * esz
        t = S.sb(name, poff[0], shape, dt, esz, dsem)
        poff[0] += (n + 63) // 64 * 64
        return t

    pvec = pers("pvec", [128, 16, 11], F32, 4, True)
    cvec = pers("cvec", [128, 16, 4], F32, 4)
    sinkb = pers("sinkb", [128, 16], F32, 4, True)
    esink = pers("esink", [128, 16], F32, 4)
    esbc = pers("esbc", [128, 16, 128], F32, 4)
    masks = pers("masks", [128, 4, 128], BF16, 2, True)
    ident = pers("ident", [128, 128], BF16, 2, True)
    ones = pers("ones", [128, 128], BF16, 2)
    AE = pers("AE", [128, 128], F32, 4, True)
    G = pers("G", [128, NCORES, 128], F32, 4, True)
    Hf = pers("Hf", [128, 17, 16], F32, 4)
    Hb = pers("Hb", [128, 17, 16], F32, 4)
    Hin = pers("Hin", [128, 64], F32, 4)
    sel = pers("sel", [128, 2, 16], F32, 4, True)
    htmp = pers("htmp", [128, 16, 16], F32, 4)
    rsum = pers("rsum", [128, 4], F32, 4)
    ss_t = pers("ss", [128, 8], F32, 4)
    rs_t = pers("rs", [128, 2], F32, 4)
    lam_t = pers("lam_t", [128, 32, 4], F32, 4)
    assert poff[0] <= P_OFF + 20480, poff[0]
    RING_OFF = P_OFF + 20480
    NSLOT = 8
    ring = [S.sb("ring%d" % i, RING_OFF + i * 4096, [128, KC, 128], BF16, 2, True) for i in range(NSLOT)]
    XN_OFF = RING_OFF + NSLOT * 4096
    XN = S.sb("XN", XN_OFF, [128, KC, TX], BF16, 2, True)
    PH = XN_OFF + KC * TX * 2
    PH_SIZE = SBUF_END - PH
    assert PH_SIZE >= 118400, PH_SIZE

    def ph(name, rel, shape, dt, esz, dsem=False):
        n = int(np.prod(shape[1:])) * esz
        assert rel + n <= PH_SIZE, (name, rel, n, PH_SIZE)
        return S.sb(name, PH + rel, shape, dt, esz, dsem)

    YB = ph("YB", 0, [128, 16, SB], BF16, 2, True)
    YA = ph("YA", 32768, [128, 16, SB], BF16, 2, True)
    MG = ph("MG", 65536, [128, 16, SB], BF16, 2, True)
    EXTRA = 98304

    pst = nc.alloc_psum_tensor("pst", [128, 8, 512], F32)
    bank = [S.raw("bank%d" % i) for i in range(8)]
    pbv = [pst[:, b, :].bitcast(BF16) for b in range(8)]
    bp = [0]

    def nb(k=1):
        p = bp[0]
        if k == 2 and p % 2 == 1:
            p += 1
        if p + k > 8:
            p = 0
        bp[0] = (p + k) % 8
        return p

    wsched = []
    for s in range(NSB):
        for n in range(16):
            wsched.append(("in", CB_U + n))
    for s in range(NSB):
        for g in range(4):
            wsched.append(("in", CB_K + g))
        for g in range(4):
            wsched.append(("in", CB_V + g))
        for g in range(4):
            for hh in range(4):
                wsched.append(("in", CB_Q + 4 * g + hh))
                wsched.append(("in", CB_GA + 4 * g + hh))
        for n in range(16):
            wsched.append(("in", CB_GL + n))
        for f in range(16):
            wsched.append(("a", f))
            wsched.append(("in", CB_ML + f))
            wsched.append(("b", f))
            wsched.append(("in", CB_MA + f))
    wsrc = {"in": w_in, "a": w_a, "b": w_b}
    wstate = {"issued": 0, "next": 0}
    PREFETCH = 4

    def wissue(upto):
        while wstate["issued"] < min(upto, len(wsched)):
            j = wstate["issued"]
            kind, cb = wsched[j]
            src = wsrc[kind][:, cb * 128:(cb + 1) * 128].rearrange("(kc p) n -> p kc n", p=128)
            slot = ring[j % NSLOT]
            S.dma("pool", slot[:], src, slot.ds, writes=[slot])
            wstate["issued"] += 1

    def wnext(key):
        i = wstate["next"]
        assert wsched[i] == key, (i, wsched[i], key)
        wissue(i + PREFETCH + 1)
        wstate["next"] += 1
        return ring[i % NSLOT]

    S.dma("sp", pvec[:], pvec_d, pvec.ds, writes=[pvec])
    S.dma("sp", sinkb[:], sink_d.partition_broadcast(128), sinkb.ds, writes=[sinkb])
    S.dma("sp", masks[:], masks_d, masks.ds, writes=[masks])
    S.dma("sp", ident[:], ident_d, ident.ds, writes=[ident])
    S.dma("sp", sel[:], sel_d, sel.ds, writes=[sel])
    S.op("pool", lambda e: e.memset(ones[:], 1.0), [], [ones])
    S.op("pool", lambda e: e.memset(Hf[:], 0.0), [], [Hf])
    S.op("pool", lambda e: e.memset(Hb[:], 0.0), [], [Hb])
    lamv = pvec[:, :, 9:11]
    y_ = lam_t[:, 0:16, 0:2]
    z_ = lam_t[:, 0:16, 2:4]
    z2_ = lam_t[:, 16:32, 0:2]
    acc_ = lam_t[:, 16:32, 2:4]
    S.op("act", lambda e: e.activation(out=y_, in_=lamv, func=AF.Exp, scale=-1.0), [pvec], [lam_t])
    S.op("dve", lambda e: e.tensor_scalar(out=z_, in0=y_, scalar1=2.0, scalar2=None, op0=ALU.add), [lam_t], [lam_t])
    S.op("dve", lambda e: e.reciprocal(out=z_, in_=z_), [lam_t], [lam_t])
    S.op("dve", lambda e: e.tensor_tensor(out=z_, in0=z_, in1=y_, op=ALU.mult), [lam_t], [lam_t])
    S.op("dve", lambda e: e.tensor_tensor(out=z2_, in0=z_, in1=z_, op=ALU.mult), [lam_t], [lam_t])
    NT = 9
    S.op("dve", lambda e: e.memset(acc_, 1.0 / (2 * NT + 1)), [], [lam_t])
    for k in range(NT - 1, -1, -1):
        S.op("dve", lambda e: e.tensor_tensor(out=acc_, in0=acc_, in1=z2_, op=ALU.mult), [lam_t], [lam_t])
        S.op("dve", lambda e, k=k: e.tensor_scalar(out=acc_, in0=acc_, scalar1=1.0 / (2 * k + 1), scalar2=None,
                                                  op0=ALU.add), [lam_t], [lam_t])
    S.op("dve", lambda e: e.tensor_tensor(out=acc_, in0=acc_, in1=z_, op=ALU.mult), [lam_t], [lam_t])
    S.op("dve", lambda e: e.tensor_scalar(out=cvec[:, :, 0:2], in0=acc_, scalar1=-2.0 * LRU_C, scalar2=None,
                                          op0=ALU.mult), [lam_t], [cvec])
    S.op("dve", lambda e: e.tensor_scalar(out=cvec[:, :, 2:4], in0=acc_, scalar1=-4.0 * LRU_C, scalar2=None,
                                          op0=ALU.mult), [lam_t], [cvec])
    S.op("act", lambda e: e.activation(out=esink[:], in_=sinkb[:], func=AF.Exp), [sinkb], [esink])
    S.op("pool", lambda e: e.tensor_copy(out=esbc[:], in_=esink[:].unsqueeze(2).broadcast_to([128, 16, 128])),
         [esink], [esbc])

    wpre_bc = ph("wpre_bc", 0, [128, D], F32, 4, True)
    xt0 = [ph("xt0_%d" % i, 8192 + i * 8192, [128, D], F32, 4, True) for i in range(3)]
    xs0 = [ph("xs0_%d" % i, 32768 + i * 4096, [128, D], BF16, 2) for i in range(2)]
    xT0 = [ph("xT0_%d" % i, 40960 + i * 4096, [128, KC, 128], BF16, 2, True) for i in range(2)]
    S.dma("sp", wpre_bc[:], wpre_d.partition_broadcast(128), wpre_bc.ds, writes=[wpre_bc])
    def p0_A(i):
        xt, xs = xt0[i % 3], xs0[i % 2]
        sc = ss_t[:, (i % 2):(i % 2) + 1]
        rc = rs_t[:, (i % 2):(i % 2) + 1]
        S.dma("pool", xt[:], x_ext[i * 128:(i + 1) * 128, :], xt.ds, writes=[xt])
        S.op("act", lambda e: e.activation(out=xs[:], in_=xt[:], func=AF.Square, accum_out=sc), [xt], [xs, ss_t])
        S.op("dve", lambda e: e.tensor_scalar(out=rc, in0=sc, scalar1=1.0 / D, scalar2=EPS,
                                              op0=ALU.mult, op1=ALU.add), [ss_t], [rs_t])
        S.op("act", lambda e: e.activation(out=rc, in_=rc, func=AF.Sqrt), [rs_t], [rs_t])
        S.op("dve", lambda e: e.reciprocal(out=rc, in_=rc), [rs_t], [rs_t])
        S.op("dve", lambda e: e.scalar_tensor_tensor(
            out=xs[:], in0=xt[:], scalar=rc, in1=wpre_bc[:], op0=ALU.mult, op1=ALU.mult), [xt, rs_t, wpre_bc], [xs])

    def p0_B(i):
        xs, xT = xs0[i % 2], xT0[i % 2]
        b = nb(2)

        def tr(e):
            for kc in range(KC):
                ins = e.transpose(out=pbv[b + kc // 8][:, (kc % 8) * 128:(kc % 8 + 1) * 128],
                                  in_=xs[:, kc * 128:(kc + 1) * 128], identity=ident[:])
            return ins
        S.op("pe", tr, [xs, ident], [bank[b], bank[b + 1]])
        S.op("act", lambda e: e.activation(
            out=xT[:, 0:8, :], in_=pbv[b].rearrange("p (k t) -> p k t", k=8), func=AF.Copy), [bank[b]], [xT])
        S.op("dve", lambda e: e.tensor_copy(
            out=xT[:, 8:16, :], in_=pbv[b + 1].rearrange("p (k t) -> p k t", k=8)), [bank[b + 1]], [xT])
        S.dma("sp", xnT_d[:, :, i * 128:(i + 1) * 128].rearrange("k p t -> p k t"), xT[:], xT.ds,
              reads=[xT], writes=[xnT_t[i]])

    NT0 = TE // 128
    for i in range(NT0):
        p0_A(i)
        if i >= 1:
            p0_B(i - 1)
    p0_B(NT0 - 1)

    def load_xn(sb_i):
        tiles = xnT_t[sb_i * 8: sb_i * 8 + 10]
        S.dma("sp", XN[:], xnT_d[:, :, sb_i * SB: sb_i * SB + TX].rearrange("k p t -> p k t"), XN.ds,
              reads=tiles, writes=[XN])

    if stop_after < 1:
        S.final.append((xT0[1].ds.h, xT0[1].ds.n))
        S.final.append((xT0[0].ds.h, xT0[0].ds.n))
        S.dma("sp", out_d[0:128, :], xt0[0][:], xt0[0].ds, reads=[xt0[0]])
        S.final.append((xt0[0].ds.h, xt0[0].ds.n))
        S.finish()
        return nc

    gateW = ph("gateW", 0, [128, 2, 2, 16, 128], BF16, 2, True)
    for dd_ in range(2):
        for gi_ in range(2):
            S.dma("pool", gateW[:, dd_, gi_], gate_w[dd_, gi_].rearrange("n i j -> i n j"), gateW.ds, writes=[gateW])
    SET1 = 43072

    def p1set(s):
        base = 16384 + s * SET1
        d = {}
        d["u_sb"] = ph("u_sb%d" % s, base, [128, 1028], F32, 4)
        d["u"] = ph("u%d" % s, base + 4160, [128, SB], F32, 4)
        d["u_bf"] = ph("u_bf%d" % s, base + 8256, [128, SB], BF16, 2)
        o = base + 10304
        for dd in range(2):
            d["r%d" % dd] = ph("rbuf%d_%d" % (s, dd), o, [128, SB], F32, 4)
            d["i%d" % dd] = ph("ibuf%d_%d" % (s, dd), o + 4096, [128, SB], F32, 4, True)
            d["a%d" % dd] = ph("abuf%d_%d" % (s, dd), o + 8192, [128, SB], F32, 4, True)
            d["h%d" % dd] = ph("hscr%d_%d" % (s, dd), o + 12288, [128, SB], F32, 4)
            o += 16384
        return d
    p1sets = [p1set(0), p1set(1)]
    AE5 = AE[:].rearrange("p (s d a n) -> p s d a n", s=2, d=2, a=2)
    AE_A = S.raw("AE_A")
    AE_E = S.raw("AE_E")
    def p1_A(sb_i, n, d):
        W = wnext(("in", CB_U + n))
        b0 = 0
        b2 = 7

        def mm(e, W=W, b0=b0, b2=b2):
            for kc in range(KC):
                for (bk, lo, n_) in ((b0, H - 2, 512), (b0 + 1, H + 510, 512), (b2, H + 1022, 4)):
                    ins = e.matmul(pst[:, bk, 0:n_], lhsT=W[:, kc, :], rhs=XN[:, kc, lo:lo + n_],
                                   start=(kc == 0), stop=(kc == KC - 1))
            return ins
        S.op("pe", mm, [W, XN], [bank[b0], bank[b0 + 1], bank[b2]])
        u_sb, u, u_bf = d["u_sb"], d["u"], d["u_bf"]
        S.op("dve", lambda e: e.tensor_copy(
            out=u_sb[:, 0:1024].rearrange("p (a b) -> p a b", a=2), in_=pst[:, b0:b0 + 2, :]),
            [bank[b0], bank[b0 + 1]], [u_sb])
        S.op("dve", lambda e: e.tensor_copy(out=u_sb[:, 1024:1028], in_=pst[:, b2, 0:4]), [bank[b2]], [u_sb])
        S.op("dve", lambda e: e.tensor_scalar(
            out=u[:], in0=u_sb[:, 0:SB], scalar1=pvec[:, n, 0:1], scalar2=pvec[:, n, 4:5],
            op0=ALU.mult, op1=ALU.add), [u_sb, pvec], [u])
        for tap in range(1, 4):
            S.op("dve", lambda e, tap=tap: e.scalar_tensor_tensor(
                out=u[:], in0=u_sb[:, tap:tap + SB], scalar=pvec[:, n, tap:tap + 1], in1=u[:],
                op0=ALU.mult, op1=ALU.add), [u_sb, pvec, u], [u])
        S.op("dve", lambda e: e.tensor_copy(out=u_bf[:], in_=u[:]), [u], [u_bf])
        for dd in range(2):
            br = 3
            bi = 5

            def gm(e, dd=dd, br=br, bi=bi):
                for gi, bb in ((0, br), (1, bi)):
                    for tb in range(2):
                        ins = e.matmul(pst[:, bb + tb, :], lhsT=gateW[:, dd, gi, n, :],
                                       rhs=u_bf[:, tb * 512:(tb + 1) * 512], start=True, stop=True)
                return ins
            S.op("pe", gm, [gateW, u_bf], [bank[br], bank[br + 1], bank[bi], bank[bi + 1]])
            rb, ib = d["r%d" % dd], d["i%d" % dd]
            S.op("act", lambda e, rb=rb, br=br, dd=dd: e.activation(
                out=rb[:].rearrange("p (a b) -> p a b", a=2), in_=pst[:, br:br + 2, :], func=AF.Sigmoid,
                bias=pvec[:, n, 5 + dd:6 + dd], accum_out=rsum[:, dd:dd + 1]),
                [bank[br], bank[br + 1], pvec], [rb, rsum])
            S.op("act", lambda e, ib=ib, bi=bi, dd=dd: e.activation(
                out=ib[:].rearrange("p (a b) -> p a b", a=2), in_=pst[:, bi:bi + 2, :], func=AF.Sigmoid,
                bias=pvec[:, n, 7 + dd:8 + dd]), [bank[bi], bank[bi + 1], pvec], [ib])
        for dd in range(2):
            rb, ab = d["r%d" % dd], d["a%d" % dd]
            S.op("act", lambda e, rb=rb, ab=ab, dd=dd: e.activation(
                out=ab[:], in_=rb[:], func=AF.Exp, scale=cvec[:, n, dd:dd + 1]), [rb, cvec], [ab])
            S.op("act", lambda e, rb=rb, dd=dd: e.activation(
                out=rb[:], in_=rb[:], func=AF.Exp, scale=cvec[:, n, 2 + dd:3 + dd]), [rb, cvec], [rb])
            S.op("act", lambda e, dd=dd: e.activation(
                out=AE5[:, sb_i, dd, 0, n:n + 1], in_=rsum[:, dd:dd + 1], func=AF.Exp,
                scale=cvec[:, n, dd:dd + 1]), [rsum, cvec], [AE_A])
        for dd in range(2):
            rb = d["r%d" % dd]
            S.op("act", lambda e, rb=rb: e.activation(out=rb[:], in_=rb[:], func=AF.Sqrt, scale=-1.0, bias=1.0),
                 [rb], [rb])

    def p1_B(sb_i, n, d):
        u = d["u"]
        for dd in range(2):
            ib = d["i%d" % dd]
            S.op("pool", lambda e, ib=ib: e.tensor_tensor(out=ib[:], in0=ib[:], in1=u[:], op=ALU.mult),
                 [ib, u], [ib])
        for dd in range(2):
            rb, ib, ab, hs = d["r%d" % dd], d["i%d" % dd], d["a%d" % dd], d["h%d" % dd]
            S.op("pool", lambda e, ib=ib, rb=rb: e.tensor_tensor(out=ib[:], in0=ib[:], in1=rb[:], op=ALU.mult),
                 [ib, rb], [ib])
            if dd == 0:
                S.op("dve", lambda e, ab=ab, ib=ib, hs=hs: e.tensor_tensor_scan(
                    out=hs[:], data0=ab[:], data1=ib[:], initial=0.0, op0=ALU.mult, op1=ALU.add), [ab, ib], [hs])
                S.op("pool", lambda e, hs=hs: e.tensor_copy(
                    out=AE5[:, sb_i, 0, 1, n:n + 1], in_=hs[:, SB - 1:SB]), [hs], [AE_E])
            else:
                S.op("dve", lambda e, ab=ab, ib=ib, hs=hs: e.tensor_tensor_scan(
                    out=hs[:, ::-1], data0=ab[:, ::-1], data1=ib[:, ::-1], initial=0.0,
                    op0=ALU.mult, op1=ALU.add), [ab, ib], [hs])
                S.op("pool", lambda e, hs=hs: e.tensor_copy(
                    out=AE5[:, sb_i, 1, 1, n:n + 1], in_=hs[:, 0:1]), [hs], [AE_E])
            S.dma("sp", ab_d[sb_i, n, 2 * dd], ab[:], ab.ds, reads=[ab], writes=[ab_t[sb_i][n][2 * dd]])
            S.dma("sp", ab_d[sb_i, n, 2 * dd + 1], ib[:], ib.ds, reads=[ib], writes=[ab_t[sb_i][n][2 * dd + 1]])

    pend = None
    it = 0
    for sb_i in range(NSB):
        load_xn(sb_i)
        for n in range(16):
            d = p1sets[it % 2]
            it += 1
            p1_A(sb_i, n, d)
            if pend is not None:
                p1_B(*pend)
            pend = (sb_i, n, d)
    p1_B(*pend)
    bp[0] = 0

    S.dma("sp", ag_in.ap(), AE[:], AE.ds, reads=[AE, AE_A, AE_E], writes=[agin_t])
    cc_sem = S.dsem("cc_sem")
    S.dma("pool", None, None, cc_sem, reads=[agin_t], writes=[agout_t], inc=1,
          fn=lambda e: e.collective_compute("AllGather", ALU.bypass, replica_groups=[list(range(NCORES))],
                                            ins=[ag_in.ap().opt()], outs=[ag_out.ap().opt()]))
    S.dma("sp", G[:], ag_out.ap().rearrange("(r p) f -> p r f", p=128), G.ds, reads=[agout_t], writes=[G])
    G6 = G[:].rearrange("p r (s d a n) -> p r s d a n", s=2, d=2, a=2)

    def carry_chain():
        for v in range(16):
            r, s = divmod(v, 2)
            S.op("dve", lambda e, v=v, r=r, s=s: e.tensor_tensor(
                out=htmp[:, 0, :], in0=Hf[:, v, :], in1=G6[:, r, s, 0, 0, :], op=ALU.mult), [Hf, G], [htmp])
            S.op("dve", lambda e, v=v, r=r, s=s: e.tensor_tensor(
                out=Hf[:, v + 1, :], in0=htmp[:, 0, :], in1=G6[:, r, s, 0, 1, :], op=ALU.add), [htmp, G], [Hf])
        for v in range(15, 0, -1):
            r, s = divmod(v, 2)
            S.op("dve", lambda e, v=v, r=r, s=s: e.tensor_tensor(
                out=htmp[:, 1, :], in0=Hb[:, v, :], in1=G6[:, r, s, 1, 0, :], op=ALU.mult), [Hb, G], [htmp])
            S.op("dve", lambda e, v=v, r=r, s=s: e.tensor_tensor(
                out=Hb[:, v - 1, :], in0=htmp[:, 1, :], in1=G6[:, r, s, 1, 1, :], op=ALU.add), [htmp, G], [Hb])
        Hin4 = Hin[:].rearrange("p (s d n) -> p s d n", s=2, d=2)
        for s in range(2):
            for dd, HH in ((0, Hf), (1, Hb)):
                S.op("dve", lambda e, s=s, HH=HH: e.tensor_tensor(
                    out=htmp[:], in0=HH[:, 0:16, :], in1=sel[:, s, :].unsqueeze(2).broadcast_to([128, 16, 16]),
                    op=ALU.mult), [HH, sel], [htmp])
                S.op("dve", lambda e, s=s, dd=dd: e.tensor_reduce(
                    out=Hin4[:, s, dd, :], in_=htmp[:].rearrange("p v n -> p n v"), axis=AX.X, op=ALU.add),
                    [htmp], [Hin])
    Hin4 = Hin[:].rearrange("p (s d n) -> p s d n", s=2, d=2)

    if stop_after < 2:
        carry_chain()
        hd = S.dsem("d_hin")
        S.final.append(S.dma("sp", dbg["hin"], Hin[:], hd, reads=[Hin]))
        S.dma("sp", out_d[0:128, 0:128], G[:, 0, :], G.ds, reads=[G])
        S.final.append((G.ds.h, G.ds.n))
        for dd in range(2):
            for s in range(2):
                for k in ("a", "i"):
                    t = p1sets[s]["%s%d" % (k, dd)]
                    S.final.append((t.ds.h, t.ds.n))
        S.finish()
        return nc

    P2B = 32768
    cs_t = ph("cs_t", P2B, [32, 2, TX], F32, 4, True)
    kT = ph("kT", P2B + 10240, [128, 4, TX], BF16, 2)
    Vt = ph("Vt", P2B + 20480, [128, 10, 512], BF16, 2)
    qT = [ph("qT%d" % i, P2B + 30720 + i * 8192, [128, 4, SB], BF16, 2) for i in range(2)]
    PT = [ph("PT%d" % i, P2B + 47104 + i * 3072, [128, 3, 512], BF16, 2) for i in range(2)]
    qf = ph("qf", P2B + 53248, [32, TX], F32, 4)
    sw = ph("sw", P2B + 58368, [32, TX], F32, 4)
    Dt = [ph("Dt%d" % i, P2B + 63488 + i * 2048, [128, 512], F32, 4) for i in range(2)]
    ot = [ph("ot%d" % i, P2B + 67584 + i * 2048, [128, 512], F32, 4) for i in range(2)]
    P3B = 65536
    RL = [[ph("RL%d_%d" % (i, j), P3B + i * 16384 + j * 4096, [128, SB], F32, 4, True) for j in range(4)]
          for i in range(2)]
    hfb2 = [ph("hfb%d" % i, P3B + 32768 + i * 4096, [128, SB], F32, 4) for i in range(2)]
    hbb = ph("hbb", P3B + 40960, [128, SB], F32, 4)
    sgb = ph("sgb", P3B + 45056, [128, SB], F32, 4)
    sm = [ph("sm%d" % i, EXTRA + i * 4096, [128, SB], F32, 4) for i in range(2)]
    t12 = [ph("t12_%d" % i, EXTRA + 8192 + i * 4096, [128, SB], F32, 4) for i in range(2)]
    WO = [ph("WO%d" % i, i * 16384, [128, KC, 512], BF16, 2, True) for i in range(4)]
    otile = [ph("otile%d" % i, EXTRA + i * 8192, [128, D], F32, 4, True) for i in range(2)]
    xt5 = [S.sb("xt5_%d" % i, XN_OFF + i * 8192, [128, D], F32, 4, True) for i in range(3)]
    wpost_bc = S.sb("wpost_bc", XN_OFF + 24576, [128, D], F32, 4, True)
    shuf = [(i + 16) % 32 for i in range(32)]

    def mmq(e, W, b0):
        for kc in range(KC):
            for tb in range(2):
                ins = e.matmul(pst[:, b0 + tb, :], lhsT=W[:, kc, :], rhs=XN[:, kc, H + tb * 512:H + (tb + 1) * 512],
                               start=(kc == 0), stop=(kc == KC - 1))
        return ins

    def rope(regions, Tn, dst, csoff):
        rb = []
        for (b0, nbk, c0, ncols) in regions:
            rb += [bank[b0 + k] for k in range(nbk)]
        for (b0, nbk, c0, ncols) in regions:
            if nbk == 2:
                src_all = pst[:, b0:b0 + 2, :]
                src_lo = pst[0:32, b0:b0 + 2, :]
                d_all = dst[:, c0:c0 + ncols].rearrange("p (a b) -> p a b", a=2)
                d_lo = qf[:, c0:c0 + ncols].rearrange("p (a b) -> p a b", a=2)
            else:
                src_all = pst[:, b0, 0:ncols]
                src_lo = pst[0:32, b0, 0:ncols]
                d_all = dst[:, c0:c0 + ncols]
                d_lo = qf[:, c0:c0 + ncols]
            S.op("act", lambda e, s_=src_all, d_=d_all: e.activation(out=d_, in_=s_, func=AF.Copy), rb, [dst_tile[0]])
            S.op("act", lambda e, s_=src_lo, d_=d_lo: e.activation(out=d_, in_=s_, func=AF.Copy), rb, [qf])
        S.op("dve", lambda e: e.stream_shuffle(out=sw[:, 0:Tn], in_=qf[:, 0:Tn], mask=shuf), [qf], [sw])
        S.op("pool", lambda e: e.tensor_tensor(out=qf[:, 0:Tn], in0=qf[:, 0:Tn], in1=cs_t[:, 0, csoff:csoff + Tn],
                                               op=ALU.mult), [qf, cs_t], [qf])
        S.op("pool", lambda e: e.tensor_tensor(out=sw[:, 0:Tn], in0=sw[:, 0:Tn], in1=cs_t[:, 1, csoff:csoff + Tn],
                                               op=ALU.mult), [sw, cs_t], [sw])
        S.op("pool", lambda e: e.tensor_tensor(out=dst[0:32, 0:Tn], in0=qf[:, 0:Tn], in1=sw[:, 0:Tn], op=ALU.add),
             [qf, sw], [dst_tile[0]])

    dst_tile = [None]
    att_it = 0
    for sb_i in range(NSB):
        if sb_i > 0 or True:
            load_xn(sb_i)
        S.dma("sp", cs_t[:], cs_d[:, :, sb_i * SB: sb_i * SB + TX], cs_t.ds, writes=[cs_t])
        for g in range(4):
            W = wnext(("in", CB_K + g))
            b0 = nb(2)
            b2 = nb(1)

            def mmk(e, W=W, b0=b0, b2=b2):
                for kc in range(KC):
                    for (bk, lo, n_) in ((b0, 0, 512), (b0 + 1, 512, 512), (b2, 1024, 256)):
                        ins = e.matmul(pst[:, bk, 0:n_], lhsT=W[:, kc, :], rhs=XN[:, kc, lo:lo + n_],
                                       start=(kc == 0), stop=(kc == KC - 1))
                return ins
            S.op("pe", mmk, [W, XN], [bank[b0], bank[b0 + 1], bank[b2]])
            dst_tile[0] = kT
            rope([(b0, 2, 0, 1024), (b2, 1, 1024, 256)], TX, kT[:, g, :], 0)
        W4 = [wnext(("in", CB_V + g)) for g in range(4)]
        for j in range(10):
            bv = nb(1)

            def mmv(e, j=j, bv=bv, W4=W4):
                for g in range(4):
                    for kc in range(KC):
                        ins = e.matmul(pst[:, bv, g * 128:(g + 1) * 128], lhsT=XN[:, kc, j * 128:(j + 1) * 128],
                                       rhs=W4[g][:, kc, :], start=(kc == 0), stop=(kc == KC - 1))
                return ins
            S.op("pe", mmv, W4 + [XN], [bank[bv]])
            if j % 2 == 0:
                S.op("act", lambda e, j=j, bv=bv: e.activation(out=Vt[:, j, :], in_=pst[:, bv, :], func=AF.Copy),
                     [bank[bv]], [Vt])
            else:
                S.op("dve", lambda e, j=j, bv=bv: e.tensor_copy(out=Vt[:, j, :], in_=pst[:, bv, :]),
                     [bank[bv]], [Vt])
        if debug:
            S.dma("sp", dbg["kT"][sb_i], kT[:], dbg_sem, reads=[kT])
            S.dma("sp", dbg["v"][sb_i], Vt[:], dbg_sem, reads=[Vt])
        def proj_group(g):
            qTg = qT[g % 2]
            for hh in range(4):
                h = 4 * g + hh
                W = wnext(("in", CB_Q + h))
                b0 = nb(2)
                S.op("pe", lambda e, W=W, b0=b0: mmq(e, W, b0), [W, XN], [bank[b0], bank[b0 + 1]])
                dst_tile[0] = qTg
                rope([(b0, 2, 0, 1024)], SB, qTg[:, hh, :], H)
                W = wnext(("in", CB_GA + h))
                b0 = nb(2)
                S.op("pe", lambda e, W=W, b0=b0: mmq(e, W, b0), [W, XN], [bank[b0], bank[b0 + 1]])
                S.op("act", lambda e, h=h, b0=b0: e.activation(
                    out=YB[:, h, :].rearrange("p (a b) -> p a b", a=2), in_=pst[:, b0:b0 + 2, :], func=AF.Silu),
                    [bank[b0], bank[b0 + 1]], [YB])

        def att_scores(g, n, k):
            qTg = qT[g % 2]
            pt, dtt, ott = PT[k % 2], Dt[k % 2], ot[k % 2]
            sbk = [nb(1) for _ in range(3)]

            def mms(e):
                for dj in range(3):
                    ins = e.matmul(pst[:, sbk[dj], :], lhsT=kT[:, g, (n + dj) * 128:(n + dj + 1) * 128],
                                   rhs=qTg[:, :, n * 128:(n + 1) * 128], start=True, stop=True)
                return ins
            S.op("pe", mms, [kT, qTg], [bank[b] for b in sbk])
            for dj in range(3):
                S.op("act", lambda e, dj=dj: e.activation(
                    out=pt[:, dj, :], in_=pst[:, sbk[dj], :], func=AF.Exp, scale=SCALE), [bank[sbk[dj]]], [pt])
            mprev = 2 if (sb_i == 0 and n == 0) else 0
            mnext = 3 if (sb_i == NSB - 1 and n == 7) else 1
            for dj, mi in ((0, mprev), (2, mnext)):
                S.op("pool", lambda e, dj=dj, mi=mi: e.tensor_tensor(
                    out=pt[:, dj, :].rearrange("p (a b) -> p a b", a=4),
                    in0=pt[:, dj, :].rearrange("p (a b) -> p a b", a=4),
                    in1=masks[:, mi, :].unsqueeze(1).broadcast_to([128, 4, 128]), op=ALU.mult), [pt, masks], [pt])
            return (g, n, pt, dtt, ott)

        def att_rest(g, n, pt, dtt, ott):
            bd = nb(1)
            bo = nb(1)

            def mmd(e):
                for dj in range(3):
                    ins = e.matmul(pst[:, bd, :], lhsT=ones[:], rhs=pt[:, dj, :], start=(dj == 0), stop=(dj == 2))
                return ins
            S.op("pe", mmd, [ones, pt], [bank[bd]])

            def mmo(e):
                for dj in range(3):
                    ins = e.matmul(pst[:, bo, :], lhsT=Vt[:, n + dj, g * 128:(g + 1) * 128], rhs=pt[:, dj, :],
                                   start=(dj == 0), stop=(dj == 2))
                return ins
            S.op("pe", mmo, [Vt, pt], [bank[bo]])
            S.op("dve", lambda e: e.tensor_tensor(
                out=dtt[:].rearrange("p (a b) -> p a b", a=4), in0=pst[:, bd, :].rearrange("p (a b) -> p a b", a=4),
                in1=esbc[:, 4 * g:4 * g + 4, :], op=ALU.add), [bank[bd], esbc], [dtt])
            S.op("act", lambda e: e.activation(out=dtt[:], in_=dtt[:], func=AF.Ln), [dtt], [dtt])
            S.op("act", lambda e: e.activation(out=dtt[:], in_=dtt[:], func=AF.Exp, scale=-1.0), [dtt], [dtt])
            S.op("dve", lambda e: e.tensor_tensor(out=ott[:], in0=pst[:, bo, :], in1=dtt[:], op=ALU.mult),
                 [bank[bo], dtt], [ott])
            S.op("pool", lambda e: e.tensor_tensor(
                out=YB[:, 4 * g:4 * g + 4, n * 128:(n + 1) * 128],
                in0=YB[:, 4 * g:4 * g + 4, n * 128:(n + 1) * 128],
                in1=ott[:].rearrange("p (a b) -> p a b", a=4), op=ALU.mult), [YB, ott], [YB])

        proj_group(0)
        for g in range(4):
            if g + 1 < 4:
                proj_group(g + 1)
            prev = None
            for n in range(8):
                cur = att_scores(g, n, att_it)
                att_it += 1
                if prev is not None:
                    att_rest(*prev)
                prev = cur
            att_rest(*prev)
        if debug:
            S.dma("sp", dbg["yb"][sb_i], YB[:], dbg_sem, reads=[YB])
        if stop_after < 3:
            continue
        if sb_i == 0:
            carry_chain()
        def reload(n, slot):
            for j in range(4):
                t = RL[slot][j]
                S.dma("sp", t[:], ab_d[sb_i, n, j], t.ds, reads=[ab_t[sb_i][n][j]], writes=[t])
        reload(0, 0)
        for n in range(16):
            if n + 1 < 16:
                reload(n + 1, (n + 1) % 2)
            af, bf_, ab_, bb_ = RL[n % 2]
            hfb = hfb2[n % 2]
            W = wnext(("in", CB_GL + n))
            b0 = nb(2)
            S.op("pe", lambda e, W=W, b0=b0: mmq(e, W, b0), [W, XN], [bank[b0], bank[b0 + 1]])
            S.op("act", lambda e, b0=b0: e.activation(
                out=sgb[:].rearrange("p (a b) -> p a b", a=2), in_=pst[:, b0:b0 + 2, :], func=AF.Silu),
                [bank[b0], bank[b0 + 1]], [sgb])
            S.op("dve", lambda e, af=af, bf_=bf_, n=n, sb_i=sb_i, hfb=hfb: e.tensor_tensor_scan(
                out=hfb[:], data0=af[:], data1=bf_[:], initial=Hin4[:, sb_i, 0, n:n + 1],
                op0=ALU.mult, op1=ALU.add), [af, bf_, Hin], [hfb])
            S.op("dve", lambda e, ab_=ab_, bb_=bb_, n=n, sb_i=sb_i: e.tensor_tensor_scan(
                out=hbb[:, ::-1], data0=ab_[:, ::-1], data1=bb_[:, ::-1], initial=Hin4[:, sb_i, 1, n:n + 1],
                op0=ALU.mult, op1=ALU.add), [ab_, bb_, Hin], [hbb])
            S.op("dve", lambda e, hfb=hfb: e.tensor_tensor(out=hfb[:], in0=hfb[:], in1=hbb[:], op=ALU.add),
                 [hfb, hbb], [hfb])
            S.op("pool", lambda e, n=n, hfb=hfb: e.tensor_tensor(out=YA[:, n, :], in0=hfb[:], in1=sgb[:], op=ALU.mult),
                 [hfb, sgb], [YA])
        if debug:
            S.dma("sp", dbg["ya"][sb_i], YA[:], dbg_sem, reads=[YA])
        if stop_after < 4:
            continue
        for f in range(16):
            for half, (wk, mk_cb, Y) in enumerate(((("a", f), CB_ML + f, YA), (("b", f), CB_MA + f, YB))):
                Wp = wnext(wk)
                Wm = wnext(("in", mk_cb))
                bpj = nb(2)
                bm = nb(2)

                def mmp(e, Wp=Wp, bpj=bpj, Y=Y):
                    for kc in range(KC):
                        for tb in range(2):
                            ins = e.matmul(pst[:, bpj + tb, :], lhsT=Wp[:, kc, :], rhs=Y[:, kc, tb * 512:(tb + 1) * 512],
                                           start=(kc == 0), stop=(kc == KC - 1))
                    return ins
                S.op("pe", mmp, [Wp, Y], [bank[bpj], bank[bpj + 1]])
                S.op("pe", lambda e, Wm=Wm, bm=bm: mmq(e, Wm, bm), [Wm, XN], [bank[bm], bank[bm + 1]])
                S.op("act", lambda e, half=half, bm=bm: e.activation(
                    out=sm[half][:].rearrange("p (a b) -> p a b", a=2), in_=pst[:, bm:bm + 2, :], func=AF.Sigmoid),
                    [bank[bm], bank[bm + 1]], [sm[half]])
                S.op("dve", lambda e, half=half, bpj=bpj: e.tensor_tensor(
                    out=t12[half][:].rearrange("p (a b) -> p a b", a=2), in0=pst[:, bpj:bpj + 2, :],
                    in1=sm[half][:].rearrange("p (a b) -> p a b", a=2), op=ALU.mult),
                    [bank[bpj], bank[bpj + 1], sm[half]], [t12[half]])
            S.op("pool", lambda e, f=f: e.tensor_tensor(out=MG[:, f, :], in0=t12[0][:], in1=t12[1][:], op=ALU.add),
                 [t12[0], t12[1]], [MG])
        if debug:
            S.dma("sp", dbg["mg"][sb_i], MG[:], dbg_sem, reads=[MG])
        if stop_after < 5:
            continue
        for cg in range(4):
            S.dma("pool", WO[cg][:], w_o[:, cg * 512:(cg + 1) * 512].rearrange("(kc p) n -> p kc n", p=128),
                  WO[cg].ds, writes=[WO[cg]])
        S.dma("sp", wpost_bc[:], wpost_d.partition_broadcast(128), wpost_bc.ds, writes=[wpost_bc])
        for i in range(8):
            xt = xt5[i % 3]
            ot_ = otile[i % 2]
            r0 = H + sb_i * SB + i * 128
            S.dma("sp", xt[:], x_ext[r0:r0 + 128, :], xt.ds, writes=[xt])
            bq = [nb(2), nb(2)]
            bks = [bq[0], bq[0] + 1, bq[1], bq[1] + 1]

            def mmf(e, i=i, bks=bks):
                for cg in range(4):
                    for kc in range(KC):
                        ins = e.matmul(pst[:, bks[cg], :], lhsT=MG[:, kc, i * 128:(i + 1) * 128],
                                       rhs=WO[cg][:, kc, :], start=(kc == 0), stop=(kc == KC - 1))
                return ins
            S.op("pe", mmf, [MG] + WO, [bank[b] for b in bks])
            for cg in range(4):
                S.op("act", lambda e, cg=cg, ot_=ot_, bks=bks: e.activation(
                    out=ot_[:, cg * 512:(cg + 1) * 512], in_=pst[:, bks[cg], :], func=AF.Square,
                    accum_out=ss_t[:, 4 + cg:5 + cg]), [bank[bks[cg]]], [ot_, ss_t])
            S.op("dve", lambda e: e.tensor_reduce(out=rs_t[:, 0:1], in_=ss_t[:, 4:8], axis=AX.X, op=ALU.add),
                 [ss_t], [rs_t])
            S.op("dve", lambda e: e.tensor_scalar(out=rs_t[:, 0:1], in0=rs_t[:, 0:1], scalar1=1.0 / D, scalar2=EPS,
                                                  op0=ALU.mult, op1=ALU.add), [rs_t], [rs_t])
            S.op("act", lambda e: e.activation(out=rs_t[:, 0:1], in_=rs_t[:, 0:1], func=AF.Sqrt), [rs_t], [rs_t])
            S.op("dve", lambda e: e.reciprocal(out=rs_t[:, 0:1], in_=rs_t[:, 0:1]), [rs_t], [rs_t])
            for cg in range(4):
                S.op("dve", lambda e, cg=cg, ot_=ot_, bks=bks: e.scalar_tensor_tensor(
                    out=ot_[:, cg * 512:(cg + 1) * 512], in0=pst[:, bks[cg], :], scalar=rs_t[:, 0:1],
                    in1=wpost_bc[:, cg * 512:(cg + 1) * 512], op0=ALU.mult, op1=ALU.mult),
                    [bank[bks[cg]], rs_t, wpost_bc], [ot_])
            S.op("pool", lambda e, ot_=ot_, xt=xt: e.tensor_tensor(out=ot_[:], in0=ot_[:], in1=xt[:], op=ALU.add),
                 [ot_, xt], [ot_])
            o0 = sb_i * SB + i * 128
            S.final.append(S.dma("sp", out_d[o0:o0 + 128, :], ot_[:], ot_.ds, reads=[ot_]))

    if debug:
        S.final.append((dbg_sem.h, dbg_sem.n))
    if stop_after < 5:
        S.dma("sp", out_d[0:128, 0:128], G[:, 0, :], G.ds, reads=[G])
        S.final.append((G.ds.h, G.ds.n))
    S.finish()
    return nc


def make_in_maps(x, norm_pre_w, w_in, conv_w, conv_b, lru_w_r, lru_b_r, lru_w_i, lru_b_i, lru_lambda,
                 attn_sink, w_proj_a, w_proj_b, w_out, norm_post_w):
    f32 = np.float32
    x2 = np.asarray(x, f32).reshape(S_FULL, D)
    xpad = np.zeros((S_FULL + 2 * H, D), f32)
    xpad[H:H + S_FULL] = x2
    w_in2 = np.ascontiguousarray(np.asarray(w_in, f32)[0])
    w_a2 = np.ascontiguousarray(np.asarray(w_proj_a, f32)[0])
    w_b2 = np.ascontiguousarray(np.asarray(w_proj_b, f32)[0])
    w_o2 = np.ascontiguousarray(np.asarray(w_out, f32)[0])
    gate_w = np.ascontiguousarray(np.stack([np.asarray(lru_w_r, f32)[0], np.asarray(lru_w_i, f32)[0]], axis=1))
    vecs = [np.asarray(conv_w, f32)[0, t] for t in range(4)] + [np.asarray(conv_b, f32)[0]]
    vecs += [np.asarray(lru_b_r, f32)[0, 0], np.asarray(lru_b_r, f32)[0, 1]]
    vecs += [np.asarray(lru_b_i, f32)[0, 0], np.asarray(lru_b_i, f32)[0, 1]]
    vecs += [np.asarray(lru_lambda, f32)[0, 0], np.asarray(lru_lambda, f32)[0, 1]]
    pvec = np.ascontiguousarray(np.stack([v.reshape(16, 128).T for v in vecs], axis=-1))
    wpre = np.ascontiguousarray(np.asarray(norm_pre_w, f32)[0])
    wpost = np.ascontiguousarray(np.asarray(norm_post_w, f32)[0])
    sink = np.ascontiguousarray(np.asarray(attn_sink, f32)[0])
    ident = np.eye(128, dtype=f32).astype(ml_dtypes.bfloat16)
    inv_freq = (np.float32(500000.0) ** (-np.arange(0, 32, 2, dtype=f32) / np.float32(32))).astype(f32)
    kk = np.arange(128)[:, None]
    qq = np.arange(128)[None, :]
    tri_prev = (kk >= qq).astype(f32)
    tri_next = (kk <= qq).astype(f32)
    maps = []
    for c in range(NCORES):
        pos = (np.arange(TE, dtype=np.int64) + c * T - H).astype(f32)
        ang = pos[:, None] * inv_freq[None, :]
        cos, sin = np.cos(ang).astype(f32), np.sin(ang).astype(f32)
        cs = np.zeros((32, 2, TE), f32)
        cs[0:16, 0] = cos.T
        cs[16:32, 0] = cos.T
        cs[0:16, 1] = -sin.T
        cs[16:32, 1] = sin.T
        masks = np.stack([tri_prev, tri_next, tri_prev * (0.0 if c == 0 else 1.0),
                          tri_next * (0.0 if c == NCORES - 1 else 1.0)], axis=1).astype(ml_dtypes.bfloat16)
        sel = np.zeros((128, 2, 16), f32)
        for s in range(2):
            sel[:, s, 2 * c + s] = 1.0
        maps.append({
            "x_ext": np.ascontiguousarray(xpad[c * T: c * T + TE]),
            "w_in": w_in2, "w_a": w_a2, "w_b": w_b2, "w_o": w_o2, "gate_w": gate_w, "pvec": pvec,
            "wpre": wpre, "wpost": wpost, "sink": sink, "cs": cs, "masks": np.ascontiguousarray(masks),
            "ident": ident, "sel": sel,
        })
    return maps


_NC_CACHE = {}


def kernel(**inputs):
    maps = make_in_maps(**inputs)
    if "nc" not in _NC_CACHE:
        _NC_CACHE["nc"] = build_program()
    res = run_bass_kernel_spmd(_NC_CACHE["nc"], maps, core_ids=list(range(NCORES)))
    out = np.concatenate([np.asarray(r["out"], np.float32) for r in res.results], axis=0)
    return out.reshape(1, S_FULL, D)
```

```python
import numpy as np
import ml_dtypes
import concourse.bass as bass
import concourse.mybir as mybir
from concourse.bass_utils import run_bass_kernel_spmd

F32 = mybir.dt.float32
BF16 = mybir.dt.bfloat16
ALU = mybir.AluOpType
AF = mybir.ActivationFunctionType
AX = mybir.AxisListType

NCORES = 8
S_FULL = 16384
D = 2048
T = S_FULL // NCORES
SB = 1024
NSB = T // SB
H = 128
TX = SB + 2 * H
TE = T + 2 * H
KC = 16
IN_W = 13312
EPS = 1e-6
LRU_C = 8.0
SCALE = 128 ** -0.5
CB_U, CB_GL, CB_Q, CB_K, CB_V, CB_GA, CB_ML, CB_MA = 0, 16, 32, 48, 52, 56, 72, 88

SBUF_BASE = 16640
SBUF_END = 229344


class Tile:
    __slots__ = ("name", "t", "lo", "hi", "w", "r", "over", "ds")

    def __init__(self, name, t, lo, hi):
        self.name, self.t, self.lo, self.hi = name, t, lo, hi
        self.w = None
        self.r = {}
        self.over = [self]
        self.ds = None

    def __getitem__(self, k):
        return self.t[k]


class DSem:
    def __init__(self, h):
        self.h, self.n = h, 0


class Sched:
    def __init__(self, nc):
        self.nc = nc
        self.E = {"pe": nc.tensor, "act": nc.scalar, "dve": nc.vector, "pool": nc.gpsimd, "sp": nc.sync}
        self.sem = {e: nc.alloc_semaphore("sem_" + e) for e in ("pe", "act", "dve", "pool")}
        self.cnt = {e: 0 for e in self.sem}
        self.prog = {e: [] for e in self.E}
        self.seen = {e: {} for e in self.E}
        self.sb_tiles = []
        self.final = []
        self.nops = 0
        self.nwaits = 0

    def sb(self, name, off, shape, dtype, esz, dsem=False):
        n = 1
        for s in shape[1:]:
            n *= s
        assert SBUF_BASE <= off and off + n * esz <= SBUF_END, (name, off, n * esz)
        t = self.nc.alloc_sbuf_tensor_at(name, list(shape), dtype, offset=off)
        tl = Tile(name, t, off, off + n * esz)
        for o in self.sb_tiles:
            if o.lo < tl.hi and tl.lo < o.hi:
                o.over.append(tl)
                tl.over.append(o)
        self.sb_tiles.append(tl)
        if dsem:
            tl.ds = self.dsem("d_" + name)
        return tl

    def raw(self, name, t=None, dsem=False):
        tl = Tile(name, t, 0, 0)
        if dsem:
            tl.ds = self.dsem("d_" + name)
        return tl

    def dsem(self, name):
        return DSem(self.nc.alloc_semaphore(name))

    def _deps(self, eng, reads, writes, extra=()):
        deps = {}

        def add(ev):
            if ev is None:
                return
            s, v = ev
            k = s.name
            if k not in deps or deps[k][1] < v:
                deps[k] = (s, v)

        for t in reads:
            for o in t.over:
                add(o.w)
        for t in writes:
            for o in t.over:
                add(o.w)
                for ev in o.r.values():
                    add(ev)
        for ev in extra:
            add(ev)
        out = []
        seen = self.seen[eng]
        own = self.sem[eng].name if eng in self.sem else None
        for k, (s, v) in deps.items():
            if eng == "pe" and k == own:
                continue
            if seen.get(k, 0) >= v:
                continue
            seen[k] = v
            out.append((s, v))
        return out

    def _commit(self, ev, reads, writes):
        k = ev[0].name
        for t in writes:
            t.w = ev
            t.r = {}
        for t in reads:
            if k not in t.r or t.r[k][1] < ev[1]:
                t.r[k] = ev

    def op(self, eng, fn, reads=(), writes=(), extra=()):
        waits = self._deps(eng, reads, writes, extra)
        sem = self.sem[eng]
        self.cnt[eng] += 1
        ev = (sem, self.cnt[eng])
        self.nops += 1
        self.nwaits += len(waits)

        def emit(e, waits=waits, fn=fn, sem=sem):
            for s, v in waits:
                e.wait_ge(s, v)
            fn(e).then_inc(sem, 1)

        self.prog[eng].append(emit)
        self._commit(ev, reads, writes)
        return ev

    def dma(self, q, out_ap, in_ap, ds, reads=(), writes=(), extra=(), inc=16, fn=None):
        ex = list(extra)
        if ds.n > 0:
            ex.append((ds.h, ds.n))
        waits = self._deps(q, reads, writes, ex)
        ds.n += inc
        ev = (ds.h, ds.n)
        self.nops += 1
        self.nwaits += len(waits)

        def emit(e, waits=waits, out_ap=out_ap, in_ap=in_ap, h=ds.h, inc=inc, fn=fn):
            for s, v in waits:
                e.wait_ge(s, v)
            ins = fn(e) if fn is not None else e.dma_start(out=out_ap, in_=in_ap)
            ins.then_inc(h, inc)

        self.prog[q].append(emit)
        self._commit(ev, reads, writes)
        return ev

    def finish(self):
        fin = {}
        for s, v in self.final:
            if s.name not in fin or fin[s.name][1] < v:
                fin[s.name] = (s, v)

        def run(name):
            def f(e):
                for fn in self.prog[name]:
                    fn(e)
                if name == "sp":
                    for s, v in fin.values():
                        e.wait_ge(s, v)
            return f

        with self.nc.Block() as block:
            block.tensor(run("pe"))
            block.scalar(run("act"))
            block.vector(run("dve"))
            block.gpsimd(run("pool"))
            block.sync(run("sp"))


def build_program(stop_after=5, debug=False):
    nc = bass.Bass("TRN2", target_bir_lowering=False)
    S = Sched(nc)
    dk = "ExternalOutput" if debug else "Internal"

    def din(name, shape, dt=F32):
        return nc.dram_tensor(name, list(shape), dt, kind="ExternalInput").ap()

    x_ext = din("x_ext", [TE, D])
    w_in = din("w_in", [D, IN_W])
    w_a = din("w_a", [D, D])
    w_b = din("w_b", [D, D])
    w_o = din("w_o", [D, D])
    gate_w = din("gate_w", [2, 2, 16, 128, 128])
    pvec_d = din("pvec", [128, 16, 11])
    wpre_d = din("wpre", [D])
    wpost_d = din("wpost", [D])
    sink_d = din("sink", [16])
    cs_d = din("cs", [32, 2, TE])
    masks_d = din("masks", [128, 4, 128], BF16)
    ident_d = din("ident", [128, 128], BF16)
    sel_d = din("sel", [128, 2, 16])
    out_d = nc.dram_tensor("out", [T, D], F32, kind="ExternalOutput").ap()
    xnT_d = nc.dram_tensor("xnT_d", [KC, 128, TE], BF16, kind=dk).ap()
    ab_d = nc.dram_tensor("ab_d", [NSB, 16, 4, 128, SB], F32, kind=dk).ap()
    ag_in = nc.dram_tensor("ag_in", [128, 128], F32)
    ag_out = nc.dram_tensor("ag_out", [NCORES * 128, 128], F32)
    dbg = {}
    if debug:
        dbg["ya"] = nc.dram_tensor("dbg_ya", [NSB, 128, 16, SB], BF16, kind="ExternalOutput").ap()
        dbg["yb"] = nc.dram_tensor("dbg_yb", [NSB, 128, 16, SB], BF16, kind="ExternalOutput").ap()
        dbg["mg"] = nc.dram_tensor("dbg_mg", [NSB, 128, 16, SB], BF16, kind="ExternalOutput").ap()
        dbg["hin"] = nc.dram_tensor("dbg_hin", [128, 64], F32, kind="ExternalOutput").ap()
        dbg["kT"] = nc.dram_tensor("dbg_kT", [NSB, 128, 4, TX], BF16, kind="ExternalOutput").ap()
        dbg["v"] = nc.dram_tensor("dbg_v", [NSB, 128, 10, 512], BF16, kind="ExternalOutput").ap()
        dbg["qT"] = nc.dram_tensor("dbg_qT", [NSB, 4, 128, 4, SB], BF16, kind="ExternalOutput").ap()
    dbg_sem = S.dsem("d_dbg")
    xnT_t = [S.raw("xnT_d%d" % i) for i in range(18)]
    ab_t = [[[S.raw("ab_d_%d_%d_%d" % (s, n, j)) for j in range(4)] for n in range(16)] for s in range(NSB)]
    agin_t = S.raw("ag_in")
    agout_t = S.raw("ag_out")

    P_OFF = SBUF_BASE
    poff = [P_OFF]

    def pers(name, shape, dt, esz, dsem=False):
        n = int(np.prod(shape[1:])) * esz
        t = S.sb(name, poff[0], shape, dt, esz, dsem)
        poff[0] += (n + 63) // 64 * 64
        return t

    pvec = pers("pvec", [128, 16, 11], F32, 4, True)
    cvec = pers("cvec", [128, 16, 4], F32, 4)
    sinkb = pers("sinkb", [128, 16], F32, 4, True)
    esink = pers("esink", [128, 16], F32, 4)
    esbc = pers("esbc", [128, 16, 128], F32, 4)
    masks = pers("masks", [128, 4, 128], BF16, 2, True)
    ident = pers("ident", [128, 128], BF16, 2, True)
    ones = pers("ones", [128, 128], BF16, 2)
    AE = pers("AE", [128, 128], F32, 4, True)
    G = pers("G", [128, NCORES, 128], F32, 4, True)
    Hf = pers("Hf", [128, 17, 16], F32, 4)
    Hb = pers("Hb", [128, 17, 16], F32, 4)
    Hin = pers("Hin", [128, 64], F32, 4)
    sel = pers("sel", [128, 2, 16], F32, 4, True)
    htmp = pers("htmp", [128, 16, 16], F32, 4)
    rsum = pers("rsum", [128, 4], F32, 4)
    ss_t = pers("ss", [128, 8], F32, 4)
    rs_t = pers("rs", [128, 2], F32, 4)
    lam_t = pers("lam_t", [128, 32, 4], F32, 4)
    assert poff[0] <= P_OFF + 20480, poff[0]
    RING_OFF = P_OFF + 20480
    NSLOT = 8
    ring = [S.sb("ring%d" % i, RING_OFF + i * 4096, [128, KC, 128], BF16, 2, True) for i in range(NSLOT)]
    XN_OFF = RING_OFF + NSLOT * 4096
    XN = S.sb("XN", XN_OFF, [128, KC, TX], BF16, 2, True)
    PH = XN_OFF + KC * TX * 2
    PH_SIZE = SBUF_END - PH
    assert PH_SIZE >= 118400, PH_SIZE

    def ph(name, rel, shape, dt, esz, dsem=False):
        n = int(np.prod(shape[1:])) * esz
        assert rel + n <= PH_SIZE, (name, rel, n, PH_SIZE)
        return S.sb(name, PH + rel, shape, dt, esz, dsem)

    YB = ph("YB", 0, [128, 16, SB], BF16, 2, True)
    YA = ph("YA", 32768, [128, 16, SB], BF16, 2, True)
    MG = ph("MG", 65536, [128, 16, SB], BF16, 2, True)
    EXTRA = 98304

    pst = nc.alloc_psum_tensor("pst", [128, 8, 512], F32)
    bank = [S.raw("bank%d" % i) for i in range(8)]
    pbv = [pst[:, b, :].bitcast(BF16) for b in range(8)]
    bp = [0]

    def nb(k=1):
        p = bp[0]
        if k == 2 and p % 2 == 1:
            p += 1
        if p + k > 8:
            p = 0
        bp[0] = (p + k) % 8
        return p

    wsched = []
    for s in range(NSB):
        for n in range(16):
            wsched.append(("in", CB_U + n))
    for s in range(NSB):
        for g in range(4):
            wsched.append(("in", CB_K + g))
        for g in range(4):
            wsched.append(("in", CB_V + g))
        for g in range(4):
            for hh in range(4):
                wsched.append(("in", CB_Q + 4 * g + hh))
                wsched.append(("in", CB_GA + 4 * g + hh))
        for n in range(16):
            wsched.append(("in", CB_GL + n))
        for f in range(16):
            wsched.append(("a", f))
            wsched.append(("in", CB_ML + f))
            wsched.append(("b", f))
            wsched.append(("in", CB_MA + f))
    wsrc = {"in": w_in, "a": w_a, "b": w_b}
    wstate = {"issued": 0, "next": 0}
    PREFETCH = 4

    def wissue(upto):
        while wstate["issued"] < min(upto, len(wsched)):
            j = wstate["issued"]
            kind, cb = wsched[j]
            src = wsrc[kind][:, cb * 128:(cb + 1) * 128].rearrange("(kc p) n -> p kc n", p=128)
            slot = ring[j % NSLOT]
            S.dma("pool", slot[:], src, slot.ds, writes=[slot])
            wstate["issued"] += 1

    def wnext(key):
        i = wstate["next"]
        assert wsched[i] == key, (i, wsched[i], key)
        wissue(i + PREFETCH + 1)
        wstate["next"] += 1
        return ring[i % NSLOT]

    S.dma("sp", pvec[:], pvec_d, pvec.ds, writes=[pvec])
    S.dma("sp", sinkb[:], sink_d.partition_broadcast(128), sinkb.ds, writes=[sinkb])
    S.dma("sp", masks[:], masks_d, masks.ds, writes=[masks])
    S.dma("sp", ident[:], ident_d, ident.ds, writes=[ident])
    S.dma("sp", sel[:], sel_d, sel.ds, writes=[sel])
    S.op("pool", lambda e: e.memset(ones[:], 1.0), [], [ones])
    S.op("pool", lambda e: e.memset(Hf[:], 0.0), [], [Hf])
    S.op("pool", lambda e: e.memset(Hb[:], 0.0), [], [Hb])
    lamv = pvec[:, :, 9:11]
    y_ = lam_t[:, 0:16, 0:2]
    z_ = lam_t[:, 0:16, 2:4]
    z2_ = lam_t[:, 16:32, 0:2]
    acc_ = lam_t[:, 16:32, 2:4]
    S.op("act", lambda e: e.activation(out=y_, in_=lamv, func=AF.Exp, scale=-1.0), [pvec], [lam_t])
    S.op("dve", lambda e: e.tensor_scalar(out=z_, in0=y_, scalar1=2.0, scalar2=None, op0=ALU.add), [lam_t], [lam_t])
    S.op("dve", lambda e: e.reciprocal(out=z_, in_=z_), [lam_t], [lam_t])
    S.op("dve", lambda e: e.tensor_tensor(out=z_, in0=z_, in1=y_, op=ALU.mult), [lam_t], [lam_t])
    S.op("dve", lambda e: e.tensor_tensor(out=z2_, in0=z_, in1=z_, op=ALU.mult), [lam_t], [lam_t])
    NT = 9
    S.op("dve", lambda e: e.memset(acc_, 1.0 / (2 * NT + 1)), [], [lam_t])
    for k in range(NT - 1, -1, -1):
        S.op("dve", lambda e: e.tensor_tensor(out=acc_, in0=acc_, in1=z2_, op=ALU.mult), [lam_t], [lam_t])
        S.op("dve", lambda e, k=k: e.tensor_scalar(out=acc_, in0=acc_, scalar1=1.0 / (2 * k + 1), scalar2=None,
                                                  op0=ALU.add), [lam_t], [lam_t])
    S.op("dve", lambda e: e.tensor_tensor(out=acc_, in0=acc_, in1=z_, op=ALU.mult), [lam_t], [lam_t])
    S.op("dve", lambda e: e.tensor_scalar(out=cvec[:, :, 0:2], in0=acc_, scalar1=-2.0 * LRU_C, scalar2=None,
                                          op0=ALU.mult), [lam_t], [cvec])
    S.op("dve", lambda e: e.tensor_scalar(out=cvec[:, :, 2:4], in0=acc_, scalar1=-4.0 * LRU_C, scalar2=None,
                                          op0=ALU.mult), [lam_t], [cvec])
    S.op("act", lambda e: e.activation(out=esink[:], in_=sinkb[:], func=AF.Exp), [sinkb], [esink])
    S.op("pool", lambda e: e.tensor_copy(out=esbc[:], in_=esink[:].unsqueeze(2).broadcast_to([128, 16, 128])),
         [esink], [esbc])

    wpre_bc = ph("wpre_bc", 0, [128, D], F32, 4, True)
    xt0 = [ph("xt0_%d" % i, 8192 + i * 8192, [128, D], F32, 4, True) for i in range(3)]
    xs0 = [ph("xs0_%d" % i, 32768 + i * 4096, [128, D], BF16, 2) for i in range(2)]
    xT0 = [ph("xT0_%d" % i, 40960 + i * 4096, [128, KC, 128], BF16, 2, True) for i in range(2)]
    S.dma("sp", wpre_bc[:], wpre_d.partition_broadcast(128), wpre_bc.ds, writes=[wpre_bc])
    def p0_A(i):
        xt, xs = xt0[i % 3], xs0[i % 2]
        sc = ss_t[:, (i % 2):(i % 2) + 1]
        rc = rs_t[:, (i % 2):(i % 2) + 1]
        S.dma("pool", xt[:], x_ext[i * 128:(i + 1) * 128, :], xt.ds, writes=[xt])
        S.op("act", lambda e: e.activation(out=xs[:], in_=xt[:], func=AF.Square, accum_out=sc), [xt], [xs, ss_t])
        S.op("dve", lambda e: e.tensor_scalar(out=rc, in0=sc, scalar1=1.0 / D, scalar2=EPS,
                                              op0=ALU.mult, op1=ALU.add), [ss_t], [rs_t])
        S.op("act", lambda e: e.activation(out=rc, in_=rc, func=AF.Sqrt), [rs_t], [rs_t])
        S.op("dve", lambda e: e.reciprocal(out=rc, in_=rc), [rs_t], [rs_t])
        S.op("dve", lambda e: e.scalar_tensor_tensor(
            out=xs[:], in0=xt[:], scalar=rc, in1=wpre_bc[:], op0=ALU.mult, op1=ALU.mult), [xt, rs_t, wpre_bc], [xs])

    def p0_B(i):
        xs, xT = xs0[i % 2], xT0[i % 2]
        b = nb(2)

        def tr(e):
            for kc in range(KC):
                ins = e.transpose(out=pbv[b + kc // 8][:, (kc % 8) * 128:(kc % 8 + 1) * 128],
                                  in_=xs[:, kc * 128:(kc + 1) * 128], identity=ident[:])
            return ins
        S.op("pe", tr, [xs, ident], [bank[b], bank[b + 1]])
        S.op("act", lambda e: e.activation(
            out=xT[:, 0:8, :], in_=pbv[b].rearrange("p (k t) -> p k t", k=8), func=AF.Copy), [bank[b]], [xT])
        S.op("dve", lambda e: e.tensor_copy(
            out=xT[:, 8:16, :], in_=pbv[b + 1].rearrange("p (k t) -> p k t", k=8)), [bank[b + 1]], [xT])
        S.dma("sp", xnT_d[:, :, i * 128:(i + 1) * 128].rearrange("k p t -> p k t"), xT[:], xT.ds,
              reads=[xT], writes=[xnT_t[i]])

    NT0 = TE // 128
    for i in range(NT0):
        p0_A(i)
        if i >= 1:
            p0_B(i - 1)
    p0_B(NT0 - 1)

    def load_xn(sb_i):
        tiles = xnT_t[sb_i * 8: sb_i * 8 + 10]
        S.dma("sp", XN[:], xnT_d[:, :, sb_i * SB: sb_i * SB + TX].rearrange("k p t -> p k t"), XN.ds,
              reads=tiles, writes=[XN])

    if stop_after < 1:
        S.final.append((xT0[1].ds.h, xT0[1].ds.n))
        S.final.append((xT0[0].ds.h, xT0[0].ds.n))
        S.dma("sp", out_d[0:128, :], xt0[0][:], xt0[0].ds, reads=[xt0[0]])
        S.final.append((xt0[0].ds.h, xt0[0].ds.n))
        S.finish()
        return nc

    gateW = ph("gateW", 0, [128, 2, 2, 16, 128], BF16, 2, True)
    for dd_ in range(2):
        for gi_ in range(2):
            S.dma("pool", gateW[:, dd_, gi_], gate_w[dd_, gi_].rearrange("n i j -> i n j"), gateW.ds, writes=[gateW])
    SET1 = 32768

    def p1set(s):
        base = 16384 + s * SET1
        d = {}
        o = base
        for dd in range(2):
            d["r%d" % dd] = ph("rbuf%d_%d" % (s, dd), o, [128, SB], F32, 4)
            d["i%d" % dd] = ph("ibuf%d_%d" % (s, dd), o + 4096, [128, SB], F32, 4, True)
            d["a%d" % dd] = ph("abuf%d_%d" % (s, dd), o + 8192, [128, SB], F32, 4, True)
            d["h%d" % dd] = ph("hscr%d_%d" % (s, dd), o + 12288, [128, SB], F32, 4)
            o += 16384
        return d

    def p1uset(s):
        base = 16384 + 2 * SET1 + s * 10304
        return {"u_sb": ph("u_sb%d" % s, base, [128, 1028], F32, 4),
                "u": ph("u%d" % s, base + 4160, [128, SB], F32, 4),
                "u_bf": ph("u_bf%d" % s, base + 8256, [128, SB], BF16, 2)}
    p1sets = [p1set(0), p1set(1)]
    p1usets = [p1uset(0), p1uset(1), p1uset(2)]
    AE5 = AE[:].rearrange("p (s d a n) -> p s d a n", s=2, d=2, a=2)
    AE_A = S.raw("AE_A")
    AE_E = S.raw("AE_E")
    def p1_A0(sb_i, n, du):
        W = wnext(("in", CB_U + n))
        b0 = 0
        b2 = 7

        def mm(e, W=W, b0=b0, b2=b2):
            for kc in range(KC):
                for (bk, lo, n_) in ((b0, H - 2, 512), (b0 + 1, H + 510, 512), (b2, H + 1022, 4)):
                    ins = e.matmul(pst[:, bk, 0:n_], lhsT=W[:, kc, :], rhs=XN[:, kc, lo:lo + n_],
                                   start=(kc == 0), stop=(kc == KC - 1))
            return ins
        S.op("pe", mm, [W, XN], [bank[b0], bank[b0 + 1], bank[b2]])
        u_sb, u, u_bf = du["u_sb"], du["u"], du["u_bf"]
        S.op("dve", lambda e: e.tensor_copy(
            out=u_sb[:, 0:1024].rearrange("p (a b) -> p a b", a=2), in_=pst[:, b0:b0 + 2, :]),
            [bank[b0], bank[b0 + 1]], [u_sb])
        S.op("dve", lambda e: e.tensor_copy(out=u_sb[:, 1024:1028], in_=pst[:, b2, 0:4]), [bank[b2]], [u_sb])
        S.op("dve", lambda e: e.tensor_scalar(
            out=u[:], in0=u_sb[:, 0:SB], scalar1=pvec[:, n, 0:1], scalar2=pvec[:, n, 4:5],
            op0=ALU.mult, op1=ALU.add), [u_sb, pvec], [u])
        for tap in range(1, 4):
            S.op("dve", lambda e, tap=tap: e.scalar_tensor_tensor(
                out=u[:], in0=u_sb[:, tap:tap + SB], scalar=pvec[:, n, tap:tap + 1], in1=u[:],
                op0=ALU.mult, op1=ALU.add), [u_sb, pvec, u], [u])
        S.op("dve", lambda e: e.tensor_copy(out=u_bf[:], in_=u[:]), [u], [u_bf])

    def p1_A1(sb_i, n, d, du):
        u_bf = du["u_bf"]
        for dd in range(2):
            br = 3
            bi = 5

            def gm(e, dd=dd, br=br, bi=bi):
                for gi, bb in ((0, br), (1, bi)):
                    for tb in range(2):
                        ins = e.matmul(pst[:, bb + tb, :], lhsT=gateW[:, dd, gi, n, :],
                                       rhs=u_bf[:, tb * 512:(tb + 1) * 512], start=True, stop=True)
                return ins
            S.op("pe", gm, [gateW, u_bf], [bank[br], bank[br + 1], bank[bi], bank[bi + 1]])
            rb, ib = d["r%d" % dd], d["i%d" % dd]
            S.op("act", lambda e, rb=rb, br=br, dd=dd: e.activation(
                out=rb[:].rearrange("p (a b) -> p a b", a=2), in_=pst[:, br:br + 2, :], func=AF.Sigmoid,
                bias=pvec[:, n, 5 + dd:6 + dd], accum_out=rsum[:, dd:dd + 1]),
                [bank[br], bank[br + 1], pvec], [rb, rsum])
            S.op("act", lambda e, ib=ib, bi=bi, dd=dd: e.activation(
                out=ib[:].rearrange("p (a b) -> p a b", a=2), in_=pst[:, bi:bi + 2, :], func=AF.Sigmoid,
                bias=pvec[:, n, 7 + dd:8 + dd]), [bank[bi], bank[bi + 1], pvec], [ib])
        for dd in range(2):
            rb, ab = d["r%d" % dd], d["a%d" % dd]
            S.op("act", lambda e, rb=rb, ab=ab, dd=dd: e.activation(
                out=ab[:], in_=rb[:], func=AF.Exp, scale=cvec[:, n, dd:dd + 1]), [rb, cvec], [ab])
            S.op("act", lambda e, rb=rb, dd=dd: e.activation(
                out=rb[:], in_=rb[:], func=AF.Exp, scale=cvec[:, n, 2 + dd:3 + dd]), [rb, cvec], [rb])
            S.op("act", lambda e, dd=dd: e.activation(
                out=AE5[:, sb_i, dd, 0, n:n + 1], in_=rsum[:, dd:dd + 1], func=AF.Exp,
                scale=cvec[:, n, dd:dd + 1]), [rsum, cvec], [AE_A])
        for dd in range(2):
            rb = d["r%d" % dd]
            S.op("act", lambda e, rb=rb: e.activation(out=rb[:], in_=rb[:], func=AF.Sqrt, scale=-1.0, bias=1.0),
                 [rb], [rb])

    def p1_B(sb_i, n, d, du):
        u = du["u"]
        for dd in range(2):
            ib = d["i%d" % dd]
            S.op("pool", lambda e, ib=ib: e.tensor_tensor(out=ib[:], in0=ib[:], in1=u[:], op=ALU.mult),
                 [ib, u], [ib])
        for dd in range(2):
            rb, ib, ab, hs = d["r%d" % dd], d["i%d" % dd], d["a%d" % dd], d["h%d" % dd]
            S.op("pool", lambda e, ib=ib, rb=rb: e.tensor_tensor(out=ib[:], in0=ib[:], in1=rb[:], op=ALU.mult),
                 [ib, rb], [ib])
            if dd == 0:
                S.op("dve", lambda e, ab=ab, ib=ib, hs=hs: e.tensor_tensor_scan(
                    out=hs[:], data0=ab[:], data1=ib[:], initial=0.0, op0=ALU.mult, op1=ALU.add), [ab, ib], [hs])
                S.op("pool", lambda e, hs=hs: e.tensor_copy(
                    out=AE5[:, sb_i, 0, 1, n:n + 1], in_=hs[:, SB - 1:SB]), [hs], [AE_E])
            else:
                S.op("dve", lambda e, ab=ab, ib=ib, hs=hs: e.tensor_tensor_scan(
                    out=hs[:, ::-1], data0=ab[:, ::-1], data1=ib[:, ::-1], initial=0.0,
                    op0=ALU.mult, op1=ALU.add), [ab, ib], [hs])
                S.op("pool", lambda e, hs=hs: e.tensor_copy(
                    out=AE5[:, sb_i, 1, 1, n:n + 1], in_=hs[:, 0:1]), [hs], [AE_E])
            S.dma("sp", ab_d[sb_i, n, 2 * dd], ab[:], ab.ds, reads=[ab], writes=[ab_t[sb_i][n][2 * dd]])
            S.dma("sp", ab_d[sb_i, n, 2 * dd + 1], ib[:], ib.ds, reads=[ib], writes=[ab_t[sb_i][n][2 * dd + 1]])

    steps = [(sb_i, n) for sb_i in range(NSB) for n in range(16)]
    NS = len(steps)

    def args(k):
        return steps[k] + (p1sets[k % 2], p1usets[k % 3])
    for k in range(NS + 2):
        if k < NS:
            sb_i, n = steps[k]
            if n == 0:
                load_xn(sb_i)
            p1_A0(sb_i, n, p1usets[k % 3])
        if 1 <= k <= NS:
            p1_A1(*args(k - 1))
        if 2 <= k:
            p1_B(*args(k - 2))
    bp[0] = 0

    S.dma("sp", ag_in.ap(), AE[:], AE.ds, reads=[AE, AE_A, AE_E], writes=[agin_t])
    cc_sem = S.dsem("cc_sem")
    S.dma("pool", None, None, cc_sem, reads=[agin_t], writes=[agout_t], inc=1,
          fn=lambda e: e.collective_compute("AllGather", ALU.bypass, replica_groups=[list(range(NCORES))],
                                            ins=[ag_in.ap().opt()], outs=[ag_out.ap().opt()]))
    S.dma("sp", G[:], ag_out.ap().rearrange("(r p) f -> p r f", p=128), G.ds, reads=[agout_t], writes=[G])
    G6 = G[:].rearrange("p r (s d a n) -> p r s d a n", s=2, d=2, a=2)

    def carry_chain():
        for v in range(16):
            r, s = divmod(v, 2)
            S.op("dve", lambda e, v=v, r=r, s=s: e.tensor_tensor(
                out=htmp[:, 0, :], in0=Hf[:, v, :], in1=G6[:, r, s, 0, 0, :], op=ALU.mult), [Hf, G], [htmp])
            S.op("dve", lambda e, v=v, r=r, s=s: e.tensor_tensor(
                out=Hf[:, v + 1, :], in0=htmp[:, 0, :], in1=G6[:, r, s, 0, 1, :], op=ALU.add), [htmp, G], [Hf])
        for v in range(15, 0, -1):
            r, s = divmod(v, 2)
            S.op("dve", lambda e, v=v, r=r, s=s: e.tensor_tensor(
                out=htmp[:, 1, :], in0=Hb[:, v, :], in1=G6[:, r, s, 1, 0, :], op=ALU.mult), [Hb, G], [htmp])
            S.op("dve", lambda e, v=v, r=r, s=s: e.tensor_tensor(
                out=Hb[:, v - 1, :], in0=htmp[:, 1, :], in1=G6[:, r, s, 1, 1, :], op=ALU.add), [htmp, G], [Hb])
        Hin4 = Hin[:].rearrange("p (s d n) -> p s d n", s=2, d=2)
        for s in range(2):
            for dd, HH in ((0, Hf), (1, Hb)):
                S.op("dve", lambda e, s=s, HH=HH: e.tensor_tensor(
                    out=htmp[:], in0=HH[:, 0:16, :], in1=sel[:, s, :].unsqueeze(2).broadcast_to([128, 16, 16]),
                    op=ALU.mult), [HH, sel], [htmp])
                S.op("dve", lambda e, s=s, dd=dd: e.tensor_reduce(
                    out=Hin4[:, s, dd, :], in_=htmp[:].rearrange("p v n -> p n v"), axis=AX.X, op=ALU.add),
                    [htmp], [Hin])
    Hin4 = Hin[:].rearrange("p (s d n) -> p s d n", s=2, d=2)

    if stop_after < 2:
        carry_chain()
        hd = S.dsem("d_hin")
        S.final.append(S.dma("sp", dbg["hin"], Hin[:], hd, reads=[Hin]))
        S.dma("sp", out_d[0:128, 0:128], G[:, 0, :], G.ds, reads=[G])
        S.final.append((G.ds.h, G.ds.n))
        for dd in range(2):
            for s in range(2):
                for k in ("a", "i"):
                    t = p1sets[s]["%s%d" % (k, dd)]
                    S.final.append((t.ds.h, t.ds.n))
        S.finish()
        return nc

    P2B = 32768
    cs_t = ph("cs_t", P2B, [32, 2, TX], F32, 4, True)
    kT = ph("kT", P2B + 10240, [128, 4, TX], BF16, 2)
    Vt = ph("Vt", P2B + 20480, [128, 10, 512], BF16, 2)
    qT = [ph("qT%d" % i, P2B + 30720 + i * 8192, [128, 4, SB], BF16, 2) for i in range(2)]
    PT = [ph("PT%d" % i, P2B + 47104 + i * 3072, [128, 3, 512], BF16, 2) for i in range(2)]
    qf = ph("qf", P2B + 53248, [32, TX], F32, 4)
    sw = ph("sw", P2B + 58368, [32, TX], F32, 4)
    Dt = [ph("Dt%d" % i, P2B + 63488 + i * 2048, [128, 512], F32, 4) for i in range(2)]
    ot = [ph("ot%d" % i, P2B + 67584 + i * 2048, [128, 512], F32, 4) for i in range(2)]
    P3B = 65536
    RL = [[ph("RL%d_%d" % (i, j), P3B + i * 16384 + j * 4096, [128, SB], F32, 4, True) for j in range(4)]
          for i in range(2)]
    hfb2 = [ph("hfb%d" % i, P3B + 32768 + i * 4096, [128, SB], F32, 4) for i in range(2)]
    hbb = ph("hbb", P3B + 40960, [128, SB], F32, 4)
    sgb = ph("sgb", P3B + 45056, [128, SB], F32, 4)
    sm = [ph("sm%d" % i, EXTRA + i * 4096, [128, SB], F32, 4) for i in range(2)]
    t12 = [ph("t12_%d" % i, EXTRA + 8192 + i * 4096, [128, SB], F32, 4) for i in range(2)]
    WO = [ph("WO%d" % i, i * 16384, [128, KC, 512], BF16, 2, True) for i in range(4)]
    otile = [ph("otile%d" % i, EXTRA + i * 8192, [128, D], F32, 4, True) for i in range(2)]
    xt5 = [S.sb("xt5_%d" % i, XN_OFF + i * 8192, [128, D], F32, 4, True) for i in range(3)]
    wpost_bc = S.sb("wpost_bc", XN_OFF + 24576, [128, D], F32, 4, True)
    shuf = [(i + 16) % 32 for i in range(32)]

    def mmq(e, W, b0):
        for kc in range(KC):
            for tb in range(2):
                ins = e.matmul(pst[:, b0 + tb, :], lhsT=W[:, kc, :], rhs=XN[:, kc, H + tb * 512:H + (tb + 1) * 512],
                               start=(kc == 0), stop=(kc == KC - 1))
        return ins

    def rope(regions, Tn, dst, csoff):
        rb = []
        for (b0, nbk, c0, ncols) in regions:
            rb += [bank[b0 + k] for k in range(nbk)]
        for (b0, nbk, c0, ncols) in regions:
            if nbk == 2:
                src_all = pst[:, b0:b0 + 2, :]
                src_lo = pst[0:32, b0:b0 + 2, :]
                d_all = dst[:, c0:c0 + ncols].rearrange("p (a b) -> p a b", a=2)
                d_lo = qf[:, c0:c0 + ncols].rearrange("p (a b) -> p a b", a=2)
            else:
                src_all = pst[:, b0, 0:ncols]
                src_lo = pst[0:32, b0, 0:ncols]
                d_all = dst[:, c0:c0 + ncols]
                d_lo = qf[:, c0:c0 + ncols]
            S.op("act", lambda e, s_=src_all, d_=d_all: e.activation(out=d_, in_=s_, func=AF.Copy), rb, [dst_tile[0]])
            S.op("act", lambda e, s_=src_lo, d_=d_lo: e.activation(out=d_, in_=s_, func=AF.Copy), rb, [qf])
        S.op("dve", lambda e: e.stream_shuffle(out=sw[:, 0:Tn], in_=qf[:, 0:Tn], mask=shuf), [qf], [sw])
        S.op("pool", lambda e: e.tensor_tensor(out=qf[:, 0:Tn], in0=qf[:, 0:Tn], in1=cs_t[:, 0, csoff:csoff + Tn],
                                               op=ALU.mult), [qf, cs_t], [qf])
        S.op("pool", lambda e: e.tensor_tensor(out=sw[:, 0:Tn], in0=sw[:, 0:Tn], in1=cs_t[:, 1, csoff:csoff + Tn],
                                               op=ALU.mult), [sw, cs_t], [sw])
        S.op("pool", lambda e: e.tensor_tensor(out=dst[0:32, 0:Tn], in0=qf[:, 0:Tn], in1=sw[:, 0:Tn], op=ALU.add),
             [qf, sw], [dst_tile[0]])

    dst_tile = [None]
    att_it = 0
    for sb_i in range(NSB):
        if sb_i > 0 or True:
            load_xn(sb_i)
        S.dma("sp", cs_t[:], cs_d[:, :, sb_i * SB: sb_i * SB + TX], cs_t.ds, writes=[cs_t])
        for g in range(4):
            W = wnext(("in", CB_K + g))
            b0 = nb(2)
            b2 = nb(1)

            def mmk(e, W=W, b0=b0, b2=b2):
                for kc in range(KC):
                    for (bk, lo, n_) in ((b0, 0, 512), (b0 + 1, 512, 512), (b2, 1024, 256)):
                        ins = e.matmul(pst[:, bk, 0:n_], lhsT=W[:, kc, :], rhs=XN[:, kc, lo:lo + n_],
                                       start=(kc == 0), stop=(kc == KC - 1))
                return ins
            S.op("pe", mmk, [W, XN], [bank[b0], bank[b0 + 1], bank[b2]])
            dst_tile[0] = kT
            rope([(b0, 2, 0, 1024), (b2, 1, 1024, 256)], TX, kT[:, g, :], 0)
        W4 = [wnext(("in", CB_V + g)) for g in range(4)]
        for j in range(10):
            bv = nb(1)

            def mmv(e, j=j, bv=bv, W4=W4):
                for g in range(4):
                    for kc in range(KC):
                        ins = e.matmul(pst[:, bv, g * 128:(g + 1) * 128], lhsT=XN[:, kc, j * 128:(j + 1) * 128],
                                       rhs=W4[g][:, kc, :], start=(kc == 0), stop=(kc == KC - 1))
                return ins
            S.op("pe", mmv, W4 + [XN], [bank[bv]])
            if j % 2 == 0:
                S.op("act", lambda e, j=j, bv=bv: e.activation(out=Vt[:, j, :], in_=pst[:, bv, :], func=AF.Copy),
                     [bank[bv]], [Vt])
            else:
                S.op("dve", lambda e, j=j, bv=bv: e.tensor_copy(out=Vt[:, j, :], in_=pst[:, bv, :]),
                     [bank[bv]], [Vt])
        if debug:
            S.dma("sp", dbg["kT"][sb_i], kT[:], dbg_sem, reads=[kT])
            S.dma("sp", dbg["v"][sb_i], Vt[:], dbg_sem, reads=[Vt])
        def proj_group(g):
            qTg = qT[g % 2]
            for hh in range(4):
                h = 4 * g + hh
                W = wnext(("in", CB_Q + h))
                b0 = nb(2)
                S.op("pe", lambda e, W=W, b0=b0: mmq(e, W, b0), [W, XN], [bank[b0], bank[b0 + 1]])
                dst_tile[0] = qTg
                rope([(b0, 2, 0, 1024)], SB, qTg[:, hh, :], H)
                W = wnext(("in", CB_GA + h))
                b0 = nb(2)
                S.op("pe", lambda e, W=W, b0=b0: mmq(e, W, b0), [W, XN], [bank[b0], bank[b0 + 1]])
                S.op("act", lambda e, h=h, b0=b0: e.activation(
                    out=YB[:, h, :].rearrange("p (a b) -> p a b", a=2), in_=pst[:, b0:b0 + 2, :], func=AF.Silu),
                    [bank[b0], bank[b0 + 1]], [YB])

        def att_scores(g, n, k):
            qTg = qT[g % 2]
            pt, dtt, ott = PT[k % 2], Dt[k % 2], ot[k % 2]
            sbk = [nb(1) for _ in range(3)]

            def mms(e):
                for dj in range(3):
                    ins = e.matmul(pst[:, sbk[dj], :], lhsT=kT[:, g, (n + dj) * 128:(n + dj + 1) * 128],
                                   rhs=qTg[:, :, n * 128:(n + 1) * 128], start=True, stop=True)
                return ins
            S.op("pe", mms, [kT, qTg], [bank[b] for b in sbk])
            for dj in range(3):
                S.op("act", lambda e, dj=dj: e.activation(
                    out=pt[:, dj, :], in_=pst[:, sbk[dj], :], func=AF.Exp, scale=SCALE), [bank[sbk[dj]]], [pt])
            mprev = 2 if (sb_i == 0 and n == 0) else 0
            mnext = 3 if (sb_i == NSB - 1 and n == 7) else 1
            for dj, mi in ((0, mprev), (2, mnext)):
                S.op("pool", lambda e, dj=dj, mi=mi: e.tensor_tensor(
                    out=pt[:, dj, :].rearrange("p (a b) -> p a b", a=4),
                    in0=pt[:, dj, :].rearrange("p (a b) -> p a b", a=4),
                    in1=masks[:, mi, :].unsqueeze(1).broadcast_to([128, 4, 128]), op=ALU.mult), [pt, masks], [pt])
            return (g, n, pt, dtt, ott)

        def att_rest(g, n, pt, dtt, ott):
            bd = nb(1)
            bo = nb(1)

            def mmd(e):
                for dj in range(3):
                    ins = e.matmul(pst[:, bd, :], lhsT=ones[:], rhs=pt[:, dj, :], start=(dj == 0), stop=(dj == 2))
                return ins
            S.op("pe", mmd, [ones, pt], [bank[bd]])

            def mmo(e):
                for dj in range(3):
                    ins = e.matmul(pst[:, bo, :], lhsT=Vt[:, n + dj, g * 128:(g + 1) * 128], rhs=pt[:, dj, :],
                                   start=(dj == 0), stop=(dj == 2))
                return ins
            S.op("pe", mmo, [Vt, pt], [bank[bo]])
            S.op("dve", lambda e: e.tensor_tensor(
                out=dtt[:].rearrange("p (a b) -> p a b", a=4), in0=pst[:, bd, :].rearrange("p (a b) -> p a b", a=4),
                in1=esbc[:, 4 * g:4 * g + 4, :], op=ALU.add), [bank[bd], esbc], [dtt])
            S.op("act", lambda e: e.activation(out=dtt[:], in_=dtt[:], func=AF.Ln), [dtt], [dtt])
            S.op("act", lambda e: e.activation(out=dtt[:], in_=dtt[:], func=AF.Exp, scale=-1.0), [dtt], [dtt])
            S.op("dve", lambda e: e.tensor_tensor(out=ott[:], in0=pst[:, bo, :], in1=dtt[:], op=ALU.mult),
                 [bank[bo], dtt], [ott])
            S.op("pool", lambda e: e.tensor_tensor(
                out=YB[:, 4 * g:4 * g + 4, n * 128:(n + 1) * 128],
                in0=YB[:, 4 * g:4 * g + 4, n * 128:(n + 1) * 128],
                in1=ott[:].rearrange("p (a b) -> p a b", a=4), op=ALU.mult), [YB, ott], [YB])

        proj_group(0)
        for g in range(4):
            if g + 1 < 4:
                proj_group(g + 1)
            prev = None
            for n in range(8):
                cur = att_scores(g, n, att_it)
                att_it += 1
                if prev is not None:
                    att_rest(*prev)
                prev = cur
            att_rest(*prev)
        if debug:
            S.dma("sp", dbg["yb"][sb_i], YB[:], dbg_sem, reads=[YB])
        if stop_after < 3:
            continue
        if sb_i == 0:
            carry_chain()
        def reload(n, slot):
            for j in range(4):
                t = RL[slot][j]
                S.dma("sp", t[:], ab_d[sb_i, n, j], t.ds, reads=[ab_t[sb_i][n][j]], writes=[t])
        reload(0, 0)
        for n in range(16):
            if n + 1 < 16:
                reload(n + 1, (n + 1) % 2)
            af, bf_, ab_, bb_ = RL[n % 2]
            hfb = hfb2[n % 2]
            W = wnext(("in", CB_GL + n))
            b0 = nb(2)
            S.op("pe", lambda e, W=W, b0=b0: mmq(e, W, b0), [W, XN], [bank[b0], bank[b0 + 1]])
            S.op("act", lambda e, b0=b0: e.activation(
                out=sgb[:].rearrange("p (a b) -> p a b", a=2), in_=pst[:, b0:b0 + 2, :], func=AF.Silu),
                [bank[b0], bank[b0 + 1]], [sgb])
            S.op("dve", lambda e, af=af, bf_=bf_, n=n, sb_i=sb_i, hfb=hfb: e.tensor_tensor_scan(
                out=hfb[:], data0=af[:], data1=bf_[:], initial=Hin4[:, sb_i, 0, n:n + 1],
                op0=ALU.mult, op1=ALU.add), [af, bf_, Hin], [hfb])
            S.op("dve", lambda e, ab_=ab_, bb_=bb_, n=n, sb_i=sb_i: e.tensor_tensor_scan(
                out=hbb[:, ::-1], data0=ab_[:, ::-1], data1=bb_[:, ::-1], initial=Hin4[:, sb_i, 1, n:n + 1],
                op0=ALU.mult, op1=ALU.add), [ab_, bb_, Hin], [hbb])
            S.op("dve", lambda e, hfb=hfb: e.tensor_tensor(out=hfb[:], in0=hfb[:], in1=hbb[:], op=ALU.add),
                 [hfb, hbb], [hfb])
            S.op("pool", lambda e, n=n, hfb=hfb: e.tensor_tensor(out=YA[:, n, :], in0=hfb[:], in1=sgb[:], op=ALU.mult),
                 [hfb, sgb], [YA])
        if debug:
            S.dma("sp", dbg["ya"][sb_i], YA[:], dbg_sem, reads=[YA])
        if stop_after < 4:
            continue
        for f in range(16):
            for half, (wk, mk_cb, Y) in enumerate(((("a", f), CB_ML + f, YA), (("b", f), CB_MA + f, YB))):
                Wp = wnext(wk)
                Wm = wnext(("in", mk_cb))
                bpj = nb(2)
                bm = nb(2)

                def mmp(e, Wp=Wp, bpj=bpj, Y=Y):
                    for kc in range(KC):
                        for tb in range(2):
                            ins = e.matmul(pst[:, bpj + tb, :], lhsT=Wp[:, kc, :], rhs=Y[:, kc, tb * 512:(tb + 1) * 512],
                                           start=(kc == 0), stop=(kc == KC - 1))
                    return ins
                S.op("pe", mmp, [Wp, Y], [bank[bpj], bank[bpj + 1]])
                S.op("pe", lambda e, Wm=Wm, bm=bm: mmq(e, Wm, bm), [Wm, XN], [bank[bm], bank[bm + 1]])
                S.op("act", lambda e, half=half, bm=bm: e.activation(
                    out=sm[half][:].rearrange("p (a b) -> p a b", a=2), in_=pst[:, bm:bm + 2, :], func=AF.Sigmoid),
                    [bank[bm], bank[bm + 1]], [sm[half]])
                S.op("dve", lambda e, half=half, bpj=bpj: e.tensor_tensor(
                    out=t12[half][:].rearrange("p (a b) -> p a b", a=2), in0=pst[:, bpj:bpj + 2, :],
                    in1=sm[half][:].rearrange("p (a b) -> p a b", a=2), op=ALU.mult),
                    [bank[bpj], bank[bpj + 1], sm[half]], [t12[half]])
            S.op("pool", lambda e, f=f: e.tensor_tensor(out=MG[:, f, :], in0=t12[0][:], in1=t12[1][:], op=ALU.add),
                 [t12[0], t12[1]], [MG])
        if debug:
            S.dma("sp", dbg["mg"][sb_i], MG[:], dbg_sem, reads=[MG])
        if stop_after < 5:
            continue
        for cg in range(4):
            S.dma("pool", WO[cg][:], w_o[:, cg * 512:(cg + 1) * 512].rearrange("(kc p) n -> p kc n", p=128),
                  WO[cg].ds, writes=[WO[cg]])
        S.dma("sp", wpost_bc[:], wpost_d.partition_broadcast(128), wpost_bc.ds, writes=[wpost_bc])
        for i in range(8):
            xt = xt5[i % 3]
            ot_ = otile[i % 2]
            r0 = H + sb_i * SB + i * 128
            S.dma("sp", xt[:], x_ext[r0:r0 + 128, :], xt.ds, writes=[xt])
            bq = [nb(2), nb(2)]
            bks = [bq[0], bq[0] + 1, bq[1], bq[1] + 1]

            def mmf(e, i=i, bks=bks):
                for cg in range(4):
                    for kc in range(KC):
                        ins = e.matmul(pst[:, bks[cg], :], lhsT=MG[:, kc, i * 128:(i + 1) * 128],
                                       rhs=WO[cg][:, kc, :], start=(kc == 0), stop=(kc == KC - 1))
                return ins
            S.op("pe", mmf, [MG] + WO, [bank[b] for b in bks])
            for cg in range(4):
                S.op("act", lambda e, cg=cg, ot_=ot_, bks=bks: e.activation(
                    out=ot_[:, cg * 512:(cg + 1) * 512], in_=pst[:, bks[cg], :], func=AF.Square,
                    accum_out=ss_t[:, 4 + cg:5 + cg]), [bank[bks[cg]]], [ot_, ss_t])
            S.op("dve", lambda e: e.tensor_reduce(out=rs_t[:, 0:1], in_=ss_t[:, 4:8], axis=AX.X, op=ALU.add),
                 [ss_t], [rs_t])
            S.op("dve", lambda e: e.tensor_scalar(out=rs_t[:, 0:1], in0=rs_t[:, 0:1], scalar1=1.0 / D, scalar2=EPS,
                                                  op0=ALU.mult, op1=ALU.add), [rs_t], [rs_t])
            S.op("act", lambda e: e.activation(out=rs_t[:, 0:1], in_=rs_t[:, 0:1], func=AF.Sqrt), [rs_t], [rs_t])
            S.op("dve", lambda e: e.reciprocal(out=rs_t[:, 0:1], in_=rs_t[:, 0:1]), [rs_t], [rs_t])
            for cg in range(4):
                S.op("dve", lambda e, cg=cg, ot_=ot_, bks=bks: e.scalar_tensor_tensor(
                    out=ot_[:, cg * 512:(cg + 1) * 512], in0=pst[:, bks[cg], :], scalar=rs_t[:, 0:1],
                    in1=wpost_bc[:, cg * 512:(cg + 1) * 512], op0=ALU.mult, op1=ALU.mult),
                    [bank[bks[cg]], rs_t, wpost_bc], [ot_])
            S.op("pool", lambda e, ot_=ot_, xt=xt: e.tensor_tensor(out=ot_[:], in0=ot_[:], in1=xt[:], op=ALU.add),
                 [ot_, xt], [ot_])
            o0 = sb_i * SB + i * 128
            S.final.append(S.dma("sp", out_d[o0:o0 + 128, :], ot_[:], ot_.ds, reads=[ot_]))

    if debug:
        S.final.append((dbg_sem.h, dbg_sem.n))
    if stop_after < 5:
        S.dma("sp", out_d[0:128, 0:128], G[:, 0, :], G.ds, reads=[G])
        S.final.append((G.ds.h, G.ds.n))
    S.finish()
    return nc


def make_in_maps(x, norm_pre_w, w_in, conv_w, conv_b, lru_w_r, lru_b_r, lru_w_i, lru_b_i, lru_lambda,
                 attn_sink, w_proj_a, w_proj_b, w_out, norm_post_w):
    f32 = np.float32
    x2 = np.asarray(x, f32).reshape(S_FULL, D)
    xpad = np.zeros((S_FULL + 2 * H, D), f32)
    xpad[H:H + S_FULL] = x2
    w_in2 = np.ascontiguousarray(np.asarray(w_in, f32)[0])
    w_a2 = np.ascontiguousarray(np.asarray(w_proj_a, f32)[0])
    w_b2 = np.ascontiguousarray(np.asarray(w_proj_b, f32)[0])
    w_o2 = np.ascontiguousarray(np.asarray(w_out, f32)[0])
    gate_w = np.ascontiguousarray(np.stack([np.asarray(lru_w_r, f32)[0], np.asarray(lru_w_i, f32)[0]], axis=1))
    vecs = [np.asarray(conv_w, f32)[0, t] for t in range(4)] + [np.asarray(conv_b, f32)[0]]
    vecs += [np.asarray(lru_b_r, f32)[0, 0], np.asarray(lru_b_r, f32)[0, 1]]
    vecs += [np.asarray(lru_b_i, f32)[0, 0], np.asarray(lru_b_i, f32)[0, 1]]
    vecs += [np.asarray(lru_lambda, f32)[0, 0], np.asarray(lru_lambda, f32)[0, 1]]
    pvec = np.ascontiguousarray(np.stack([v.reshape(16, 128).T for v in vecs], axis=-1))
    wpre = np.ascontiguousarray(np.asarray(norm_pre_w, f32)[0])
    wpost = np.ascontiguousarray(np.asarray(norm_post_w, f32)[0])
    sink = np.ascontiguousarray(np.asarray(attn_sink, f32)[0])
    ident = np.eye(128, dtype=f32).astype(ml_dtypes.bfloat16)
    inv_freq = (np.float32(500000.0) ** (-np.arange(0, 32, 2, dtype=f32) / np.float32(32))).astype(f32)
    kk = np.arange(128)[:, None]
    qq = np.arange(128)[None, :]
    tri_prev = (kk >= qq).astype(f32)
    tri_next = (kk <= qq).astype(f32)
    maps = []
    for c in range(NCORES):
        pos = (np.arange(TE, dtype=np.int64) + c * T - H).astype(f32)
        ang = pos[:, None] * inv_freq[None, :]
        cos, sin = np.cos(ang).astype(f32), np.sin(ang).astype(f32)
        cs = np.zeros((32, 2, TE), f32)
        cs[0:16, 0] = cos.T
        cs[16:32, 0] = cos.T
        cs[0:16, 1] = -sin.T
        cs[16:32, 1] = sin.T
        masks = np.stack([tri_prev, tri_next, tri_prev * (0.0 if c == 0 else 1.0),
                          tri_next * (0.0 if c == NCORES - 1 else 1.0)], axis=1).astype(ml_dtypes.bfloat16)
        sel = np.zeros((128, 2, 16), f32)
        for s in range(2):
            sel[:, s, 2 * c + s] = 1.0
        maps.append({
            "x_ext": np.ascontiguousarray(xpad[c * T: c * T + TE]),
            "w_in": w_in2, "w_a": w_a2, "w_b": w_b2, "w_o": w_o2, "gate_w": gate_w, "pvec": pvec,
            "wpre": wpre, "wpost": wpost, "sink": sink, "cs": cs, "masks": np.ascontiguousarray(masks),
            "ident": ident, "sel": sel,
        })
    return maps


_NC_CACHE = {}


def kernel(**inputs):
    maps = make_in_maps(**inputs)
    if "nc" not in _NC_CACHE:
        _NC_CACHE["nc"] = build_program()
    res = run_bass_kernel_spmd(_NC_CACHE["nc"], maps, core_ids=list(range(NCORES)))
    out = np.concatenate([np.asarray(r["out"], np.float32) for r in res.results], axis=0)
    return out.reshape(1, S_FULL, D)
```

```python
import numpy as np
import ml_dtypes
import concourse.bass as bass
import concourse.mybir as mybir
from concourse.bass_utils import run_bass_kernel_spmd

F32 = mybir.dt.float32
BF16 = mybir.dt.bfloat16
ALU = mybir.AluOpType
AF = mybir.ActivationFunctionType
AX = mybir.AxisListType

NCORES = 8
S_FULL = 16384
D = 2048
T = S_FULL // NCORES
SB = 1024
NSB = T // SB
H = 128
TX = SB + 2 * H
TE = T + 2 * H
KC = 16
IN_W = 13312
EPS = 1e-6
LRU_C = 8.0
SCALE = 128 ** -0.5
CB_U, CB_GL, CB_Q, CB_K, CB_V, CB_GA, CB_ML, CB_MA = 0, 16, 32, 48, 52, 56, 72, 88

SBUF_BASE = 16640
SBUF_END = 229344


class Tile:
    __slots__ = ("name", "t", "lo", "hi", "w", "r", "over", "ds")

    def __init__(self, name, t, lo, hi):
        self.name, self.t, self.lo, self.hi = name, t, lo, hi
        self.w = None
        self.r = {}
        self.over = [self]
        self.ds = None

    def __getitem__(self, k):
        return self.t[k]


class DSem:
    def __init__(self, h):
        self.h, self.n = h, 0


class Sched:
    def __init__(self, nc):
        self.nc = nc
        self.E = {"pe": nc.tensor, "act": nc.scalar, "dve": nc.vector, "pool": nc.gpsimd, "sp": nc.sync}
        self.sem = {e: nc.alloc_semaphore("sem_" + e) for e in ("pe", "act", "dve", "pool")}
        self.cnt = {e: 0 for e in self.sem}
        self.prog = {e: [] for e in self.E}
        self.seen = {e: {} for e in self.E}
        self.sb_tiles = []
        self.final = []
        self.nops = 0
        self.nwaits = 0

    def sb(self, name, off, shape, dtype, esz, dsem=False):
        n = 1
        for s in shape[1:]:
            n *= s
        assert SBUF_BASE <= off and off + n * esz <= SBUF_END, (name, off, n * esz)
        t = self.nc.alloc_sbuf_tensor_at(name, list(shape), dtype, offset=off)
        tl = Tile(name, t, off, off + n * esz)
        for o in self.sb_tiles:
            if o.lo < tl.hi and tl.lo < o.hi:
                o.over.append(tl)
                tl.over.append(o)
        self.sb_tiles.append(tl)
        if dsem:
            tl.ds = self.dsem("d_" + name)
        return tl

    def raw(self, name, t=None, dsem=False):
        tl = Tile(name, t, 0, 0)
        if dsem:
            tl.ds = self.dsem("d_" + name)
        return tl

    def dsem(self, name):
        return DSem(self.nc.alloc_semaphore(name))

    def _deps(self, eng, reads, writes, extra=()):
        deps = {}

        def add(ev):
            if ev is None:
                return
            s, v = ev
            k = s.name
            if k not in deps or deps[k][1] < v:
                deps[k] = (s, v)

        for t in reads:
            for o in t.over:
                add(o.w)
        for t in writes:
            for o in t.over:
                add(o.w)
                for ev in o.r.values():
                    add(ev)
        for ev in extra:
            add(ev)
        out = []
        seen = self.seen[eng]
        own = self.sem[eng].name if eng in self.sem else None
        for k, (s, v) in deps.items():
            if eng == "pe" and k == own:
                continue
            if seen.get(k, 0) >= v:
                continue
            seen[k] = v
            out.append((s, v))
        return out

    def _commit(self, ev, reads, writes):
        k = ev[0].name
        for t in writes:
            t.w = ev
            t.r = {}
        for t in reads:
            if k not in t.r or t.r[k][1] < ev[1]:
                t.r[k] = ev

    def op(self, eng, fn, reads=(), writes=(), extra=()):
        waits = self._deps(eng, reads, writes, extra)
        sem = self.sem[eng]
        self.cnt[eng] += 1
        ev = (sem, self.cnt[eng])
        self.nops += 1
        self.nwaits += len(waits)

        def emit(e, waits=waits, fn=fn, sem=sem):
            for s, v in waits:
                e.wait_ge(s, v)
            fn(e).then_inc(sem, 1)

        self.prog[eng].append(emit)
        self._commit(ev, reads, writes)
        return ev

    def dma(self, q, out_ap, in_ap, ds, reads=(), writes=(), extra=(), inc=16, fn=None):
        ex = list(extra)
        if ds.n > 0:
            ex.append((ds.h, ds.n))
        waits = self._deps(q, reads, writes, ex)
        ds.n += inc
        ev = (ds.h, ds.n)
        self.nops += 1
        self.nwaits += len(waits)

        def emit(e, waits=waits, out_ap=out_ap, in_ap=in_ap, h=ds.h, inc=inc, fn=fn):
            for s, v in waits:
                e.wait_ge(s, v)
            ins = fn(e) if fn is not None else e.dma_start(out=out_ap, in_=in_ap)
            ins.then_inc(h, inc)

        self.prog[q].append(emit)
        self._commit(ev, reads, writes)
        return ev

    def finish(self):
        fin = {}
        for s, v in self.final:
            if s.name not in fin or fin[s.name][1] < v:
                fin[s.name] = (s, v)

        def run(name):
            def f(e):
                for fn in self.prog[name]:
                    fn(e)
                if name == "sp":
                    for s, v in fin.values():
                        e.wait_ge(s, v)
            return f

        with self.nc.Block() as block:
            block.tensor(run("pe"))
            block.scalar(run("act"))
            block.vector(run("dve"))
            block.gpsimd(run("pool"))
            block.sync(run("sp"))


def build_program(stop_after=5, debug=False):
    nc = bass.Bass("TRN2", target_bir_lowering=False)
    S = Sched(nc)
    dk = "ExternalOutput" if debug else "Internal"

    def din(name, shape, dt=F32):
        return nc.dram_tensor(name, list(shape), dt, kind="ExternalInput").ap()

    x_ext = din("x_ext", [TE, D])
    w_in = din("w_in", [D, IN_W])
    w_a = din("w_a", [D, D])
    w_b = din("w_b", [D, D])
    w_o = din("w_o", [D, D])
    gate_w = din("gate_w", [2, 2, 16, 128, 128])
    pvec_d = din("pvec", [128, 16, 11])
    wpre_d = din("wpre", [D])
    wpost_d = din("wpost", [D])
    sink_d = din("sink", [16])
    cs_d = din("cs", [32, 2, TE])
    masks_d = din("masks", [128, 4, 128], BF16)
    ident_d = din("ident", [128, 128], BF16)
    sel_d = din("sel", [128, 2, 16])
    out_d = nc.dram_tensor("out", [T, D], F32, kind="ExternalOutput").ap()
    xnT_d = nc.dram_tensor("xnT_d", [KC, 128, TE], BF16, kind=dk).ap()
    ab_d = nc.dram_tensor("ab_d", [NSB, 16, 4, 128, SB], F32, kind=dk).ap()
    wo_bf = nc.dram_tensor("wo_bf", [128, KC, D], BF16, kind="Internal").ap()
    ag_in = nc.dram_tensor("ag_in", [128, 128], F32)
    ag_out = nc.dram_tensor("ag_out", [NCORES * 128, 128], F32)
    dbg = {}
    if debug:
        dbg["ya"] = nc.dram_tensor("dbg_ya", [NSB, 128, 16, SB], BF16, kind="ExternalOutput").ap()
        dbg["yb"] = nc.dram_tensor("dbg_yb", [NSB, 128, 16, SB], BF16, kind="ExternalOutput").ap()
        dbg["mg"] = nc.dram_tensor("dbg_mg", [NSB, 128, 16, SB], BF16, kind="ExternalOutput").ap()
        dbg["hin"] = nc.dram_tensor("dbg_hin", [128, 64], F32, kind="ExternalOutput").ap()
        dbg["kT"] = nc.dram_tensor("dbg_kT", [NSB, 128, 4, TX], BF16, kind="ExternalOutput").ap()
        dbg["v"] = nc.dram_tensor("dbg_v", [NSB, 128, 10, 512], BF16, kind="ExternalOutput").ap()
        dbg["qT"] = nc.dram_tensor("dbg_qT", [NSB, 4, 128, 4, SB], BF16, kind="ExternalOutput").ap()
    dbg_sem = S.dsem("d_dbg")
    xnT_t = [S.raw("xnT_d%d" % i) for i in range(18)]
    ab_t = [[[S.raw("ab_d_%d_%d_%d" % (s, n, j)) for j in range(4)] for n in range(16)] for s in range(NSB)]
    agin_t = S.raw("ag_in")
    wobf_t = S.raw("wo_bf")
    agout_t = S.raw("ag_out")

    P_OFF = SBUF_BASE
    poff = [P_OFF]

    def pers(name, shape, dt, esz, dsem=False):
        n = int(np.prod(shape[1:])) * esz
        t = S.sb(name, poff[0], shape, dt, esz, dsem)
        poff[0] += (n + 63) // 64 * 64
        return t

    pvec = pers("pvec", [128, 16, 11], F32, 4, True)
    cvec = pers("cvec", [128, 16, 4], F32, 4)
    sinkb = pers("sinkb", [128, 16], F32, 4, True)
    esink = pers("esink", [128, 16], F32, 4)
    esbc = pers("esbc", [128, 16, 128], F32, 4)
    masks = pers("masks", [128, 4, 128], BF16, 2, True)
    ident = pers("ident", [128, 128], BF16, 2, True)
    ones = pers("ones", [128, 128], BF16, 2)
    AE = pers("AE", [128, 128], F32, 4, True)
    G = pers("G", [128, NCORES, 128], F32, 4, True)
    Hf = pers("Hf", [128, 17, 16], F32, 4)
    Hb = pers("Hb", [128, 17, 16], F32, 4)
    Hin = pers("Hin", [128, 64], F32, 4)
    sel = pers("sel", [128, 2, 16], F32, 4, True)
    htmp = pers("htmp", [128, 16, 16], F32, 4)
    rsum = pers("rsum", [128, 4], F32, 4)
    ss_t = pers("ss", [128, 8], F32, 4)
    rs_t = pers("rs", [128, 2], F32, 4)
    lam_t = pers("lam_t", [128, 32, 4], F32, 4)
    assert poff[0] <= P_OFF + 20480, poff[0]
    RING_OFF = P_OFF + 20480
    NSLOT = 8
    ring = [S.sb("ring%d" % i, RING_OFF + i * 4096, [128, KC, 128], BF16, 2, True) for i in range(NSLOT)]
    XN_OFF = RING_OFF + NSLOT * 4096
    XN = S.sb("XN", XN_OFF, [128, KC, TX], BF16, 2, True)
    XNh = [S.sb("XNh%d" % j, XN_OFF + j * 8 * TX * 2, [128, 8, TX], BF16, 2, True) for j in range(2)]
    PH = XN_OFF + KC * TX * 2
    PH_SIZE = SBUF_END - PH
    assert PH_SIZE >= 118400, PH_SIZE

    def ph(name, rel, shape, dt, esz, dsem=False):
        n = int(np.prod(shape[1:])) * esz
        assert rel + n <= PH_SIZE, (name, rel, n, PH_SIZE)
        return S.sb(name, PH + rel, shape, dt, esz, dsem)

    YB = ph("YB", 0, [128, 16, SB], BF16, 2, True)
    YA = ph("YA", 32768, [128, 16, SB], BF16, 2, True)
    MG = ph("MG", 65536, [128, 16, SB], BF16, 2, True)
    EXTRA = 98304

    pst = nc.alloc_psum_tensor("pst", [128, 8, 512], F32)
    bank = [S.raw("bank%d" % i) for i in range(8)]
    pbv = [pst[:, b, :].bitcast(BF16) for b in range(8)]
    bp = [0]

    def nb(k=1):
        p = bp[0]
        if k == 2 and p % 2 == 1:
            p += 1
        if p + k > 8:
            p = 0
        bp[0] = (p + k) % 8
        return p

    wsched = []
    for s in range(NSB):
        for n in range(16):
            wsched.append(("in", CB_U + n))
    for s in range(NSB):
        for g in range(4):
            wsched.append(("in", CB_K + g))
        for g in range(4):
            wsched.append(("in", CB_V + g))
        for g in range(4):
            for hh in range(4):
                wsched.append(("in", CB_Q + 4 * g + hh))
                wsched.append(("in", CB_GA + 4 * g + hh))
        for n in range(16):
            wsched.append(("in", CB_GL + n))
        for f in range(16):
            wsched.append(("a", f))
            wsched.append(("in", CB_ML + f))
            wsched.append(("b", f))
            wsched.append(("in", CB_MA + f))
    wsrc = {"in": w_in, "a": w_a, "b": w_b}
    wstate = {"issued": 0, "next": 0}
    PREFETCH = 4

    def wissue(upto):
        while wstate["issued"] < min(upto, len(wsched)):
            j = wstate["issued"]
            kind, cb = wsched[j]
            src = wsrc[kind][:, cb * 128:(cb + 1) * 128].rearrange("(kc p) n -> p kc n", p=128)
            slot = ring[j % NSLOT]
            S.dma("pool", slot[:], src, slot.ds, writes=[slot])
            wstate["issued"] += 1

    def wnext(key):
        i = wstate["next"]
        assert wsched[i] == key, (i, wsched[i], key)
        wissue(i + PREFETCH + 1)
        wstate["next"] += 1
        return ring[i % NSLOT]

    S.dma("sp", pvec[:], pvec_d, pvec.ds, writes=[pvec])
    S.dma("sp", sinkb[:], sink_d.partition_broadcast(128), sinkb.ds, writes=[sinkb])
    S.dma("sp", masks[:], masks_d, masks.ds, writes=[masks])
    S.dma("sp", ident[:], ident_d, ident.ds, writes=[ident])
    S.dma("sp", sel[:], sel_d, sel.ds, writes=[sel])
    S.op("pool", lambda e: e.memset(ones[:], 1.0), [], [ones])
    S.op("pool", lambda e: e.memset(Hf[:], 0.0), [], [Hf])
    S.op("pool", lambda e: e.memset(Hb[:], 0.0), [], [Hb])
    lamv = pvec[:, :, 9:11]
    y_ = lam_t[:, 0:16, 0:2]
    z_ = lam_t[:, 0:16, 2:4]
    z2_ = lam_t[:, 16:32, 0:2]
    acc_ = lam_t[:, 16:32, 2:4]
    S.op("act", lambda e: e.activation(out=y_, in_=lamv, func=AF.Exp, scale=-1.0), [pvec], [lam_t])
    S.op("dve", lambda e: e.tensor_scalar(out=z_, in0=y_, scalar1=2.0, scalar2=None, op0=ALU.add), [lam_t], [lam_t])
    S.op("dve", lambda e: e.reciprocal(out=z_, in_=z_), [lam_t], [lam_t])
    S.op("dve", lambda e: e.tensor_tensor(out=z_, in0=z_, in1=y_, op=ALU.mult), [lam_t], [lam_t])
    S.op("dve", lambda e: e.tensor_tensor(out=z2_, in0=z_, in1=z_, op=ALU.mult), [lam_t], [lam_t])
    NT = 9
    S.op("dve", lambda e: e.memset(acc_, 1.0 / (2 * NT + 1)), [], [lam_t])
    for k in range(NT - 1, -1, -1):
        S.op("dve", lambda e: e.tensor_tensor(out=acc_, in0=acc_, in1=z2_, op=ALU.mult), [lam_t], [lam_t])
        S.op("dve", lambda e, k=k: e.tensor_scalar(out=acc_, in0=acc_, scalar1=1.0 / (2 * k + 1), scalar2=None,
                                                  op0=ALU.add), [lam_t], [lam_t])
    S.op("dve", lambda e: e.tensor_tensor(out=acc_, in0=acc_, in1=z_, op=ALU.mult), [lam_t], [lam_t])
    S.op("dve", lambda e: e.tensor_scalar(out=cvec[:, :, 0:2], in0=acc_, scalar1=-2.0 * LRU_C, scalar2=None,
                                          op0=ALU.mult), [lam_t], [cvec])
    S.op("dve", lambda e: e.tensor_scalar(out=cvec[:, :, 2:4], in0=acc_, scalar1=-4.0 * LRU_C, scalar2=None,
                                          op0=ALU.mult), [lam_t], [cvec])
    S.op("act", lambda e: e.activation(out=esink[:], in_=sinkb[:], func=AF.Exp), [sinkb], [esink])
    S.op("pool", lambda e: e.tensor_copy(out=esbc[:], in_=esink[:].unsqueeze(2).broadcast_to([128, 16, 128])),
         [esink], [esbc])

    wpre_bc = ph("wpre_bc", 0, [128, D], F32, 4, True)
    xt0 = [ph("xt0_%d" % i, 8192 + i * 8192, [128, D], F32, 4, True) for i in range(3)]
    xt0h = [[ph("xt0_%d_%d" % (i, j), 8192 + i * 8192 + j * 4096, [128, D // 2], F32, 4, True) for j in range(2)]
            for i in range(3)]
    xs0 = [ph("xs0_%d" % i, 32768 + i * 4096, [128, D], BF16, 2) for i in range(2)]
    xT0 = [ph("xT0_%d" % i, 40960 + i * 4096, [128, KC, 128], BF16, 2, True) for i in range(2)]
    S.dma("sp", wpre_bc[:], wpre_d.partition_broadcast(128), wpre_bc.ds, writes=[wpre_bc])
    def p0_A(i):
        xt, xs = xt0[i % 3], xs0[i % 2]
        sc = ss_t[:, (i % 2):(i % 2) + 1]
        rc = rs_t[:, (i % 2):(i % 2) + 1]
        for j, q in enumerate(("pool", "act")):
            xh = xt0h[i % 3][j]
            S.dma(q, xh[:], x_ext[i * 128:(i + 1) * 128, j * 1024:(j + 1) * 1024], xh.ds, writes=[xh])
        S.op("act", lambda e: e.activation(out=xs[:], in_=xt[:], func=AF.Square, accum_out=sc), [xt], [xs, ss_t])
        S.op("dve", lambda e: e.tensor_scalar(out=rc, in0=sc, scalar1=1.0 / D, scalar2=EPS,
                                              op0=ALU.mult, op1=ALU.add), [ss_t], [rs_t])
        S.op("act", lambda e: e.activation(out=rc, in_=rc, func=AF.Sqrt), [rs_t], [rs_t])
        S.op("dve", lambda e: e.reciprocal(out=rc, in_=rc), [rs_t], [rs_t])
        S.op("dve", lambda e: e.scalar_tensor_tensor(
            out=xs[:], in0=xt[:], scalar=rc, in1=wpre_bc[:], op0=ALU.mult, op1=ALU.mult), [xt, rs_t, wpre_bc], [xs])

    def p0_B(i):
        xs, xT = xs0[i % 2], xT0[i % 2]
        b = nb(2)

        def tr(e):
            for kc in range(KC):
                ins = e.transpose(out=pbv[b + kc // 8][:, (kc % 8) * 128:(kc % 8 + 1) * 128],
                                  in_=xs[:, kc * 128:(kc + 1) * 128], identity=ident[:])
            return ins
        S.op("pe", tr, [xs, ident], [bank[b], bank[b + 1]])
        S.op("act", lambda e: e.activation(
            out=xT[:, 0:8, :], in_=pbv[b].rearrange("p (k t) -> p k t", k=8), func=AF.Copy), [bank[b]], [xT])
        S.op("dve", lambda e: e.tensor_copy(
            out=xT[:, 8:16, :], in_=pbv[b + 1].rearrange("p (k t) -> p k t", k=8)), [bank[b + 1]], [xT])
        S.dma("sp", xnT_d[:, :, i * 128:(i + 1) * 128].rearrange("k p t -> p k t"), xT[:], xT.ds,
              reads=[xT], writes=[xnT_t[i]])

    NT0 = TE // 128
    for i in range(NT0):
        p0_A(i)
        if i >= 1:
            p0_B(i - 1)
    p0_B(NT0 - 1)

    def load_xn(sb_i):
        tiles = xnT_t[sb_i * 8: sb_i * 8 + 10]
        for j, q in enumerate(("sp", "pool")):
            S.dma(q, XNh[j][:], xnT_d[8 * j:8 * j + 8, :, sb_i * SB: sb_i * SB + TX].rearrange("k p t -> p k t"),
                  XNh[j].ds, reads=tiles, writes=[XNh[j]])

    if stop_after < 1:
        S.final.append((xT0[1].ds.h, xT0[1].ds.n))
        S.final.append((xT0[0].ds.h, xT0[0].ds.n))
        S.dma("sp", out_d[0:128, :], xt0[0][:], xt0[0].ds, reads=[xt0[0]])
        S.final.append((xt0[0].ds.h, xt0[0].ds.n))
        S.finish()
        return nc

    gateW = ph("gateW", 0, [128, 2, 2, 16, 128], BF16, 2, True)
    for dd_ in range(2):
        for gi_ in range(2):
            S.dma("pool", gateW[:, dd_, gi_], gate_w[dd_, gi_].rearrange("n i j -> i n j"), gateW.ds, writes=[gateW])
    SET1 = 32768

    def p1set(s):
        base = 16384 + s * SET1
        d = {}
        o = base
        for dd in range(2):
            d["r%d" % dd] = ph("rbuf%d_%d" % (s, dd), o, [128, SB], F32, 4)
            d["i%d" % dd] = ph("ibuf%d_%d" % (s, dd), o + 4096, [128, SB], F32, 4, True)
            d["a%d" % dd] = ph("abuf%d_%d" % (s, dd), o + 8192, [128, SB], F32, 4, True)
            d["h%d" % dd] = ph("hscr%d_%d" % (s, dd), o + 12288, [128, SB], F32, 4)
            o += 16384
        return d

    def p1uset(s):
        base = 16384 + 2 * SET1 + s * 10304
        return {"u_sb": ph("u_sb%d" % s, base, [128, 1028], F32, 4),
                "u": ph("u%d" % s, base + 4160, [128, SB], F32, 4),
                "u_bf": ph("u_bf%d" % s, base + 8256, [128, SB], BF16, 2)}
    p1sets = [p1set(0), p1set(1)]
    p1usets = [p1uset(0), p1uset(1), p1uset(2)]
    AE5 = AE[:].rearrange("p (s d a n) -> p s d a n", s=2, d=2, a=2)
    AE_A = S.raw("AE_A")
    AE_E = S.raw("AE_E")
    def p1_A0(sb_i, n, du):
        W = wnext(("in", CB_U + n))
        b0 = 0
        b2 = 7

        def mm(e, W=W, b0=b0, b2=b2):
            for kc in range(KC):
                for (bk, lo, n_) in ((b0, H - 2, 512), (b0 + 1, H + 510, 512), (b2, H + 1022, 4)):
                    ins = e.matmul(pst[:, bk, 0:n_], lhsT=W[:, kc, :], rhs=XN[:, kc, lo:lo + n_],
                                   start=(kc == 0), stop=(kc == KC - 1))
            return ins
        S.op("pe", mm, [W, XN], [bank[b0], bank[b0 + 1], bank[b2]])
        u_sb, u, u_bf = du["u_sb"], du["u"], du["u_bf"]
        S.op("dve", lambda e: e.tensor_copy(
            out=u_sb[:, 0:1024].rearrange("p (a b) -> p a b", a=2), in_=pst[:, b0:b0 + 2, :]),
            [bank[b0], bank[b0 + 1]], [u_sb])
        S.op("dve", lambda e: e.tensor_copy(out=u_sb[:, 1024:1028], in_=pst[:, b2, 0:4]), [bank[b2]], [u_sb])
        S.op("dve", lambda e: e.tensor_scalar(
            out=u[:], in0=u_sb[:, 0:SB], scalar1=pvec[:, n, 0:1], scalar2=pvec[:, n, 4:5],
            op0=ALU.mult, op1=ALU.add), [u_sb, pvec], [u])
        for tap in range(1, 4):
            S.op("dve", lambda e, tap=tap: e.scalar_tensor_tensor(
                out=u[:], in0=u_sb[:, tap:tap + SB], scalar=pvec[:, n, tap:tap + 1], in1=u[:],
                op0=ALU.mult, op1=ALU.add), [u_sb, pvec, u], [u])
        S.op("dve", lambda e: e.tensor_copy(out=u_bf[:], in_=u[:]), [u], [u_bf])

    def p1_A1(sb_i, n, d, du):
        u_bf = du["u_bf"]
        for dd in range(2):
            br = 3
            bi = 5

            def gm(e, dd=dd, br=br, bi=bi):
                for gi, bb in ((0, br), (1, bi)):
                    for tb in range(2):
                        ins = e.matmul(pst[:, bb + tb, :], lhsT=gateW[:, dd, gi, n, :],
                                       rhs=u_bf[:, tb * 512:(tb + 1) * 512], start=True, stop=True)
                return ins
            S.op("pe", gm, [gateW, u_bf], [bank[br], bank[br + 1], bank[bi], bank[bi + 1]])
            rb, ib = d["r%d" % dd], d["i%d" % dd]
            S.op("act", lambda e, rb=rb, br=br, dd=dd: e.activation(
                out=rb[:].rearrange("p (a b) -> p a b", a=2), in_=pst[:, br:br + 2, :], func=AF.Sigmoid,
                bias=pvec[:, n, 5 + dd:6 + dd], accum_out=rsum[:, dd:dd + 1]),
                [bank[br], bank[br + 1], pvec], [rb, rsum])
            S.op("act", lambda e, ib=ib, bi=bi, dd=dd: e.activation(
                out=ib[:].rearrange("p (a b) -> p a b", a=2), in_=pst[:, bi:bi + 2, :], func=AF.Sigmoid,
                bias=pvec[:, n, 7 + dd:8 + dd]), [bank[bi], bank[bi + 1], pvec], [ib])
        for dd in range(2):
            rb, ab = d["r%d" % dd], d["a%d" % dd]
            S.op("act", lambda e, rb=rb, ab=ab, dd=dd: e.activation(
                out=ab[:], in_=rb[:], func=AF.Exp, scale=cvec[:, n, dd:dd + 1]), [rb, cvec], [ab])
            S.op("act", lambda e, rb=rb, dd=dd: e.activation(
                out=rb[:], in_=rb[:], func=AF.Exp, scale=cvec[:, n, 2 + dd:3 + dd]), [rb, cvec], [rb])
            S.op("act", lambda e, dd=dd: e.activation(
                out=AE5[:, sb_i, dd, 0, n:n + 1], in_=rsum[:, dd:dd + 1], func=AF.Exp,
                scale=cvec[:, n, dd:dd + 1]), [rsum, cvec], [AE_A])
        for dd in range(2):
            rb = d["r%d" % dd]
            S.op("act", lambda e, rb=rb: e.activation(out=rb[:], in_=rb[:], func=AF.Sqrt, scale=-1.0, bias=1.0),
                 [rb], [rb])

    def p1_B(sb_i, n, d, du):
        u = du["u"]
        for dd in range(2):
            ib = d["i%d" % dd]
            S.op("dve", lambda e, ib=ib: e.tensor_tensor(out=ib[:], in0=ib[:], in1=u[:], op=ALU.mult),
                 [ib, u], [ib])
        for dd in range(2):
            rb, ib, ab, hs = d["r%d" % dd], d["i%d" % dd], d["a%d" % dd], d["h%d" % dd]
            S.op("dve", lambda e, ib=ib, rb=rb: e.tensor_tensor(out=ib[:], in0=ib[:], in1=rb[:], op=ALU.mult),
                 [ib, rb], [ib])
            if dd == 0:
                S.op("dve", lambda e, ab=ab, ib=ib, hs=hs: e.tensor_tensor_scan(
                    out=hs[:], data0=ab[:], data1=ib[:], initial=0.0, op0=ALU.mult, op1=ALU.add), [ab, ib], [hs])
                S.op("pool", lambda e, hs=hs: e.tensor_copy(
                    out=AE5[:, sb_i, 0, 1, n:n + 1], in_=hs[:, SB - 1:SB]), [hs], [AE_E])
            else:
                S.op("dve", lambda e, ab=ab, ib=ib, hs=hs: e.tensor_tensor_scan(
                    out=hs[:, ::-1], data0=ab[:, ::-1], data1=ib[:, ::-1], initial=0.0,
                    op0=ALU.mult, op1=ALU.add), [ab, ib], [hs])
                S.op("pool", lambda e, hs=hs: e.tensor_copy(
                    out=AE5[:, sb_i, 1, 1, n:n + 1], in_=hs[:, 0:1]), [hs], [AE_E])
            S.dma("sp", ab_d[sb_i, n, 2 * dd], ab[:], ab.ds, reads=[ab], writes=[ab_t[sb_i][n][2 * dd]])
            S.dma("sp", ab_d[sb_i, n, 2 * dd + 1], ib[:], ib.ds, reads=[ib], writes=[ab_t[sb_i][n][2 * dd + 1]])

    steps = [(sb_i, n) for sb_i in range(NSB - 1, -1, -1) for n in range(16)]
    NS = len(steps)

    def args(k):
        return steps[k] + (p1sets[k % 2], p1usets[k % 3])
    for k in range(NS + 2):
        if k < NS:
            sb_i, n = steps[k]
            if n == 0:
                load_xn(sb_i)
            p1_A0(sb_i, n, p1usets[k % 3])
        if 1 <= k <= NS:
            p1_A1(*args(k - 1))
        if 2 <= k:
            p1_B(*args(k - 2))
    bp[0] = 0

    S.dma("sp", ag_in.ap(), AE[:], AE.ds, reads=[AE, AE_A, AE_E], writes=[agin_t])
    cc_sem = S.dsem("cc_sem")
    S.dma("pool", None, None, cc_sem, reads=[agin_t], writes=[agout_t], inc=1,
          fn=lambda e: e.collective_compute("AllGather", ALU.bypass, replica_groups=[list(range(NCORES))],
                                            ins=[ag_in.ap().opt()], outs=[ag_out.ap().opt()]))
    def load_G():
        S.dma("sp", G[:], ag_out.ap().rearrange("(r p) f -> p r f", p=128), G.ds, reads=[agout_t], writes=[G])
    G6 = G[:].rearrange("p r (s d a n) -> p r s d a n", s=2, d=2, a=2)

    def carry_chain():
        for v in range(16):
            r, s = divmod(v, 2)
            S.op("dve", lambda e, v=v, r=r, s=s: e.tensor_tensor(
                out=htmp[:, 0, :], in0=Hf[:, v, :], in1=G6[:, r, s, 0, 0, :], op=ALU.mult), [Hf, G], [htmp])
            S.op("dve", lambda e, v=v, r=r, s=s: e.tensor_tensor(
                out=Hf[:, v + 1, :], in0=htmp[:, 0, :], in1=G6[:, r, s, 0, 1, :], op=ALU.add), [htmp, G], [Hf])
        for v in range(15, 0, -1):
            r, s = divmod(v, 2)
            S.op("dve", lambda e, v=v, r=r, s=s: e.tensor_tensor(
                out=htmp[:, 1, :], in0=Hb[:, v, :], in1=G6[:, r, s, 1, 0, :], op=ALU.mult), [Hb, G], [htmp])
            S.op("dve", lambda e, v=v, r=r, s=s: e.tensor_tensor(
                out=Hb[:, v - 1, :], in0=htmp[:, 1, :], in1=G6[:, r, s, 1, 1, :], op=ALU.add), [htmp, G], [Hb])
        Hin4 = Hin[:].rearrange("p (s d n) -> p s d n", s=2, d=2)
        for s in range(2):
            for dd, HH in ((0, Hf), (1, Hb)):
                S.op("dve", lambda e, s=s, HH=HH: e.tensor_tensor(
                    out=htmp[:], in0=HH[:, 0:16, :], in1=sel[:, s, :].unsqueeze(2).broadcast_to([128, 16, 16]),
                    op=ALU.mult), [HH, sel], [htmp])
                S.op("dve", lambda e, s=s, dd=dd: e.tensor_reduce(
                    out=Hin4[:, s, dd, :], in_=htmp[:].rearrange("p v n -> p n v"), axis=AX.X, op=ALU.add),
                    [htmp], [Hin])
    Hin4 = Hin[:].rearrange("p (s d n) -> p s d n", s=2, d=2)

    if stop_after < 2:
        load_G()
        carry_chain()
        hd = S.dsem("d_hin")
        S.final.append(S.dma("sp", dbg["hin"], Hin[:], hd, reads=[Hin]))
        S.dma("sp", out_d[0:128, 0:128], G[:, 0, :], G.ds, reads=[G])
        S.final.append((G.ds.h, G.ds.n))
        for dd in range(2):
            for s in range(2):
                for k in ("a", "i"):
                    t = p1sets[s]["%s%d" % (k, dd)]
                    S.final.append((t.ds.h, t.ds.n))
        S.finish()
        return nc

    P2B = 32768
    cs_t = ph("cs_t", P2B, [32, 2, TX], F32, 4, True)
    kT = ph("kT", P2B + 10240, [128, 4, TX], BF16, 2)
    Vt = ph("Vt", P2B + 20480, [128, 10, 512], BF16, 2)
    qT = [ph("qT%d" % i, P2B + 30720 + i * 8192, [128, 4, SB], BF16, 2) for i in range(2)]
    PT = [ph("PT%d" % i, P2B + 47104 + i * 3072, [128, 3, 512], BF16, 2) for i in range(2)]
    qf = ph("qf", P2B + 53248, [32, TX], F32, 4)
    sw = ph("sw", P2B + 58368, [32, TX], F32, 4)
    Dt = [ph("Dt%d" % i, P2B + 63488 + i * 2048, [128, 512], F32, 4) for i in range(2)]
    ot = [ph("ot%d" % i, P2B + 67584 + i * 2048, [128, 512], F32, 4) for i in range(2)]
    P3B = 65536
    RL = [[ph("RL%d_%d" % (i, j), P3B + i * 16384 + j * 4096, [128, SB], F32, 4, True) for j in range(4)]
          for i in range(2)]
    hfb2 = [ph("hfb%d" % i, P3B + 32768 + i * 4096, [128, SB], F32, 4) for i in range(2)]
    hbb = ph("hbb", P3B + 40960, [128, SB], F32, 4)
    sgb = ph("sgb", P3B + 45056, [128, SB], F32, 4)
    sm = [ph("sm%d" % i, EXTRA + i * 4096, [128, SB], F32, 4) for i in range(2)]
    t12 = [ph("t12_%d" % i, EXTRA + 8192 + i * 4096, [128, SB], F32, 4) for i in range(2)]
    WO = [ph("WO%d" % i, i * 16384, [128, KC, 512], BF16, 2, True) for i in range(4)]
    otile = [ph("otile%d" % i, EXTRA + i * 8192, [128, D], F32, 4, True) for i in range(2)]
    xt5 = [S.sb("xt5_%d" % i, XN_OFF + i * 8192, [128, D], F32, 4, True) for i in range(3)]
    wpost_bc = S.sb("wpost_bc", XN_OFF + 24576, [128, D], F32, 4, True)
    shuf = [(i + 16) % 32 for i in range(32)]
    wstage = ph("wstage", 104448, [128, 2, D], BF16, 2, True)

    def stage_wo(j):
        S.dma("pool", wstage[:], w_o[j * 256:(j + 1) * 256, :].rearrange("(kc p) n -> p kc n", p=128),
              wstage.ds, writes=[wstage])
        S.dma("sp", wo_bf[:, 2 * j:2 * j + 2, :], wstage[:], wstage.ds, reads=[wstage], writes=[wobf_t])

    def mmq(e, W, b0):
        for kc in range(KC):
            for tb in range(2):
                ins = e.matmul(pst[:, b0 + tb, :], lhsT=W[:, kc, :], rhs=XN[:, kc, H + tb * 512:H + (tb + 1) * 512],
                               start=(kc == 0), stop=(kc == KC - 1))
        return ins

    def rope(regions, Tn, dst, csoff):
        rb = []
        for (b0, nbk, c0, ncols) in regions:
            rb += [bank[b0 + k] for k in range(nbk)]
        for (b0, nbk, c0, ncols) in regions:
            if nbk == 2:
                src_all = pst[:, b0:b0 + 2, :]
                src_lo = pst[0:32, b0:b0 + 2, :]
                d_all = dst[:, c0:c0 + ncols].rearrange("p (a b) -> p a b", a=2)
                d_lo = qf[:, c0:c0 + ncols].rearrange("p (a b) -> p a b", a=2)
            else:
                src_all = pst[:, b0, 0:ncols]
                src_lo = pst[0:32, b0, 0:ncols]
                d_all = dst[:, c0:c0 + ncols]
                d_lo = qf[:, c0:c0 + ncols]
            S.op("act", lambda e, s_=src_all, d_=d_all: e.activation(out=d_, in_=s_, func=AF.Copy), rb, [dst_tile[0]])
            S.op("act", lambda e, s_=src_lo, d_=d_lo: e.activation(out=d_, in_=s_, func=AF.Copy), rb, [qf])
        S.op("dve", lambda e: e.stream_shuffle(out=sw[:, 0:Tn], in_=qf[:, 0:Tn], mask=shuf), [qf], [sw])
        S.op("pool", lambda e: e.tensor_tensor(out=qf[:, 0:Tn], in0=qf[:, 0:Tn], in1=cs_t[:, 0, csoff:csoff + Tn],
                                               op=ALU.mult), [qf, cs_t], [qf])
        S.op("pool", lambda e: e.tensor_tensor(out=sw[:, 0:Tn], in0=sw[:, 0:Tn], in1=cs_t[:, 1, csoff:csoff + Tn],
                                               op=ALU.mult), [sw, cs_t], [sw])
        S.op("pool", lambda e: e.tensor_tensor(out=dst[0:32, 0:Tn], in0=qf[:, 0:Tn], in1=sw[:, 0:Tn], op=ALU.add),
             [qf, sw], [dst_tile[0]])

    dst_tile = [None]
    att_it = 0
    for sb_i in range(NSB):
        if sb_i > 0:
            load_xn(sb_i)
        S.dma("sp", cs_t[:], cs_d[:, :, sb_i * SB: sb_i * SB + TX], cs_t.ds, writes=[cs_t])
        if sb_i == 0:
            load_G()
        for g in range(4):
            W = wnext(("in", CB_K + g))
            b0 = nb(2)
            b2 = nb(1)

            def mmk(e, W=W, b0=b0, b2=b2):
                for kc in range(KC):
                    for (bk, lo, n_) in ((b0, 0, 512), (b0 + 1, 512, 512), (b2, 1024, 256)):
                        ins = e.matmul(pst[:, bk, 0:n_], lhsT=W[:, kc, :], rhs=XN[:, kc, lo:lo + n_],
                                       start=(kc == 0), stop=(kc == KC - 1))
                return ins
            S.op("pe", mmk, [W, XN], [bank[b0], bank[b0 + 1], bank[b2]])
            dst_tile[0] = kT
            rope([(b0, 2, 0, 1024), (b2, 1, 1024, 256)], TX, kT[:, g, :], 0)
        W4 = [wnext(("in", CB_V + g)) for g in range(4)]
        for j in range(10):
            bv = nb(1)

            def mmv(e, j=j, bv=bv, W4=W4):
                for g in range(4):
                    for kc in range(KC):
                        ins = e.matmul(pst[:, bv, g * 128:(g + 1) * 128], lhsT=XN[:, kc, j * 128:(j + 1) * 128],
                                       rhs=W4[g][:, kc, :], start=(kc == 0), stop=(kc == KC - 1))
                return ins
            S.op("pe", mmv, W4 + [XN], [bank[bv]])
            if j % 2 == 0:
                S.op("act", lambda e, j=j, bv=bv: e.activation(out=Vt[:, j, :], in_=pst[:, bv, :], func=AF.Copy),
                     [bank[bv]], [Vt])
            else:
                S.op("dve", lambda e, j=j, bv=bv: e.tensor_copy(out=Vt[:, j, :], in_=pst[:, bv, :]),
                     [bank[bv]], [Vt])
        if debug:
            S.dma("sp", dbg["kT"][sb_i], kT[:], dbg_sem, reads=[kT])
            S.dma("sp", dbg["v"][sb_i], Vt[:], dbg_sem, reads=[Vt])
        def proj_group(g):
            qTg = qT[g % 2]
            for hh in range(4):
                h = 4 * g + hh
                if sb_i == 0 and h % 2 == 0:
                    stage_wo(h // 2)
                W = wnext(("in", CB_Q + h))
                b0 = nb(2)
                S.op("pe", lambda e, W=W, b0=b0: mmq(e, W, b0), [W, XN], [bank[b0], bank[b0 + 1]])
                dst_tile[0] = qTg
                rope([(b0, 2, 0, 1024)], SB, qTg[:, hh, :], H)
                W = wnext(("in", CB_GA + h))
                b0 = nb(2)
                S.op("pe", lambda e, W=W, b0=b0: mmq(e, W, b0), [W, XN], [bank[b0], bank[b0 + 1]])
                S.op("act", lambda e, h=h, b0=b0: e.activation(
                    out=YB[:, h, :].rearrange("p (a b) -> p a b", a=2), in_=pst[:, b0:b0 + 2, :], func=AF.Silu),
                    [bank[b0], bank[b0 + 1]], [YB])

        def att_scores(g, n, k):
            qTg = qT[g % 2]
            pt, dtt, ott = PT[k % 2], Dt[k % 2], ot[k % 2]
            sbk = [nb(1) for _ in range(3)]

            def mms(e):
                for dj in range(3):
                    ins = e.matmul(pst[:, sbk[dj], :], lhsT=kT[:, g, (n + dj) * 128:(n + dj + 1) * 128],
                                   rhs=qTg[:, :, n * 128:(n + 1) * 128], start=True, stop=True)
                return ins
            S.op("pe", mms, [kT, qTg], [bank[b] for b in sbk])
            for dj in range(3):
                S.op("act", lambda e, dj=dj: e.activation(
                    out=pt[:, dj, :], in_=pst[:, sbk[dj], :], func=AF.Exp, scale=SCALE), [bank[sbk[dj]]], [pt])
            mprev = 2 if (sb_i == 0 and n == 0) else 0
            mnext = 3 if (sb_i == NSB - 1 and n == 7) else 1
            for dj, mi in ((0, mprev), (2, mnext)):
                S.op("pool", lambda e, dj=dj, mi=mi: e.tensor_tensor(
                    out=pt[:, dj, :].rearrange("p (a b) -> p a b", a=4),
                    in0=pt[:, dj, :].rearrange("p (a b) -> p a b", a=4),
                    in1=masks[:, mi, :].unsqueeze(1).broadcast_to([128, 4, 128]), op=ALU.mult), [pt, masks], [pt])
            return (g, n, pt, dtt, ott)

        def att_rest(g, n, pt, dtt, ott):
            bd = nb(1)
            bo = nb(1)

            def mmd(e):
                for dj in range(3):
                    ins = e.matmul(pst[:, bd, :], lhsT=ones[:], rhs=pt[:, dj, :], start=(dj == 0), stop=(dj == 2))
                return ins
            S.op("pe", mmd, [ones, pt], [bank[bd]])

            def mmo(e):
                for dj in range(3):
                    ins = e.matmul(pst[:, bo, :], lhsT=Vt[:, n + dj, g * 128:(g + 1) * 128], rhs=pt[:, dj, :],
                                   start=(dj == 0), stop=(dj == 2))
                return ins
            S.op("pe", mmo, [Vt, pt], [bank[bo]])
            S.op("dve", lambda e: e.tensor_tensor(
                out=dtt[:].rearrange("p (a b) -> p a b", a=4), in0=pst[:, bd, :].rearrange("p (a b) -> p a b", a=4),
                in1=esbc[:, 4 * g:4 * g + 4, :], op=ALU.add), [bank[bd], esbc], [dtt])
            S.op("act", lambda e: e.activation(out=dtt[:], in_=dtt[:], func=AF.Ln), [dtt], [dtt])
            S.op("act", lambda e: e.activation(out=dtt[:], in_=dtt[:], func=AF.Exp, scale=-1.0), [dtt], [dtt])
            S.op("dve", lambda e: e.tensor_tensor(out=ott[:], in0=pst[:, bo, :], in1=dtt[:], op=ALU.mult),
                 [bank[bo], dtt], [ott])
            S.op("pool", lambda e: e.tensor_tensor(
                out=YB[:, 4 * g:4 * g + 4, n * 128:(n + 1) * 128],
                in0=YB[:, 4 * g:4 * g + 4, n * 128:(n + 1) * 128],
                in1=ott[:].rearrange("p (a b) -> p a b", a=4), op=ALU.mult), [YB, ott], [YB])

        proj_group(0)
        for g in range(4):
            if g + 1 < 4:
                proj_group(g + 1)
            prev = None
            for n in range(8):
                cur = att_scores(g, n, att_it)
                att_it += 1
                if prev is not None:
                    att_rest(*prev)
                prev = cur
            att_rest(*prev)
        if debug:
            S.dma("sp", dbg["yb"][sb_i], YB[:], dbg_sem, reads=[YB])
        if stop_after < 3:
            continue
        if sb_i == 0:
            carry_chain()
        def reload(n, slot):
            for j in range(4):
                t = RL[slot][j]
                S.dma("sp", t[:], ab_d[sb_i, n, j], t.ds, reads=[ab_t[sb_i][n][j]], writes=[t])
        reload(0, 0)
        for n in range(16):
            if n + 1 < 16:
                reload(n + 1, (n + 1) % 2)
            af, bf_, ab_, bb_ = RL[n % 2]
            hfb = hfb2[n % 2]
            W = wnext(("in", CB_GL + n))
            b0 = nb(2)
            S.op("pe", lambda e, W=W, b0=b0: mmq(e, W, b0), [W, XN], [bank[b0], bank[b0 + 1]])
            S.op("act", lambda e, b0=b0: e.activation(
                out=sgb[:].rearrange("p (a b) -> p a b", a=2), in_=pst[:, b0:b0 + 2, :], func=AF.Silu),
                [bank[b0], bank[b0 + 1]], [sgb])
            S.op("dve", lambda e, af=af, bf_=bf_, n=n, sb_i=sb_i, hfb=hfb: e.tensor_tensor_scan(
                out=hfb[:], data0=af[:], data1=bf_[:], initial=Hin4[:, sb_i, 0, n:n + 1],
                op0=ALU.mult, op1=ALU.add), [af, bf_, Hin], [hfb])
            S.op("dve", lambda e, ab_=ab_, bb_=bb_, n=n, sb_i=sb_i: e.tensor_tensor_scan(
                out=hbb[:, ::-1], data0=ab_[:, ::-1], data1=bb_[:, ::-1], initial=Hin4[:, sb_i, 1, n:n + 1],
                op0=ALU.mult, op1=ALU.add), [ab_, bb_, Hin], [hbb])
            S.op("dve", lambda e, hfb=hfb: e.tensor_tensor(out=hfb[:], in0=hfb[:], in1=hbb[:], op=ALU.add),
                 [hfb, hbb], [hfb])
            S.op("pool", lambda e, n=n, hfb=hfb: e.tensor_tensor(out=YA[:, n, :], in0=hfb[:], in1=sgb[:], op=ALU.mult),
                 [hfb, sgb], [YA])
        if debug:
            S.dma("sp", dbg["ya"][sb_i], YA[:], dbg_sem, reads=[YA])
        if stop_after < 4:
            continue
        for f in range(16):
            for half, (wk, mk_cb, Y) in enumerate(((("a", f), CB_ML + f, YA), (("b", f), CB_MA + f, YB))):
                Wp = wnext(wk)
                Wm = wnext(("in", mk_cb))
                bpj = nb(2)
                bm = nb(2)

                def mmp(e, Wp=Wp, bpj=bpj, Y=Y):
                    for kc in range(KC):
                        for tb in range(2):
                            ins = e.matmul(pst[:, bpj + tb, :], lhsT=Wp[:, kc, :], rhs=Y[:, kc, tb * 512:(tb + 1) * 512],
                                           start=(kc == 0), stop=(kc == KC - 1))
                    return ins
                S.op("pe", mmp, [Wp, Y], [bank[bpj], bank[bpj + 1]])
                S.op("pe", lambda e, Wm=Wm, bm=bm: mmq(e, Wm, bm), [Wm, XN], [bank[bm], bank[bm + 1]])
                S.op("act", lambda e, half=half, bm=bm: e.activation(
                    out=sm[half][:].rearrange("p (a b) -> p a b", a=2), in_=pst[:, bm:bm + 2, :], func=AF.Sigmoid),
                    [bank[bm], bank[bm + 1]], [sm[half]])
                S.op("dve", lambda e, half=half, bpj=bpj: e.tensor_tensor(
                    out=t12[half][:].rearrange("p (a b) -> p a b", a=2), in0=pst[:, bpj:bpj + 2, :],
                    in1=sm[half][:].rearrange("p (a b) -> p a b", a=2), op=ALU.mult),
                    [bank[bpj], bank[bpj + 1], sm[half]], [t12[half]])
            S.op("pool", lambda e, f=f: e.tensor_tensor(out=MG[:, f, :], in0=t12[0][:], in1=t12[1][:], op=ALU.add),
                 [t12[0], t12[1]], [MG])
        if debug:
            S.dma("sp", dbg["mg"][sb_i], MG[:], dbg_sem, reads=[MG])
        if stop_after < 5:
            continue
        for cg in range(4):
            S.dma("sp" if cg % 2 == 0 else "pool", WO[cg][:], wo_bf[:, :, cg * 512:(cg + 1) * 512],
                  WO[cg].ds, reads=[wobf_t], writes=[WO[cg]])
        S.dma("sp", wpost_bc[:], wpost_d.partition_broadcast(128), wpost_bc.ds, writes=[wpost_bc])
        for i in range(8):
            xt = xt5[i % 3]
            ot_ = otile[i % 2]
            r0 = H + sb_i * SB + i * 128
            S.dma("sp", xt[:], x_ext[r0:r0 + 128, :], xt.ds, writes=[xt])
            bq = [nb(2), nb(2)]
            bks = [bq[0], bq[0] + 1, bq[1], bq[1] + 1]

            for cg in range(4):
                def mmf(e, i=i, cg=cg, bk=bks[cg]):
                    for kc in range(KC):
                        ins = e.matmul(pst[:, bk, :], lhsT=MG[:, kc, i * 128:(i + 1) * 128],
                                       rhs=WO[cg][:, kc, :], start=(kc == 0), stop=(kc == KC - 1))
                    return ins
                S.op("pe", mmf, [MG, WO[cg]], [bank[bks[cg]]])
            for cg in range(4):
                S.op("act", lambda e, cg=cg, ot_=ot_, bks=bks: e.activation(
                    out=ot_[:, cg * 512:(cg + 1) * 512], in_=pst[:, bks[cg], :], func=AF.Square,
                    accum_out=ss_t[:, 4 + cg:5 + cg]), [bank[bks[cg]]], [ot_, ss_t])
            S.op("dve", lambda e: e.tensor_reduce(out=rs_t[:, 0:1], in_=ss_t[:, 4:8], axis=AX.X, op=ALU.add),
                 [ss_t], [rs_t])
            S.op("dve", lambda e: e.tensor_scalar(out=rs_t[:, 0:1], in0=rs_t[:, 0:1], scalar1=1.0 / D, scalar2=EPS,
                                                  op0=ALU.mult, op1=ALU.add), [rs_t], [rs_t])
            S.op("act", lambda e: e.activation(out=rs_t[:, 0:1], in_=rs_t[:, 0:1], func=AF.Sqrt), [rs_t], [rs_t])
            S.op("dve", lambda e: e.reciprocal(out=rs_t[:, 0:1], in_=rs_t[:, 0:1]), [rs_t], [rs_t])
            for cg in range(4):
                S.op("dve", lambda e, cg=cg, ot_=ot_, bks=bks: e.scalar_tensor_tensor(
                    out=ot_[:, cg * 512:(cg + 1) * 512], in0=pst[:, bks[cg], :], scalar=rs_t[:, 0:1],
                    in1=wpost_bc[:, cg * 512:(cg + 1) * 512], op0=ALU.mult, op1=ALU.mult),
                    [bank[bks[cg]], rs_t, wpost_bc], [ot_])
            S.op("pool", lambda e, ot_=ot_, xt=xt: e.tensor_tensor(out=ot_[:], in0=ot_[:], in1=xt[:], op=ALU.add),
                 [ot_, xt], [ot_])
            o0 = sb_i * SB + i * 128
            S.final.append(S.dma("sp", out_d[o0:o0 + 128, :], ot_[:], ot_.ds, reads=[ot_]))

    if debug:
        S.final.append((dbg_sem.h, dbg_sem.n))
    if stop_after < 5:
        S.dma("sp", out_d[0:128, 0:128], G[:, 0, :], G.ds, reads=[G])
        S.final.append((G.ds.h, G.ds.n))
    S.finish()
    return nc


def make_in_maps(x, norm_pre_w, w_in, conv_w, conv_b, lru_w_r, lru_b_r, lru_w_i, lru_b_i, lru_lambda,
                 attn_sink, w_proj_a, w_proj_b, w_out, norm_post_w):
    f32 = np.float32
    x2 = np.asarray(x, f32).reshape(S_FULL, D)
    xpad = np.zeros((S_FULL + 2 * H, D), f32)
    xpad[H:H + S_FULL] = x2
    w_in2 = np.ascontiguousarray(np.asarray(w_in, f32)[0])
    w_a2 = np.ascontiguousarray(np.asarray(w_proj_a, f32)[0])
    w_b2 = np.ascontiguousarray(np.asarray(w_proj_b, f32)[0])
    w_o2 = np.ascontiguousarray(np.asarray(w_out, f32)[0])
    gate_w = np.ascontiguousarray(np.stack([np.asarray(lru_w_r, f32)[0], np.asarray(lru_w_i, f32)[0]], axis=1))
    vecs = [np.asarray(conv_w, f32)[0, t] for t in range(4)] + [np.asarray(conv_b, f32)[0]]
    vecs += [np.asarray(lru_b_r, f32)[0, 0], np.asarray(lru_b_r, f32)[0, 1]]
    vecs += [np.asarray(lru_b_i, f32)[0, 0], np.asarray(lru_b_i, f32)[0, 1]]
    vecs += [np.asarray(lru_lambda, f32)[0, 0], np.asarray(lru_lambda, f32)[0, 1]]
    pvec = np.ascontiguousarray(np.stack([v.reshape(16, 128).T for v in vecs], axis=-1))
    wpre = np.ascontiguousarray(np.asarray(norm_pre_w, f32)[0])
    wpost = np.ascontiguousarray(np.asarray(norm_post_w, f32)[0])
    sink = np.ascontiguousarray(np.asarray(attn_sink, f32)[0])
    ident = np.eye(128, dtype=f32).astype(ml_dtypes.bfloat16)
    inv_freq = (np.float32(500000.0) ** (-np.arange(0, 32, 2, dtype=f32) / np.float32(32))).astype(f32)
    kk = np.arange(128)[:, None]
    qq = np.arange(128)[None, :]
    tri_prev = (kk >= qq).astype(f32)
    tri_next = (kk <= qq).astype(f32)
    maps = []
    for c in range(NCORES):
        pos = (np.arange(TE, dtype=np.int64) + c * T - H).astype(f32)
        ang = pos[:, None] * inv_freq[None, :]
        cos, sin = np.cos(ang).astype(f32), np.sin(ang).astype(f32)
        cs = np.zeros((32, 2, TE), f32)
        cs[0:16, 0] = cos.T
        cs[16:32, 0] = cos.T
        cs[0:16, 1] = -sin.T
        cs[16:32, 1] = sin.T
        masks = np.stack([tri_prev, tri_next, tri_prev * (0.0 if c == 0 else 1.0),
                          tri_next * (0.0 if c == NCORES - 1 else 1.0)], axis=1).astype(ml_dtypes.bfloat16)
        sel = np.zeros((128, 2, 16), f32)
        for s in range(2):
            sel[:, s, 2 * c + s] = 1.0
        maps.append({
            "x_ext": np.ascontiguousarray(xpad[c * T: c * T + TE]),
            "w_in": w_in2, "w_a": w_a2, "w_b": w_b2, "w_o": w_o2, "gate_w": gate_w, "pvec": pvec,
            "wpre": wpre, "wpost": wpost, "sink": sink, "cs": cs, "masks": np.ascontiguousarray(masks),
            "ident": ident, "sel": sel,
        })
    return maps


_NC_CACHE = {}


def kernel(**inputs):
    maps = make_in_maps(**inputs)
    if "nc" not in _NC_CACHE:
        _NC_CACHE["nc"] = build_program()
    res = run_bass_kernel_spmd(_NC_CACHE["nc"], maps, core_ids=list(range(NCORES)))
    out = np.concatenate([np.asarray(r["out"], np.float32) for r in res.results], axis=0)
    return out.reshape(1, S_FULL, D)
```
